# Optimizing a Trainium2 kernel written in Bass

```python
import math
import jax
import jax.numpy as jnp
from jax import lax
import numpy as np

D_MODEL = 1024
BATCH = 8
SEQ = 2048
DEPTH = 1

CTX_LEN = 256
GRID_W = 64
MIX_WIDTH = D_MODEL
HY_WIDTH = MIX_WIDTH // 2
HY_ORDER = 2
HY_EMB_BANDS = 16
HY_EMB_DIM = 1 + 2 * HY_EMB_BANDS
HY_FILTER_WIDTH = 64
HY_FAST_DECAY = 0.3
HY_SLOW_DECAY = 1.5
HY_TARGET = 1e-2
HY_FILTER_INIT = 0.01
RET_WIDTH = MIX_WIDTH - HY_WIDTH
RET_HEADS = 4
RET_QK_DIM = 64
RET_V_DIM = RET_WIDTH // RET_HEADS
RET_CHUNK = 128
ROPE_BASE = 10000.0
D_FF = 2816
EPS = 1e-6

HY_COLS = (HY_ORDER + 1) * HY_WIDTH
RET_QK_WIDTH = RET_HEADS * RET_QK_DIM
Q_OFF = HY_COLS
K_OFF = Q_OFF + RET_QK_WIDTH
V_OFF = K_OFF + RET_QK_WIDTH
G_OFF = V_OFF + RET_WIDTH
IN_COLS = G_OFF + RET_WIDTH

kernel_name = 'hyena_retention_hybrid_dit'


def rms_norm(x, gain):
    xf = x.astype(jnp.float32)
    y = xf * lax.rsqrt(jnp.mean(xf * xf, axis=-1, keepdims=True) + EPS)
    return (y * gain.astype(jnp.float32)).astype(x.dtype)


def modulate(h, shift, scale):
    return h * (1.0 + scale) + shift


def depthwise_conv3(x, w, b):
    seq = x.shape[1]
    xp = jnp.pad(x, ((0, 0), (1, 1), (0, 0)))
    return xp[:, :seq] * w[0] + xp[:, 1:seq + 1] * w[1] + xp[:, 2:] * w[2] + b


def hyena_filters(seq, w1, b1, f1, w2, b2, f2, w3):
    pos = jnp.arange(seq, dtype=jnp.float32)[:, None]
    t = jnp.linspace(0.0, 1.0, seq, dtype=jnp.float32)[:, None]
    bands = jnp.linspace(1e-4, HY_EMB_BANDS - 1, HY_EMB_BANDS, dtype=jnp.float32)[None, :]
    ang = 2.0 * math.pi * pos * bands / seq
    z = jnp.concatenate([t, jnp.cos(ang), -jnp.sin(ang)], axis=-1)
    hid = jnp.sin(f1 * (z @ w1 + b1))
    hid = jnp.sin(f2 * (hid @ w2 + b2))
    filt = (hid @ w3).reshape(seq, 2, HY_ORDER, HY_WIDTH)
    max_decay = math.log(HY_TARGET) / HY_FAST_DECAY
    min_decay = math.log(HY_TARGET) / HY_SLOW_DECAY
    deltas = jnp.linspace(min_decay, max_decay, HY_WIDTH, dtype=jnp.float32)
    window = jnp.exp(-t * jnp.abs(deltas)[None, :])
    return filt * window[:, None, None, :]


def bidirectional_fftconv(u, k_fwd, k_bwd, skip):
    seq = u.shape[1]
    kern = jnp.concatenate([k_fwd, jnp.zeros_like(k_fwd[:1]), k_bwd[:0:-1]], axis=0)
    k_f = jnp.fft.rfft(kern.astype(jnp.float32), n=2 * seq, axis=0)
    u32 = u.astype(jnp.float32)
    u_f = jnp.fft.rfft(u32, n=2 * seq, axis=1)
    y = jnp.fft.irfft(u_f * k_f[None], n=2 * seq, axis=1)[:, :seq]
    return (y + u32 * skip.astype(jnp.float32)).astype(u.dtype)


def hyena_mixer(z, p):
    z = depthwise_conv3(z, p['hy_conv_w'], p['hy_conv_b'])
    v, *gates = jnp.split(z, HY_ORDER + 1, axis=-1)
    filt = hyena_filters(z.shape[1], p['hy_w1'], p['hy_b1'], p['hy_f1'],
                         p['hy_w2'], p['hy_b2'], p['hy_f2'], p['hy_w3'])
    y = v
    for o in range(HY_ORDER):
        y = gates[o] * bidirectional_fftconv(y, filt[:, 0, o], filt[:, 1, o], p['hy_bias'][o])
    return y


def split_heads(t):
    b, seq, _ = t.shape
    return t.reshape(b, seq, RET_HEADS, -1).transpose(0, 2, 1, 3)


def rope_2d(x, row, col):
    half = x.shape[-1] // 2
    quarter = half // 2
    inv_freq = ROPE_BASE ** (-jnp.arange(quarter, dtype=jnp.float32) / quarter)
    ang = jnp.concatenate([row[:, None] * inv_freq, col[:, None] * inv_freq], axis=-1)
    cos, sin = jnp.cos(ang), jnp.sin(ang)
    x1, x2 = x[..., :half], x[..., half:]
    return jnp.concatenate([x1 * cos - x2 * sin, x1 * sin + x2 * cos], axis=-1)


def retention_chunkwise(q, k, v, log_gamma, s0, inclusive):
    b, h, seq, dk = q.shape
    dv = v.shape[-1]
    n_chunks = seq // RET_CHUNK
    qc = q.astype(jnp.float32).reshape(b, h, n_chunks, RET_CHUNK, dk)
    kc = k.astype(jnp.float32).reshape(b, h, n_chunks, RET_CHUNK, dk)
    vc = v.astype(jnp.float32).reshape(b, h, n_chunks, RET_CHUNK, dv)
    pos = jnp.arange(RET_CHUNK, dtype=jnp.float32)
    diff = pos[:, None] - pos[None, :]
    mask = (diff >= 0) if inclusive else (diff > 0)
    lg = log_gamma.astype(jnp.float32)[:, None]
    intra = jnp.where(mask[None], jnp.exp(lg[:, :, None] * jnp.maximum(diff, 0.0)[None]), 0.0)
    scores = jnp.einsum('bhnid,bhnjd->bhnij', qc, kc) * intra[None, :, None]
    o_intra = jnp.einsum('bhnij,bhnje->bhnie', scores, vc)
    k_w = jnp.exp(lg * (RET_CHUNK - 1.0 - pos))
    kv = jnp.einsum('bhnjd,hj,bhnje->nbhde', kc, k_w, vc)
    chunk_decay = jnp.exp(lg * RET_CHUNK)[None, :, :, None]

    def step(state, kv_n):
        return state * chunk_decay + kv_n, state

    _, s_prev = lax.scan(step, s0.astype(jnp.float32), kv)
    q_w = jnp.exp(lg * (pos + 1.0))
    o_cross = jnp.einsum('bhnid,nbhde,hi->bhnie', qc, s_prev, q_w)
    return (o_intra + o_cross).reshape(b, h, seq, dv)


def retention_final_state(k, v, log_gamma, reverse):
    seq = k.shape[2]
    pos = jnp.arange(seq, dtype=jnp.float32)
    steps = pos if reverse else (seq - 1.0 - pos)
    w = jnp.exp(log_gamma.astype(jnp.float32)[:, None] * steps[None, :])
    return jnp.einsum('bhld,hl,bhle->bhde', k.astype(jnp.float32), w, v.astype(jnp.float32))


def bidirectional_retention(q, k, v, lg_f, lg_b, s_f, s_b):
    o_f = retention_chunkwise(q, k, v, lg_f, s_f, True)
    o_b = retention_chunkwise(q[:, :, ::-1], k[:, :, ::-1], v[:, :, ::-1], lg_b, s_b, False)
    return o_f + o_b[:, :, ::-1]


def retention_output(o, g):
    of = o.astype(jnp.float32)
    of = of * lax.rsqrt(jnp.mean(of * of, axis=-1, keepdims=True) + EPS)
    b, h, seq, dv = o.shape
    of = of.transpose(0, 2, 1, 3).reshape(b, seq, h * dv)
    return (jax.nn.silu(g.astype(jnp.float32)) * of).astype(g.dtype)


def conv_ffn(h, p):
    a = depthwise_conv3(h @ p['ffn_w_up'], p['ffn_conv_w'], p['ffn_conv_b'])
    val, gate = jnp.split(a, 2, axis=-1)
    return (jax.nn.silu(gate) * val) @ p['ffn_w_down']


def trunk_layer(x, xc, c, c_ctx, p, ctx_out):
    mod = jax.nn.silu(c) @ p['w_mod'] + p['b_mod']
    mod_c = jax.nn.silu(c_ctx) @ p['w_mod'] + p['b_mod']
    sh1, sc1, g1, sh2, sc2, g2 = jnp.split(mod[:, None, :], 6, axis=-1)
    shc1, scc1, gc1, shc2, scc2, gc2 = jnp.split(mod_c, 6)
    seq = x.shape[1]
    rows = seq // GRID_W
    row = jnp.broadcast_to(jnp.arange(rows, dtype=jnp.float32)[:, None], (rows, GRID_W)).reshape(seq)
    col = jnp.broadcast_to(jnp.arange(GRID_W, dtype=jnp.float32)[None, :], (rows, GRID_W)).reshape(seq)
    lg_f = jax.nn.log_sigmoid(p['ret_logit_f'].astype(jnp.float32))
    lg_b = jax.nn.log_sigmoid(p['ret_logit_b'].astype(jnp.float32))
    k_scale = RET_QK_DIM ** -0.5

    h = modulate(rms_norm(x, p['norm1']), sh1, sc1)
    hc = modulate(rms_norm(xc, p['norm1']), shc1, scc1)
    u = h @ p['w_in']
    w_c = p['w_in'] if ctx_out else p['w_in'][:, K_OFF:G_OFF]
    off = 0 if ctx_out else K_OFF
    uc = hc @ w_c
    kc = split_heads(uc[..., K_OFF - off:V_OFF - off]) * k_scale
    vc = split_heads(uc[..., V_OFF - off:G_OFF - off])
    s_f = retention_final_state(kc, vc, lg_f, False)
    s_b = retention_final_state(kc, vc, lg_b, True)

    q = rope_2d(split_heads(u[..., Q_OFF:K_OFF]), row, col)
    k = rope_2d(split_heads(u[..., K_OFF:V_OFF]), row, col) * k_scale
    v = split_heads(u[..., V_OFF:G_OFF])
    y_hy = hyena_mixer(u[..., :HY_COLS], p)
    y_ret = retention_output(bidirectional_retention(q, k, v, lg_f, lg_b, s_f, s_b), u[..., G_OFF:])
    x = x + g1 * (jnp.concatenate([y_hy, y_ret], axis=-1) @ p['w_out'])

    if ctx_out:
        qc = split_heads(uc[..., Q_OFF:K_OFF])
        zero = jnp.zeros_like(s_f)
        yc_hy = hyena_mixer(uc[..., :HY_COLS], p)
        yc_ret = retention_output(bidirectional_retention(qc, kc, vc, lg_f, lg_b, zero, zero), uc[..., G_OFF:])
        xc = xc + gc1 * (jnp.concatenate([yc_hy, yc_ret], axis=-1) @ p['w_out'])

    x = x + g2 * conv_ffn(modulate(rms_norm(x, p['norm2']), sh2, sc2), p)
    if ctx_out:
        xc = xc + gc2 * conv_ffn(modulate(rms_norm(xc, p['norm2']), shc2, scc2), p)
    return x, xc


def setup_inputs(seed: int = 0) -> dict:
    key = jax.random.key(seed)
    ks = jax.random.split(key, 27)

    def nrm(k, shape, scale):
        return jax.random.normal(k, shape, jnp.float32) * scale

    base_logit = jnp.log(2.0 ** (5.0 + jnp.arange(RET_HEADS, dtype=jnp.float32)) - 1.0)
    return {
        'x': nrm(ks[0], (BATCH, SEQ, D_MODEL), 1.0),
        'c': nrm(ks[1], (BATCH, D_MODEL), 1.0),
        'ctx': nrm(ks[2], (BATCH, CTX_LEN, D_MODEL), 1.0),
        'c_ctx': nrm(ks[3], (D_MODEL,), 1.0),
        'w_mod': nrm(ks[4], (DEPTH, D_MODEL, 6 * D_MODEL), 0.5 * D_MODEL ** -0.5),
        'b_mod': nrm(ks[5], (DEPTH, 6 * D_MODEL), 0.02),
        'norm1': 1.0 + nrm(ks[6], (DEPTH, D_MODEL), 0.02),
        'w_in': nrm(ks[7], (DEPTH, D_MODEL, IN_COLS), D_MODEL ** -0.5),
        'hy_conv_w': nrm(ks[8], (DEPTH, 3, HY_COLS), 3 ** -0.5),
        'hy_conv_b': nrm(ks[9], (DEPTH, HY_COLS), 0.02),
        'hy_w1': nrm(ks[10], (DEPTH, HY_EMB_DIM, HY_FILTER_WIDTH), 2.0 * HY_EMB_DIM ** -0.5),
        'hy_b1': nrm(ks[11], (DEPTH, HY_FILTER_WIDTH), 0.02),
        'hy_f1': 1.0 + nrm(ks[12], (DEPTH, HY_FILTER_WIDTH), 0.02),
        'hy_w2': nrm(ks[13], (DEPTH, HY_FILTER_WIDTH, HY_FILTER_WIDTH), 2.0 * HY_FILTER_WIDTH ** -0.5),
        'hy_b2': nrm(ks[14], (DEPTH, HY_FILTER_WIDTH), 0.02),
        'hy_f2': 1.0 + nrm(ks[15], (DEPTH, HY_FILTER_WIDTH), 0.02),
        'hy_w3': nrm(ks[16], (DEPTH, HY_FILTER_WIDTH, 2 * HY_ORDER * HY_WIDTH), HY_FILTER_INIT),
        'hy_bias': nrm(ks[17], (DEPTH, HY_ORDER, HY_WIDTH), 0.5),
        'ret_logit_f': base_logit + nrm(ks[18], (DEPTH, RET_HEADS), 0.05),
        'ret_logit_b': base_logit + nrm(ks[19], (DEPTH, RET_HEADS), 0.05),
        'w_out': nrm(ks[20], (DEPTH, MIX_WIDTH, D_MODEL), MIX_WIDTH ** -0.5),
        'norm2': 1.0 + nrm(ks[21], (DEPTH, D_MODEL), 0.02),
        'ffn_w_up': nrm(ks[22], (DEPTH, D_MODEL, 2 * D_FF), D_MODEL ** -0.5),
        'ffn_conv_w': nrm(ks[23], (DEPTH, 3, 2 * D_FF), 3 ** -0.5),
        'ffn_conv_b': nrm(ks[24], (DEPTH, 2 * D_FF), 0.02),
        'ffn_w_down': nrm(ks[25], (DEPTH, D_FF, D_MODEL), D_FF ** -0.5),
        'norm_f': 1.0 + nrm(ks[26], (D_MODEL,), 0.02),
    }


def reference(x, c, ctx, c_ctx, w_mod, b_mod, norm1, w_in, hy_conv_w, hy_conv_b, hy_w1, hy_b1, hy_f1,
              hy_w2, hy_b2, hy_f2, hy_w3, hy_bias, ret_logit_f, ret_logit_b, w_out, norm2,
              ffn_w_up, ffn_conv_w, ffn_conv_b, ffn_w_down, norm_f):
    xc = ctx
    for layer in range(DEPTH):
        p = {
            'w_mod': w_mod[layer], 'b_mod': b_mod[layer], 'norm1': norm1[layer], 'w_in': w_in[layer],
            'hy_conv_w': hy_conv_w[layer], 'hy_conv_b': hy_conv_b[layer],
            'hy_w1': hy_w1[layer], 'hy_b1': hy_b1[layer], 'hy_f1': hy_f1[layer],
            'hy_w2': hy_w2[layer], 'hy_b2': hy_b2[layer], 'hy_f2': hy_f2[layer],
            'hy_w3': hy_w3[layer], 'hy_bias': hy_bias[layer],
            'ret_logit_f': ret_logit_f[layer], 'ret_logit_b': ret_logit_b[layer],
            'w_out': w_out[layer], 'norm2': norm2[layer],
            'ffn_w_up': ffn_w_up[layer], 'ffn_conv_w': ffn_conv_w[layer],
            'ffn_conv_b': ffn_conv_b[layer], 'ffn_w_down': ffn_w_down[layer],
        }
        x, xc = trunk_layer(x, xc, c, c_ctx, p, layer < DEPTH - 1)
    return rms_norm(x, norm_f)
```

```python
import math
from contextlib import ExitStack

import numpy as np
import ml_dtypes

import concourse.bass as bass
import concourse.mybir as mybir
from concourse.bass_utils import run_bass_kernel_spmd

F32 = mybir.dt.float32
BF16 = mybir.dt.bfloat16
I32 = mybir.dt.int32
AF = mybir.ActivationFunctionType
ALU = mybir.AluOpType

L = 2048
D = 1024
NT = 16
NFFT = 4096
EPS = 1e-6
DFF = 2816
NFC = 22
ENGS = ("pe", "act", "dve", "pool", "sp")
PI = math.pi


class Buf:
    __slots__ = ("name", "w", "r", "rd")

    def __init__(self, name=""):
        self.name = name
        self.w = None
        self.r = {}
        self.rd = []


class Op:
    __slots__ = ("eng", "fn", "deps", "signal", "count", "dma", "chan", "cval", "cprev")

    def __init__(self, eng, fn, dma):
        self.eng = eng
        self.fn = fn
        self.dma = dma
        self.deps = []
        self.signal = False
        self.count = 0
        self.chan = None
        self.cval = 0
        self.cprev = 0


class Sched:
    def __init__(self, nc, nchan=20, self_wait=True):
        self.nc = nc
        self.ops = {e: [] for e in ENGS}
        self.nchan = nchan
        self.self_wait = self_wait

    def add(self, eng, fn, reads=(), writes=(), dma=False):
        op = Op(eng, fn, dma)
        deps = {}
        for b in reads:
            if b.w is not None:
                deps[id(b.w)] = b.w
        for b in writes:
            if b.w is not None:
                deps[id(b.w)] = b.w
            for o in b.r.values():
                deps[id(o)] = o
            for o in b.rd:
                deps[id(o)] = o
        op.deps = list(deps.values())
        for d in op.deps:
            d.signal = True
        for b in reads:
            if dma:
                b.rd.append(op)
            else:
                b.r[eng] = op
        for b in writes:
            b.w = op
            b.r = {}
            b.rd = []
        if dma:
            op.signal = True
        self.ops[eng].append(op)
        return op

    def dma(self, eng, out, in_, reads=(), writes=()):
        return self.add(eng, lambda e: e.dma_start(out=out, in_=in_), reads, writes, dma=True)

    def alias(self, new_bufs, old_bufs):
        ops = {}
        for b in old_bufs:
            if b.w is not None:
                ops[id(b.w)] = b.w
            for o in b.r.values():
                ops[id(o)] = o
            for o in b.rd:
                ops[id(o)] = o
        lst = list(ops.values())
        for b in new_bufs:
            b.rd.extend(lst)

    def emit(self, final_ops=()):
        nc = self.nc
        with ExitStack() as st:
            esem = {e: st.enter_context(nc.semaphore("s_" + e)) for e in ENGS}
            csem = {}
            for e in ENGS:
                if any(o.dma for o in self.ops[e]):
                    csem[e] = [st.enter_context(nc.semaphore("c_%s_%d" % (e, i))) for i in range(self.nchan)]
            for e in ENGS:
                n = 0
                uses = [0] * self.nchan
                k = 0
                for o in self.ops[e]:
                    if o.dma:
                        c = k % self.nchan
                        k += 1
                        o.chan = csem[e][c]
                        o.cprev = 16 * uses[c]
                        uses[c] += 1
                        o.cval = 16 * uses[c]
                    elif o.signal:
                        n += 1
                        o.count = n
            self_wait = self.self_wait

            def run(e, eng):
                waited = {}
                for o in self.ops[e]:
                    need = {}
                    for d in o.deps:
                        if d.dma:
                            key, val = d.chan, d.cval
                        else:
                            if d.eng == e and (e == "pe" or not self_wait):
                                continue
                            key, val = esem[d.eng], d.count
                        kk = id(key)
                        if kk not in need or need[kk][1] < val:
                            need[kk] = (key, val)
                    if o.dma and o.cprev > 0:
                        kk = id(o.chan)
                        if kk not in need or need[kk][1] < o.cprev:
                            need[kk] = (o.chan, o.cprev)
                    for kk, (key, val) in need.items():
                        if waited.get(kk, 0) >= val:
                            continue
                        eng.wait_ge(key, val)
                        waited[kk] = val
                    ins = o.fn(eng)
                    if o.dma:
                        ins.then_inc(o.chan, 16)
                    elif o.signal:
                        ins.then_inc(esem[e], 1)
                if e == "sp":
                    for d in final_ops:
                        if d.dma:
                            eng.wait_ge(d.chan, d.cval)
                        else:
                            eng.wait_ge(esem[d.eng], d.count)

            with nc.Block() as block:
                @block.tensor
                def _(eng):
                    run("pe", eng)

                @block.scalar
                def _(eng):
                    run("act", eng)

                @block.vector
                def _(eng):
                    run("dve", eng)

                @block.gpsimd
                def _(eng):
                    run("pool", eng)

                @block.sync
                def _(eng):
                    run("sp", eng)


class Ring:
    def __init__(self, n):
        self.n = n
        self.i = -1

    def nxt(self):
        self.i = (self.i + 1) % self.n
        return self.i


_CONSTS = None


def host_consts():
    global _CONSTS
    if _CONSTS is not None:
        return _CONSTS
    c = {}
    j = np.arange(16)[:, None, None, None]
    p = np.arange(128)[None, :, None, None]
    i = np.arange(16)[None, None, :, None]
    r = np.arange(128)[None, None, None, :]
    prod = ((128 * i + p) * (128 * j + r)) % NFFT
    ang = 2.0 * np.pi * prod.astype(np.float64) / NFFT
    fd = np.stack([np.cos(ang), -np.sin(ang)], axis=3)
    c["fd"] = fd.reshape(16, 128, 4096).astype(ml_dtypes.bfloat16)
    sgn = (1.0 - 2.0 * (np.arange(128) % 2)).astype(np.float32)
    c["nyqc"] = sgn.reshape(128, 1).astype(ml_dtypes.bfloat16)
    c["nyqr"] = sgn.reshape(1, 128).astype(ml_dtypes.bfloat16)
    t = np.arange(L)
    row = (t // 64).astype(np.float32)
    col = (t % 64).astype(np.float32)
    inv_freq = (10000.0 ** (-np.arange(16, dtype=np.float32) / 16)).astype(np.float32)
    ang = np.concatenate([row[:, None] * inv_freq, col[:, None] * inv_freq], axis=-1).astype(np.float32)
    cs, sn = np.cos(ang), np.sin(ang)
    cos64 = np.concatenate([cs, cs], axis=1).T
    sin64 = np.concatenate([-sn, sn], axis=1).T
    c["rope"] = np.concatenate([np.tile(cos64, (2, 1)), np.tile(sin64, (2, 1))], axis=1).astype(np.float32)
    rc = np.zeros((128, 776), np.float32)
    jj = np.arange(128)[:, None].astype(np.float32)
    ii = np.arange(128)[None, :].astype(np.float32)
    rc[:, 0:128] = np.maximum(ii - jj, 0)
    rc[:, 128:256] = np.maximum(jj - ii, 0)
    rc[:, 256:384] = (ii >= jj)
    rc[:, 384:512] = (jj > ii)
    rc[:, 512:640] = ii + 1.0
    rc[:, 640:768] = 128.0 - ii
    pp = np.arange(128, dtype=np.float32)
    rc[:, 768] = 127.0 - pp
    rc[:, 769] = pp
    rc[:, 770] = 255.0 - pp
    rc[:, 771] = 127.0 - pp
    rc[:, 772] = pp
    rc[:, 773] = 128.0 + pp
    rc[:, 774] = 1.0
    rc[:, 775] = -PI
    c["rc"] = rc
    pos = np.arange(L, dtype=np.float32)[:, None]
    tt = np.linspace(0.0, 1.0, L, dtype=np.float32)[:, None]
    bands = np.linspace(1e-4, 15, 16, dtype=np.float32)[None, :]
    a2 = (2.0 * np.float32(math.pi) * pos * bands / L).astype(np.float32)
    z = np.concatenate([tt, np.cos(a2), -np.sin(a2)], axis=-1).astype(np.float32)
    c["zfT"] = np.ascontiguousarray(z.T)
    max_decay = math.log(1e-2) / 0.3
    min_decay = math.log(1e-2) / 1.5
    deltas = np.linspace(min_decay, max_decay, 512, dtype=np.float32)
    c["absd"] = np.abs(deltas).reshape(1, 512).astype(np.float32)
    tn = -tt[:, 0]
    c["tn"] = np.ascontiguousarray(tn.reshape(16, 128).T).astype(np.float32)
    _CONSTS = c
    return c


class _Stop(Exception):
    pass


def build(dbg=(), stop_after=None):
    nc = bass.Bass("TRN2", target_bir_lowering=False)

    def CK(name):
        if stop_after == name:
            raise _Stop()

    def din(name, shape, dt=F32):
        return nc.dram_tensor(name, list(shape), dt, kind="ExternalInput").ap()

    x_d = din("x", [L, D])
    ctx_d = din("ctx", [256, D])
    cvec_d = din("cvec", [128, 16])
    wmod_d = din("w_mod", [D, 6144])
    bmodc_d = din("bmod_col", [128, 48])
    bmodr_d = din("bmod_row", [1, 6144])
    n1c_d = din("n1c", [128, 8])
    n2c_d = din("n2c", [128, 8])
    nfr_d = din("nf_row", [1, D])
    win_d = din("w_in", [D, 3584])
    hcw_d = din("hcw", [128, 48])
    fcw_d = din("fcw", [128, 176])
    hw1_d = din("hy_w1", [33, 64])
    hyp_d = din("hyp", [64, 4])
    hw2_d = din("hy_w2", [64, 64])
    hw3_d = din("hy_w3", [64, 2048])
    hbias_d = din("hy_bias", [1, 1024])
    rlog_d = din("rlog", [1, 8])
    wout_d = din("w_out", [D, D])
    wup_d = din("w_up", [D, 2 * DFF])
    wdn_d = din("w_down", [DFF, D])
    fd_d = din("fd", [16, 128, 4096], BF16)
    nyqc_d = din("nyqc", [128, 1], BF16)
    nyqr_d = din("nyqr", [1, 128], BF16)
    rope_d = din("rope", [128, 2 * L])
    rc_d = din("rc", [128, 776])
    zfT_d = din("zfT", [33, L])
    absd_d = din("absd", [1, 512])
    tn_d = din("tn", [128, 16])
    y_d = nc.dram_tensor("y", [L, D], F32, kind="ExternalOutput").ap()
    dbg_d = {}
    for name, shape, dts in dbg:
        dbg_d[name] = nc.dram_tensor("dbg_" + name, list(shape), BF16 if dts == "bf16" else F32,
                                     kind="ExternalOutput").ap()

    st = ExitStack()
    with st:
        def sb(name, shape, dt):
            return st.enter_context(nc.sbuf_tensor("s_" + name, list(shape), dt))

        def ps(name, shape, dt):
            return st.enter_context(nc.psum_tensor(name, list(shape), dt))

        S = Sched(nc)
        final_ops = []

        BIG = sb("big", [128, 103680], BF16)
        BIGF = BIG[:].bitcast(F32)
        BIGI = BIG[:].bitcast(I32)
        XB = BIG[:, 0:32768]
        ARX = BIGF[:, 0:16384]
        ARA = BIG[:, 32768:49152]
        ARAF = BIGF[:, 16384:24576]
        ARB = BIG[:, 49152:73728]
        ARBF = BIGF[:, 24576:36864]
        ARE = BIG[:, 73728:103680]
        EF = BIGF[:, 36864:51840]

        ident = sb("ident", [128, 128], BF16)
        ones_bf = sb("ones_bf", [128, 128], BF16)
        cvf = sb("cvf", [128, 16], F32)
        sT = sb("sT", [128, 16], BF16)
        srep = sb("srep", [128, 8, 128], BF16)
        modcol = sb("modcol", [128, 32, 2], F32)
        bmodc = sb("bmodc", [128, 48], F32)
        n1c = sb("n1c", [128, 8], F32)
        n2c = sb("n2c", [128, 8], F32)
        AB = sb("AB", [128, 6, 8], F32)
        hcw = sb("hcw", [128, 12, 4], F32)
        fcw = sb("fcw", [128, 44, 4], F32)
        ssq = sb("ssq", [128, 64], F32)
        rstd = sb("rstd", [128, 64], F32)
        hid2T = ARE[0:64, 16384:18432]
        w3b = ARE[0:64, 18432:20480]
        nyqc = sb("nyqc", [128, 1], BF16)
        nyqr = sb("nyqr", [1, 128], BF16)
        tn = sb("tn", [128, 16], F32)
        negpi = sb("negpi", [128, 1], F32)
        epsc = sb("epsc", [128, 1], F32)

        pbank = [ps("pb%d" % i, [128, 512], F32) for i in range(6)]
        PB = [Buf("pb%d" % i) for i in range(6)]
        ptr = [ps("pt%d" % i, [128, 1024], BF16) for i in range(2)]
        PT = [Buf("pt%d" % i) for i in range(2)]
        pring = Ring(6)
        tring = Ring(2)

        def MM(out, lhsT, rhs, start, stop, reads, writes):
            S.add("pe", lambda e: e.matmul(out, lhsT=lhsT, rhs=rhs, start=start, stop=stop), reads, writes)

        def TR(out, in_, idn, reads, writes):
            S.add("pe", lambda e: e.transpose(out, in_, idn), reads, writes)

        def ACT(out, in_, func, reads, writes, **kw):
            S.add("act", lambda e: e.activation(out=out, in_=in_, func=func, **kw), reads, writes)

        def TT(eng, out, in0, in1, op, reads, writes):
            S.add(eng, lambda e: e.tensor_tensor(out=out, in0=in0, in1=in1, op=op), reads, writes)

        def TS(eng, out, in0, s1, s2, op0, op1, reads, writes):
            if s2 is None:
                S.add(eng, lambda e: e.tensor_scalar(out=out, in0=in0, scalar1=s1, scalar2=None, op0=op0), reads, writes)
            else:
                S.add(eng, lambda e: e.tensor_scalar(out=out, in0=in0, scalar1=s1, scalar2=s2, op0=op0, op1=op1),
                      reads, writes)

        def STT(eng, out, in0, scalar, in1, op0, op1, reads, writes):
            S.add(eng, lambda e: e.scalar_tensor_tensor(out=out, in0=in0, scalar=scalar, in1=in1, op0=op0, op1=op1),
                  reads, writes)

        def CP(eng, out, in_, reads, writes):
            if eng == "act":
                S.add("act", lambda e: e.activation(out=out, in_=in_, func=AF.Copy), reads, writes)
            else:
                S.add(eng, lambda e: e.tensor_copy(out=out, in_=in_), reads, writes)

        def RCP(out, in_, reads, writes):
            S.add("dve", lambda e: e.reciprocal(out=out, in_=in_), reads, writes)

        def MEMSET(eng, out, val, writes):
            S.add(eng, lambda e: e.memset(out, val), (), writes)

        def DBG(name, src_ap, reads, rows=None):
            if name in dbg_d:
                final_ops.append(S.dma("sp", dbg_d[name], src_ap, reads=reads))

        def bc(ap, shape):
            return ap.to_broadcast(list(shape))

        try:
            b_ident = Buf("ident")
            MEMSET("pool", ident[:], 0.0, [b_ident])
            S.add("pool", lambda e: e.affine_select(out=ident[:], in_=ident[:], pattern=[[-1, 128]],
                                                    compare_op=ALU.not_equal, fill=1.0, base=0, channel_multiplier=1),
                  [b_ident], [b_ident])
            b_ones = Buf("ones")
            MEMSET("dve", ones_bf[:], 1.0, [b_ones])
            b_negpi = Buf("negpi")
            MEMSET("dve", negpi[:], -PI, [b_negpi])
            MEMSET("dve", epsc[:], EPS, [b_negpi])
            SSQ = [Buf("ssq%d" % i) for i in range(64)]
            RSTD = [Buf("rstd%d" % i) for i in range(64)]
            MEMSET("dve", ssq[:], 0.0, SSQ)

            b_zf = Buf("zfT")
            b_hid1 = Buf("hid1")
            b_hid2 = Buf("hid2")
            b_ft = Buf("ftmp0")
            CK("c0")
            b_small = {k: Buf(k) for k in ["cvf", "sT", "srep", "bmodc", "n1c", "n2c", "hcw", "fcw", "nyq", "tn", "w3b"]}
            S.dma("sp", cvf[:], cvec_d, writes=[b_small["cvf"]])
            S.dma("sp", bmodc[:], bmodc_d, writes=[b_small["bmodc"]])
            S.dma("sp", n1c[:], n1c_d, writes=[b_small["n1c"]])
            S.dma("sp", n2c[:], n2c_d, writes=[b_small["n2c"]])
            S.dma("sp", hcw[:].rearrange("p a b -> p (a b)"), hcw_d, writes=[b_small["hcw"]])
            S.dma("sp", fcw[:].rearrange("p a b -> p (a b)"), fcw_d, writes=[b_small["fcw"]])
            S.dma("sp", nyqc[:], nyqc_d, writes=[b_small["nyq"]])
            S.dma("sp", nyqr[:], nyqr_d, writes=[b_small["nyq"]])
            S.dma("sp", tn[:], tn_d, writes=[b_small["tn"]])
            ACT(sT[:], cvf[:], AF.Silu, [b_small["cvf"]], [b_small["sT"]])
            sT3 = sT[:].rearrange("p (i t) -> p i t", t=2)
            CP("dve", srep[:], bc(sT3[:, :, 0:1], [128, 8, 128]), [b_small["sT"]], [b_small["srep"]])

            R0 = 8192
            rc = EF[:, R0 + 0:R0 + 776]
            DTm = EF[:, R0 + 776:R0 + 1288].rearrange("p (h i) -> p h i", h=4)
            rowqm = ARE[:, 2 * (R0 + 1288):2 * (R0 + 2056)].rearrange("p (h d i) -> p h d i", h=4, d=3)
            R1 = R0 + 256
            rl = EF[:, R1 + 1800:R1 + 1808]
            lg = EF[:, R1 + 1808:R1 + 1816]
            lgcol = EF[:, R1 + 1816:R1 + 1820].rearrange("p (c d) -> p c d", c=2)
            dec = EF[:, R1 + 1820:R1 + 1824].rearrange("p (c d) -> p c d", c=2)
            colfb = EF[:, R1 + 1824:R1 + 1832].rearrange("p (d h) -> p d h", d=2)
            wcx = EF[:, R1 + 1832:R1 + 1848].rearrange("p (t d h) -> p t d h", t=2, d=2)
            etmp = EF[:, R1 + 1848:R1 + 1976]
            etmp2 = EF[:, R1 + 1976:R1 + 2104]
            b_rc = Buf("rc")
            b_lg = Buf("lg")
            b_DT = Buf("DT")
            b_et = Buf("etmp")
            b_rtab = Buf("rtab")
            S.dma("sp", rc, rc_d, writes=[b_rc])
            S.dma("sp", rl, rlog_d.partition_broadcast(128), writes=[b_lg])
            ACT(lg, rl, AF.Exp, [b_lg], [b_lg], scale=-1.0)
            ACT(lg, lg, AF.Ln, [b_lg, b_rc], [b_lg], bias=rc[:, 774:775])
            TS("dve", lg, lg, -1.0, None, ALU.mult, None, [b_lg], [b_lg])
            for d in range(2):
                for half in range(2):
                    P = slice(half * 64, half * 64 + 64)
                    CP("dve", lgcol[P, :, d], lg[P, d * 4 + half:d * 4 + half + 3:2], [b_lg], [b_rtab])
            for h in range(4):
                ACT(etmp, rc[:, 0:128], AF.Exp, [b_rc, b_lg], [b_et], scale=lg[:, h:h + 1])
                TT("dve", DTm[:, h, :], etmp, rc[:, 256:384], ALU.mult, [b_et, b_rc], [b_DT])
                ACT(etmp2, rc[:, 128:256], AF.Exp, [b_rc, b_lg], [b_et], scale=lg[:, 4 + h:5 + h])
                TT("dve", etmp2, etmp2, rc[:, 384:512], ALU.mult, [b_et, b_rc], [b_et])
                TT("dve", DTm[:, h, :], DTm[:, h, :], etmp2, ALU.add, [b_et, b_DT], [b_DT])
                ACT(colfb[:, 0, h:h + 1], rc[:, 768:769], AF.Exp, [b_rc, b_lg], [b_rtab], scale=lg[:, h:h + 1])
                ACT(colfb[:, 1, h:h + 1], rc[:, 769:770], AF.Exp, [b_rc, b_lg], [b_rtab], scale=lg[:, 4 + h:5 + h])
                ACT(wcx[:, :, 0, h], rc[:, 770:772], AF.Exp, [b_rc, b_lg], [b_rtab], scale=lg[:, h:h + 1])
                ACT(wcx[:, :, 1, h], rc[:, 772:774], AF.Exp, [b_rc, b_lg], [b_rtab], scale=lg[:, 4 + h:5 + h])
            ACT(dec, lgcol, AF.Exp, [b_rtab], [b_rtab], scale=128.0)
            for c in range(2):
                if c == 0:
                    MEMSET("dve", rowqm.rearrange("p h d i -> p (h d i)"), 0.0, [b_rtab])
                for half in range(2):
                    hs = slice(half * 64, half * 64 + 64)
                    h_ = 2 * c + half
                    ACT(rowqm[hs, h_, 0, :], rc[hs, 512:640], AF.Exp, [b_rc, b_rtab], [b_rtab], scale=lgcol[hs, c, 0:1])
                    ACT(rowqm[hs, h_, 1, :], rc[hs, 640:768], AF.Exp, [b_rc, b_rtab], [b_rtab], scale=lgcol[hs, c, 1:2])
                    MEMSET("dve", rowqm[hs, h_, 2, :], 1.0, [b_rtab])

            CK("c1")
            wr = [XB[:, s * 4096:(s + 1) * 4096].rearrange("p (i n) -> p i n", i=8) for s in range(4)]
            WR = [Buf("wr%d" % s) for s in range(4)]
            wring = Ring(4)
            wmv = wmod_d.rearrange("(i p) n -> p i n", p=128)
            winv = win_d.rearrange("(i p) n -> p i n", p=128)

            def load_wblock(src):
                s = wring.nxt()
                S.dma("pool", wr[s], src, writes=[WR[s]])
                return s

            b_modcol = Buf("modcol")
            b_AB = Buf("AB")

            def mod_col_half(j, which, pi=None):
                s = load_wblock(wmv[:, :, j * 512:(j + 1) * 512])
                if pi is None:
                    pi = pring.nxt()
                for cc in range(4):
                    for k in range(8):
                        MM(pbank[pi][:, cc * 2:cc * 2 + 2], wr[s][:, k, cc * 128:(cc + 1) * 128],
                           sT3[:, k, :], k == 0, k == 7, [WR[s], b_small["sT"]], [PB[pi]])
                m0 = which * 8 + (j % 2) * 4
                TT("dve", modcol[:, m0:m0 + 4, :], pbank[pi][:, 0:8].rearrange("p (a b) -> p a b", b=2),
                   bc(bmodc[:, j * 4:j * 4 + 4].unsqueeze(2), [128, 4, 2]), ALU.add,
                   [PB[pi], b_small["bmodc"]], [b_modcol])

            def mod_bc_half(j, dst, dst_buf, tmp, tmp_buf, wbuf, wbuf_b):
                S.dma("pool", wbuf, wmv[:, :, j * 512:(j + 1) * 512], writes=[wbuf_b])
                S.dma("sp", tmp, bmodr_d[0:1, j * 512:(j + 1) * 512].partition_broadcast(128), writes=[tmp_buf])
                pi = pring.nxt()
                for k in range(8):
                    MM(pbank[pi][:], srep[:, k, :], wbuf[:, k, :], k == 0, k == 7, [wbuf_b, b_small["srep"]], [PB[pi]])
                h0 = (j % 2) * 512
                TT("dve", dst[:, h0:h0 + 512], pbank[pi][:], tmp, ALU.add, [PB[pi], tmp_buf], [dst_buf])

            mod_col_half(0, 0)
            mod_col_half(1, 0)
            mod_col_half(2, 1)
            mod_col_half(3, 1)
            STT("dve", AB[:, 0, :], modcol[:, 8:16, 0], 1.0, n1c[:], ALU.add, ALU.mult, [b_modcol, b_small["n1c"]], [b_AB])
            CP("dve", AB[:, 1, :], modcol[:, 0:8, 0], [b_modcol], [b_AB])
            STT("dve", AB[:, 2, :], modcol[:, 8:16, 1], 1.0, n1c[:], ALU.add, ALU.mult, [b_modcol, b_small["n1c"]], [b_AB])
            CP("dve", AB[:, 3, :], modcol[:, 0:8, 1], [b_modcol], [b_AB])

            CK("c2")
            xt = [ARX[:, 8192 + i * 1024:8192 + (i + 1) * 1024] for i in range(3)]
            XT = [Buf("xt%d" % i) for i in range(3)]
            xtring = Ring(3)
            xn = [XB[:, 22528 + i * 1024:22528 + (i + 1) * 1024] for i in range(2)]
            XN = [Buf("xn%d" % i) for i in range(2)]
            xnring = Ring(2)
            b_rstd = Buf("rstd")
            evac_flip = [0]

            def norm_s1(src, src_bufs, scol, xn_list, xn_bufs, xn_ring):
                xi = xn_ring.nxt()
                ACT(xn_list[xi], src, AF.Square, src_bufs + [SSQ[scol]], [SSQ[scol], xn_bufs[xi]],
                    accum_out=ssq[:, scol:scol + 1])
                ACT(rstd[:, scol:scol + 1], ssq[:, scol:scol + 1], AF.Sqrt, [SSQ[scol], b_negpi], [RSTD[scol]], scale=1.0 / D,
                    bias=epsc[:, 0:1])
                RCP(rstd[:, scol:scol + 1], rstd[:, scol:scol + 1], [RSTD[scol]], [RSTD[scol]])
                TS("dve", xn_list[xi], src, rstd[:, scol:scol + 1], None, ALU.mult, None, src_bufs + [RSTD[scol]],
                   [xn_bufs[xi]])
                return xi

            def norm_s2(xi, acol, bcol, dst_fn, dst_bufs, xn_list, xn_bufs):
                ta = tring.nxt()
                tb = tring.nxt()
                NA = 3
                for c in range(NA):
                    TR(ptr[ta][:, c * 128:(c + 1) * 128], xn_list[xi][:, c * 128:(c + 1) * 128], ident[:],
                       [xn_bufs[xi], b_ident], [PT[ta]])
                for c in range(NA, 8):
                    TR(ptr[tb][:, (c - NA) * 128:(c - NA + 1) * 128], xn_list[xi][:, c * 128:(c + 1) * 128], ident[:],
                       [xn_bufs[xi], b_ident], [PT[tb]])
                for c in range(8 - NA):
                    if c < NA:
                        ACT(dst_fn(c), ptr[ta][:, c * 128:(c + 1) * 128], AF.Identity, [PT[ta], b_AB], [dst_bufs[0]],
                            scale=AB[:, acol, c:c + 1], bias=AB[:, bcol, c:c + 1])
                    cc_ = c + NA
                    TS("dve", dst_fn(cc_), ptr[tb][:, c * 128:(c + 1) * 128], AB[:, acol, cc_:cc_ + 1],
                       AB[:, bcol, cc_:cc_ + 1], ALU.mult, ALU.add, [PT[tb], b_AB], [dst_bufs[1]])

            hT = ARA[:].rearrange("p (c t) -> p c t", c=8)
            HT = [Buf("hT%d" % i) for i in range(2 * NT)]
            hcT = ARB[:, 8192:10240].rearrange("p (c t) -> p c t", c=8)
            HCT = [Buf("hcT%d" % i) for i in range(4)]
            def p1_s1(i):
                xi = xtring.nxt()
                src = x_d[i * 128:(i + 1) * 128, :] if i < NT else ctx_d[(i - NT) * 128:(i - NT + 1) * 128, :]
                S.dma("sp", xt[xi], src, writes=[XT[xi]])
                return norm_s1(xt[xi], [XT[xi]], i, xn, XN, xnring)

            def p1_s2(i, xi):
                if i < NT:
                    norm_s2(xi, 0, 1, lambda c: hT[:, c, i * 128:(i + 1) * 128], HT[2 * i:2 * i + 2], xn, XN)
                else:
                    j = i - NT
                    norm_s2(xi, 2, 3, lambda c: hcT[:, c, j * 128:(j + 1) * 128], HCT[2 * j:2 * j + 2], xn, XN)
            if "hT" in dbg_d:
                for c in range(8):
                    final_ops.append(S.dma("sp", dbg_d["hT"][c * 128:(c + 1) * 128, :], hT[:, c, :], reads=HT))

            CK("p1")
            rope = ARBF[:, 0:4096]
            b_rope = Buf("rope")
            S.dma("sp", rope, rope_d, writes=[b_rope])
            qrot = ARE[:, 0:4096].rearrange("p (c t) -> p c t", c=2)
            krot = ARE[:, 4096:8192].rearrange("p (c t) -> p c t", c=2)
            vtok = ARE[:, 8192:16384].rearrange("p (i n) -> p i n", i=NT)
            QR = [[Buf("qr%d_%d" % (c, q)) for q in range(4)] for c in range(2)]
            KR = [[Buf("kr%d_%d" % (c, q)) for q in range(4)] for c in range(2)]
            VT = [Buf("vt%d" % i) for i in range(NT)]
            tA = [ARX[:, 12288 + i * 512:12288 + (i + 1) * 512] for i in range(4)]
            TA = [Buf("tA%d" % i) for i in range(4)]
            taring = Ring(2)

            def proj_fm(s, col0, q, pi):
                for k in range(8):
                    MM(pbank[pi][:], wr[s][:, k, col0:col0 + 128], hT[:, k, q * 512:(q + 1) * 512], k == 0, k == 7,
                       [WR[s]] + HT[q * 8:q * 8 + 8], [PB[pi]])

            s_qk = load_wblock(winv[:, :, 1536:2048])
            s_qks = load_wblock(winv[:, :, 3072:3584])
            s_v = load_wblock(winv[:, :, 2048:2560])
            def p2_block(q):
                for cc in range(4):
                    isk = cc >= 2
                    dst = krot if isk else qrot
                    dbuf = KR if isk else QR
                    p1 = pring.nxt()
                    proj_fm(s_qk, cc * 128, q, p1)
                    p2 = pring.nxt()
                    proj_fm(s_qks, cc * 128, q, p2)
                    ta = taring.nxt()
                    cosv = rope[:, q * 512:(q + 1) * 512]
                    sinv = rope[:, L + q * 512:L + (q + 1) * 512]
                    if isk:
                        STT("dve", tA[2 * ta], pbank[p1][:], 0.125, cosv, ALU.mult, ALU.mult, [PB[p1], b_rope], [TA[2 * ta]])
                        STT("dve", tA[2 * ta + 1], pbank[p2][:], 0.125, sinv, ALU.mult, ALU.mult, [PB[p2], b_rope],
                            [TA[2 * ta + 1]])
                    else:
                        TT("dve", tA[2 * ta], pbank[p1][:], cosv, ALU.mult, [PB[p1], b_rope], [TA[2 * ta]])
                        TT("dve", tA[2 * ta + 1], pbank[p2][:], sinv, ALU.mult, [PB[p2], b_rope], [TA[2 * ta + 1]])
                    TT("dve", dst[:, cc % 2, q * 512:(q + 1) * 512], tA[2 * ta], tA[2 * ta + 1], ALU.add,
                       [TA[2 * ta], TA[2 * ta + 1]], [dbuf[cc % 2][q]])
                for i in range(4 * q, 4 * q + 4):
                    pi = pring.nxt()
                    for k in range(8):
                        MM(pbank[pi][:], hT[:, k, i * 128:(i + 1) * 128], wr[s_v][:, k, :], k == 0, k == 7,
                           [WR[s_v]] + HT[2 * i:2 * i + 2], [PB[pi]])
                    CP("act", vtok[:, i, :], pbank[pi][:], [PB[pi]], [VT[i]])

            xi_cur = p1_s1(0)
            for i in range(NT + 2):
                xi_nxt = p1_s1(i + 1) if i + 1 < NT + 2 else None
                p1_s2(i, xi_cur)
                xi_cur = xi_nxt
                if i < NT and i % 4 == 3:
                    p2_block(i // 4)
            gT = ARB[:, 10240:18432].rearrange("p (c t) -> p c t", c=4)
            GTB = [[Buf("gT%d_%d" % (c, q)) for q in range(4)] for c in range(4)]
            s_g = load_wblock(winv[:, :, 2560:3072])

            def gproj_job(j):
                cg, q = j // 4, j % 4
                pi = pring.nxt()
                proj_fm(s_g, cg * 128, q, pi)
                ACT(gT[:, cg, q * 512:(q + 1) * 512], pbank[pi][:], AF.Silu, [PB[pi]], [GTB[cg][q]])

            if "qk" in dbg_d:
                for c in range(2):
                    for (nm, src, bb) in (("q", qrot, QR), ("k", krot, KR)):
                        r0 = (0 if nm == "q" else 256) + c * 128
                        final_ops.append(S.dma("sp", dbg_d["qk"][r0:r0 + 128, :], src[:, c, :], reads=bb[c]))

            CK("p2")
            XU = 8192
            Sallb = XB[:, 2 * XU:2 * XU + 4096].rearrange("p (n c e) -> p n c e", n=NT, c=2)
            o = XU + 2048
            Pm = [XB[:, 2 * (o + i * 256):2 * (o + i * 256) + 512].rearrange("p (h i) -> p h i", h=4) for i in range(2)]
            o += 512
            sq = [XB[:, 2 * (o + i * 256):2 * (o + i * 256) + 512] for i in range(2)]
            o += 512
            sg = [XB[:, 2 * (o + i * 256):2 * (o + i * 256) + 512].rearrange("p (h i) -> p h i", h=4) for i in range(2)]
            o += 512
            rsb = ARX[:, o:o + 512]
            o += 512
            tyb = ARX[:, o:o + 512].rearrange("p (h i) -> p h i", h=4)
            o += 512
            S32 = [ARX[:, o + d * 256:o + (d + 1) * 256].rearrange("p (c e) -> p c e", c=2) for d in range(2)]
            o += 512
            tmpS = ARX[:, o:o + 256].rearrange("p (c e) -> p c e", c=2)
            o += 256
            Sfb = [XB[:, 2 * (o + i * 128):2 * (o + i * 128) + 256].rearrange("p (c e) -> p c e", c=2) for i in range(2)]
            o += 256
            ktile = [XB[:, 2 * (o + i * 128):2 * (o + i * 128) + 256] for i in range(2)]
            o += 256
            qfb0 = XB[:, 2 * o:2 * o + 1536]
            o += 768
            qfb1 = XB[:, 2 * o:2 * o + 1536]
            qfb = [q_.rearrange("p (h d i) -> p h d i", h=4, d=3) for q_ in (qfb0, qfb1)]
            kct = XB[:, 2 * o:2 * o + 1024].rearrange("p (t d n) -> p t d n", t=2, d=2)
            o += 512
            vct = XB[:, 2 * o:2 * o + 1024].rearrange("p (t n) -> p t n", t=2)
            o += 512
            sg.append(XB[:, 2 * o:2 * o + 512].rearrange("p (h i) -> p h i", h=4))
            o += 256
            assert o <= 16384, o
            ret_bufs = {k: Buf(k) for k in ["Sallb", "rsb", "tyb", "tmpS", "kct", "vct"]}
            PM = [Buf("pm%d" % i) for i in range(2)]
            SQ = [Buf("sq%d" % i) for i in range(2)]
            SG = [Buf("sg%d" % i) for i in range(3)]
            S32B = [Buf("s32_%d" % i) for i in range(2)]
            SFB = [Buf("sfb%d" % i) for i in range(2)]
            KTL = [Buf("kt%d" % i) for i in range(2)]
            QFB = [Buf("qfb%d" % i) for i in range(2)]
            SALLB = [Buf("sallb%d" % i) for i in range(NT)]
            all_ret = list(ret_bufs.values()) + PM + SQ + SG + S32B + SFB + KTL + QFB + SALLB
            S.alias(all_ret, XT + XN + TA)

            CK("rconst")
            for t in range(2):
                pk = pring.nxt()
                for k in range(8):
                    MM(pbank[pk][:, 0:256], hcT[:, k, t * 128:(t + 1) * 128], wr[s_qk][:, k, 256:512], k == 0, k == 7,
                       [WR[s_qk]] + HCT[2 * t:2 * t + 2], [PB[pk]])
                pv = pring.nxt()
                for k in range(8):
                    MM(pbank[pv][:], hcT[:, k, t * 128:(t + 1) * 128], wr[s_v][:, k, :], k == 0, k == 7,
                       [WR[s_v]] + HCT[2 * t:2 * t + 2], [PB[pv]])
                for d in range(2):
                    STT("dve", kct[:, t, d, :].rearrange("p (h e) -> p h e", h=4),
                        pbank[pk][:, 0:256].rearrange("p (h e) -> p h e", h=4), 0.125,
                        bc(wcx[:, t, d, :].unsqueeze(2), [128, 4, 64]), ALU.mult, ALU.mult,
                        [PB[pk], b_rtab], [ret_bufs["kct"]])
                CP("act", vct[:, t, :], pbank[pv][:], [PB[pv]], [ret_bufs["vct"]])
            for d in range(2):
                pi = pring.nxt()
                for h in range(4):
                    c = h // 2
                    for t in range(2):
                        MM(pbank[pi][:, h * 128:(h + 1) * 128], kct[:, t, d, c * 128:(c + 1) * 128],
                           vct[:, t, h * 128:(h + 1) * 128], t == 0, t == 1,
                           [ret_bufs["kct"], ret_bufs["vct"]], [PB[pi]])
                pv4 = pbank[pi][:].rearrange("p (s e) -> p s e", s=4)
                CP("dve", S32[d][0:64, :, :], pv4[0:64, 0:4:2, :], [PB[pi]], [S32B[d]])
                CP("dve", S32[d][64:128, :, :], pv4[64:128, 1:4:2, :], [PB[pi]], [S32B[d]])
            DBG("s_f", S32[0], [S32B[0]])
            DBG("s_b", S32[1], [S32B[1]])

            CK("ctx")
            Sallf = ARB[:, 0:4096].rearrange("p (n c e) -> p n c e", n=NT, c=2)
            B_SF = Buf("sallf")
            S.alias([B_SF], [b_rope])
            CP("act", Sallb[:, 15, :, :], S32[1], [S32B[1]], [SALLB[15]])
            CP("act", Sallf[:, 0, :, :], S32[0], [S32B[0]], [B_SF])
            kt4 = [ktile[0], ktile[1], sg[0].rearrange("p h i -> p (h i)")[:, 0:256], sg[1].rearrange("p h i -> p (h i)")[:, 0:256]]
            KT4 = [KTL[0], KTL[1], SG[0], SG[1]]
            tmpS2 = [tmpS, sq[0].bitcast(F32).rearrange("p (c e) -> p c e", c=2)]
            TMPS2 = [ret_bufs["tmpS"], SQ[0]]
            ktring = Ring(4)

            Dm = [[sg[2].rearrange("p h i -> p (h i)")[:, (d * 2 + c) * 128:(d * 2 + c + 1) * 128] for c in range(2)]
                  for d in range(2)]
            b_Dm = SG[2]
            for d in range(2):
                for c in range(2):
                    TS("dve", Dm[d][c], ident[:], dec[:, c, d:d + 1], None, ALU.mult, None, [b_ident, b_rtab], [b_Dm])

            def scan_step(n, d, s_prev, prev_bufs, s_new, new_bufs):
                ti = tring.nxt()
                for c in range(2):
                    TR(ptr[ti][:, c * 128:(c + 1) * 128], krot[:, c, n * 128:(n + 1) * 128], ident[:],
                       [KR[c][n // 4], b_ident], [PT[ti]])
                ki = ktring.nxt()
                TT("dve", kt4[ki].rearrange("p (h e) -> p h e", h=4),
                   ptr[ti][:, 0:256].rearrange("p (h e) -> p h e", h=4),
                   bc(colfb[:, d, :].unsqueeze(2), [128, 4, 64]), ALU.mult, [PT[ti], b_rtab], [KT4[ki]])
                pi = pring.nxt()
                for h in range(4):
                    c = h // 2
                    MM(pbank[pi][:, h * 128:(h + 1) * 128], kt4[ki][:, c * 128:(c + 1) * 128],
                       vtok[:, n, h * 128:(h + 1) * 128], True, False, [KT4[ki], VT[n]], [PB[pi]])
                    MM(pbank[pi][:, h * 128:(h + 1) * 128], Dm[d][c], s_prev[:, c, :], False, True,
                       [b_Dm] + prev_bufs, [PB[pi]])
                pv4 = pbank[pi][:].rearrange("p (s e) -> p s e", s=4)
                CP("act", s_new[0:64, :, :], pv4[0:64, 0:4:2, :], [PB[pi]], new_bufs)
                CP("act", s_new[64:128, :, :], pv4[64:128, 1:4:2, :], [PB[pi]], new_bufs)

            gproj_job(0)
            for s_ in range(15):
                scan_step(s_, 0, Sallf[:, s_, :, :], [B_SF], Sallf[:, s_ + 1, :, :], [B_SF])
                scan_step(15 - s_, 1, Sallb[:, 15 - s_, :, :], [SALLB[15 - s_]], Sallb[:, 14 - s_, :, :], [SALLB[14 - s_]])
                gproj_job(s_ + 1)

            S.alias([QFB[1]], [ret_bufs["kct"], ret_bufs["vct"]])
            CK("bwd")
            mixR = ARE[:, 21504:29696].rearrange("p (c t) -> p c t", c=4)
            mixHy = ARA[:, 8192:16384].rearrange("p (c t) -> p c t", c=4)

            def mix_chunk(k):
                return mixHy[:, k, :] if k < 4 else mixR[:, k - 4, :]
            MIXR = [Buf("mixr%d" % i) for i in range(NT)]
            stt = {}

            def O1(n):
                tsl = slice(n * 128, (n + 1) * 128)
                qbuf = [QR[0][n // 4], QR[1][n // 4]]
                kbuf = [KR[0][n // 4], KR[1][n // 4]]
                qi = n % 2
                pS = n % 2
                TT("dve", qfb[qi].rearrange("p (c hh) d i -> p c (hh d) i", c=2),
                   bc(qrot[:, :, tsl].unsqueeze(2), [128, 2, 6, 128]),
                   rowqm.rearrange("p (c hh) d i -> p c (hh d) i", c=2), ALU.mult, qbuf + [b_rtab], [QFB[qi]])
                for h in range(4):
                    MM(pbank[pS][:, h * 128:(h + 1) * 128], krot[:, h // 2, tsl], qfb[qi][:, h, 2, :], True, True,
                       kbuf + [QFB[qi]], [PB[pS]])

            def O2(n):
                pS = n % 2
                pmi = n % 2
                TT("dve", Pm[pmi], pbank[pS][:].rearrange("p (h i) -> p h i", h=4), DTm, ALU.mult,
                   [PB[pS], b_DT], [PM[pmi]])

            def O3(n):
                qi = n % 2
                pmi = n % 2
                pO = 2 + (n % 2)
                for h in range(4):
                    c = h // 2
                    osl = pbank[pO][:, h * 128:(h + 1) * 128]
                    MM(osl, vtok[:, n, h * 128:(h + 1) * 128], Pm[pmi][:, h, :], True, False, [VT[n], PM[pmi]], [PB[pO]])
                    MM(osl, Sallf[:, n, c, :], qfb[qi][:, h, 0, :], False, False, [B_SF, QFB[qi]], [PB[pO]])
                    MM(osl, Sallb[:, n, c, :], qfb[qi][:, h, 1, :], False, True, [SALLB[n], QFB[qi]], [PB[pO]])

            def O4a(n):
                pO = 2 + (n % 2)
                sqi = n % 2
                ACT(sq[sqi], pbank[pO][:], AF.Square, [PB[pO]], [SQ[sqi]])
                pR = 4
                MM(pbank[pR][:], ones_bf[:], sq[sqi], True, True, [b_ones, SQ[sqi]], [PB[pR]])
                ACT(rsb, pbank[pR][:], AF.Ln, [PB[pR], b_negpi], [ret_bufs["rsb"]], scale=1.0 / 128, bias=epsc[:, 0:1])
                ACT(rsb, rsb, AF.Exp, [ret_bufs["rsb"]], [ret_bufs["rsb"]], scale=-0.5)

            def O4b(n):
                tsl = slice(n * 128, (n + 1) * 128)
                pO = 2 + (n % 2)
                TT("dve", tyb, gT[:, :, tsl], rsb.rearrange("p (h i) -> p h i", h=4), ALU.mult,
                   [ret_bufs["rsb"]] + [GTB[c_][n // 4] for c_ in range(4)], [ret_bufs["tyb"]])
                TT("dve", mixR[:, :, tsl], pbank[pO][:].rearrange("p (h i) -> p h i", h=4), tyb, ALU.mult,
                   [PB[pO], ret_bufs["tyb"]], [MIXR[n]])

            for i in range(NT + 4):
                if 0 <= i - 4 < NT:
                    O4b(i - 4)
                if 0 <= i - 3 < NT:
                    O4a(i - 3)
                if 0 <= i - 2 < NT:
                    O3(i - 2)
                if 0 <= i - 1 < NT:
                    O2(i - 1)
                if i < NT:
                    O1(i)
                if i in (3, 7, 11, 15):
                    jm = 6 + (i - 3) // 4
                    mod_col_half(jm, 2 if jm < 8 else 3, pi=5)
            if "yret" in dbg_d:
                for c in range(4):
                    final_ops.append(S.dma("sp", dbg_d["yret"][c * 128:(c + 1) * 128, :], mixR[:, c, :], reads=MIXR))
            CK("ret")
            b_g1b = Buf("g1b")
            b_g2b = Buf("g2b")
            b_bmt = Buf("bmt")
            STT("dve", AB[:, 4, :], modcol[:, 24:32, 0], 1.0, n2c[:], ALU.add, ALU.mult, [b_modcol, b_small["n2c"]], [b_AB])
            CP("dve", AB[:, 5, :], modcol[:, 16:24, 0], [b_modcol], [b_AB])

            ztok = ARB[:].rearrange("p (i n) -> p i n", i=NT)
            ZT = [[Buf("zt%d_%d" % (g, i)) for i in range(NT)] for g in range(3)]
            S.alias([b for l in ZT for b in l], [b for l in GTB for b in l] + [b_rope, B_SF] + HCT)
            o = XU
            araw = [ARX[:, o + i * 2052:o + i * 2052 + 2050] for i in range(2)]
            o += 2 * 2052
            zc = [XB[:, 2 * o:2 * o + 2048]]
            o += 1024
            rcv = [ARX[:, 6144:8192], ARX[:, o:o + 2048]]
            o += 2048
            assert o <= 16384, o
            rconv = ARX[:, 6144:8192]
            ARAW = [Buf("araw%d" % i) for i in range(2)]
            ZC = [Buf("zc0")]
            RCV = [Buf("rcv%d" % i) for i in range(2)]
            b_rconv = Buf("rconv")
            S.alias(ARAW + ZC, all_ret)
            S.alias([b_rconv], [WR[3]])
            S.alias([RCV[0]], [WR[3]])
            S.alias([RCV[1]], all_ret)
            zfT = EF[0:33, 0:2048]
            hid1T = EF[0:64, 2048:4096]
            fa = EF[0:64, 4096:4608]
            fk = EF[0:64, 4608:5120]
            fki = BIGI[0:64, 36864 + 5120:36864 + 5632]
            fw = EF[0:64, 5632:6144]
            hw1 = EF[0:33, 6144:6208]
            hw2 = EF[0:64, 6208:6272]
            hyp = EF[0:64, 6272:6276]
            b_hw = Buf("hw")
            b_ft = Buf("ftmp")
            qkv_bufs = [b for l in QR + KR for b in l] + VT
            S.alias([b_hw, b_ft, b_zf, b_hid1], qkv_bufs)
            S.alias([b_hid2, b_small["w3b"]], [b_rc, b_lg, b_DT, b_et, b_rtab])
            S.dma("sp", zfT, zfT_d, writes=[b_zf])
            S.dma("sp", hw1, hw1_d, writes=[b_hw])
            S.dma("sp", hw2, hw2_d, writes=[b_hw])
            S.dma("sp", hyp, hyp_d, writes=[b_hw])
            w3sc = EF[0:64, 6400:7424]
            b_w3sc = Buf("w3sc")
            S.alias([b_w3sc], qkv_bufs)
            for od_ in range(2):
                S.dma("sp", w3sc[:, 0:512], hw3_d[:, od_ * 512:(od_ + 1) * 512], writes=[b_w3sc])
                S.dma("sp", w3sc[:, 512:1024], hw3_d[:, 1024 + od_ * 512:1024 + (od_ + 1) * 512], writes=[b_w3sc])
                TT("dve", w3b[:, od_ * 512:(od_ + 1) * 512], w3sc[:, 0:512], w3sc[:, 512:1024], ALU.add,
                   [b_w3sc], [b_small["w3b"]])
                TT("dve", w3b[:, 1024 + od_ * 512:1024 + (od_ + 1) * 512], w3sc[:, 0:512], w3sc[:, 512:1024], ALU.subtract,
                   [b_w3sc], [b_small["w3b"]])

            def sin_layer(pi, bcol, fcol, out_ap, out_buf):
                TS("dve", fa, pbank[pi][0:64, :], bcol, fcol, ALU.add, ALU.mult, [PB[pi], b_hw], [b_ft])
                TS("dve", fki, fa, 1.0 / (2 * PI), 16.5, ALU.mult, ALU.add, [b_ft], [b_ft])
                CP("dve", fk, fki, [b_ft], [b_ft])
                STT("dve", fw, fk, -2 * PI, fa, ALU.mult, ALU.add, [b_ft], [b_ft])
                TS("dve", fk, fw, -33 * PI, 2 * PI, ALU.is_lt, ALU.mult, [b_ft], [b_ft])
                STT("dve", fw, fw, 33 * PI, fk, ALU.add, ALU.add, [b_ft], [b_ft])
                ACT(out_ap, fw, AF.Sin, [b_ft, b_negpi], [out_buf], bias=negpi[0:64, 0:1])

            def filt_unit(u):
                q = u % 4
                pi = pring.nxt()
                if u < 4:
                    MM(pbank[pi][0:64, :], hw1, zfT[:, q * 512:(q + 1) * 512], True, True, [b_hw, b_zf], [PB[pi]])
                    sin_layer(pi, hyp[:, 0:1], hyp[:, 1:2], hid1T[:, q * 512:(q + 1) * 512], b_hid1)
                else:
                    MM(pbank[pi][0:64, :], hw2, hid1T[:, q * 512:(q + 1) * 512], True, True, [b_hw, b_hid1], [PB[pi]])
                    sin_layer(pi, hyp[:, 2:3], hyp[:, 3:4], hid2T[:, q * 512:(q + 1) * 512], b_hid2)
            for i in range(2):
                MEMSET("pool", araw[i][:, 0:1], 0.0, [ARAW[i]])
                MEMSET("pool", araw[i][:, 2049:2050], 0.0, [ARAW[i]])
            arring = Ring(2)
            zcring = Ring(1)
            evq = [0]

            hy_slots = {}

            def hy_A(chunk):
                g, cc = chunk // 4, chunk % 4
                if cc == 0:
                    hy_slots[g] = load_wblock(winv[:, :, g * 512:(g + 1) * 512])
                s_ = hy_slots[g]
                ai = arring.nxt()
                for q in range(4):
                    pi = pring.nxt()
                    proj_fm(s_, cc * 128, q, pi)
                    CP("act", araw[ai][:, 1 + q * 512:1 + (q + 1) * 512], pbank[pi][:], [PB[pi]], [ARAW[ai]])
                    ACT(rcv[ai][:, q * 512:(q + 1) * 512], pbank[pi][:], AF.Identity, [PB[pi], b_small["hcw"]], [RCV[ai]],
                        scale=hcw[:, chunk, 1:2], bias=hcw[:, chunk, 3:4])
                return ai

            def hy_B(chunk, ai):
                wq = hcw[:, chunk, :]
                zi = zcring.nxt()
                STT("dve", rcv[ai], araw[ai][:, 0:2048], wq[:, 0:1], rcv[ai], ALU.mult, ALU.add,
                    [ARAW[ai], b_small["hcw"], RCV[ai]], [RCV[ai]])
                STT("dve", zc[zi], araw[ai][:, 2:2050], wq[:, 2:3], rcv[ai], ALU.mult, ALU.add,
                    [ARAW[ai], b_small["hcw"], RCV[ai]], [ZC[zi]])
                return zi

            def hy_C(chunk, zi):
                g = chunk // 4
                for half in range(2):
                    ti = tring.nxt()
                    for a in range(8):
                        i = half * 8 + a
                        TR(ptr[ti][:, a * 128:(a + 1) * 128], zc[zi][:, i * 128:(i + 1) * 128], ident[:],
                           [ZC[zi], b_ident], [PT[ti]])
                    CP("act", ztok[:, half * 8:half * 8 + 8, chunk * 128:(chunk + 1) * 128],
                       ptr[ti][:].rearrange("p (a c) -> p a c", a=8), [PT[ti]], ZT[g][half * 8:half * 8 + 8])

            ai_cur = hy_A(0)
            for chunk in range(12):
                ai_nxt = hy_A(chunk + 1) if chunk + 1 < 12 else None
                zi = hy_B(chunk, ai_cur)
                hy_C(chunk, zi)
                if chunk < 8:
                    filt_unit(chunk)
                ai_cur = ai_nxt
            if "ztok" in dbg_d:
                for i in range(NT):
                    final_ops.append(S.dma("sp", dbg_d["ztok"][i * 128:(i + 1) * 128, :], ztok[:, i, :],
                                           reads=[ZT[0][i], ZT[1][i], ZT[2][i]]))

            CK("hyproj")
            ksum = ARA[:, 0:8192].rearrange("p (i n) -> p i n", i=NT)
            kdiff = ARA[:, 8192:16384].rearrange("p (i n) -> p i n", i=NT)
            KS = [Buf("ks%d" % i) for i in range(NT)]
            S.alias(KS, HT)
            Yre = ARE[:, 0:8192].rearrange("p (j n) -> p j n", j=16)
            Yim = ARE[:, 8192:16384].rearrange("p (j n) -> p j n", j=16)
            YB = [Buf("y%d" % j) for j in range(16)]
            S.alias(YB, qkv_bufs + [b_hw, b_ft, b_zf, b_hid1])
            fblk = [XB[:, i * 4096:(i + 1) * 4096].rearrange("p (i s r) -> p i s r", i=16, s=2) for i in range(2)]
            FB = [Buf("fb%d" % i) for i in range(2)]
            gt = [ARX[:, 4096 + i * 512:4096 + (i + 1) * 512] for i in range(4)]
            GT = [Buf("gt%d" % i) for i in range(4)]
            yhy = [XB[:, 2 * (6144 + i * 256):2 * (6144 + i * 256) + 512] for i in range(2)]
            YH = [Buf("yhy%d" % i) for i in range(2)]
            S.alias(FB + GT + YH, WR + [b_rconv] + RCV)
            wpc = [XB[:, 2 * (6656 + i * 512):2 * (6656 + i * 512) + 1024].rearrange("p (i n) -> p i n", i=8) for i in range(2)]
            bpc = [ARX[:, 7680 + i * 128:7680 + (i + 1) * 128] for i in range(2)]
            gpc = [ARX[:, 7936 + i * 128:7936 + (i + 1) * 128] for i in range(2)]
            WPC = [Buf("wpc%d" % i) for i in range(2)]
            BPC = [Buf("bpc%d" % i) for i in range(2)]
            GPC = [Buf("gpc%d" % i) for i in range(2)]
            S.alias(WPC + BPC + GPC, WR + [b_rconv] + RCV)

            def g1_piece(c8):
                i = c8 % 2
                c0 = 2048 + c8 * 128
                S.dma("pool", wpc[i], wmv[:, :, c0:c0 + 128], writes=[WPC[i]])
                S.dma("sp", bpc[i], bmodr_d[0:1, c0:c0 + 128].partition_broadcast(128), writes=[BPC[i]])
                pi = pring.nxt()
                for k in range(8):
                    MM(pbank[pi][:, 0:128], srep[:, k, :], wpc[i][:, k, :], k == 0, k == 7, [WPC[i], b_small["srep"]],
                       [PB[pi]])
                TT("dve", gpc[i], pbank[pi][:, 0:128], bpc[i], ALU.add, [PB[pi], BPC[i]], [GPC[i]])
                TT("dve", woutS[:, :, c8 * 128:(c8 + 1) * 128], woutS[:, :, c8 * 128:(c8 + 1) * 128],
                   bc(gpc[i].unsqueeze(1), [128, 8, 128]), ALU.mult, [b_wout, GPC[i]], [b_wout])
            o = XU
            hbb = ARX[:, o:o + 512]; o += 512
            absb = ARX[:, o:o + 512]; o += 512
            wint = [ARX[:, o + i * 512:o + (i + 1) * 512] for i in range(2)]; o += 1024
            kfw = [ARX[:, o + i * 512:o + (i + 1) * 512] for i in range(2)]; o += 1024
            Ksb = [ARX[:, o + i * 512:o + (i + 1) * 512] for i in range(4)]; o += 2048
            tq = [ARX[:, o + i * 512:o + (i + 1) * 512] for i in range(4)]; o += 2048
            ynq = XB[0:1, 2 * o:2 * o + 512]; o += 256
            knq = ARX[0:1, o:o + 512]; o += 512
            assert o <= 16384, o
            b_hbb = Buf("hbb")
            b_absb = Buf("absb")
            WINT = [Buf("win%d" % i) for i in range(2)]
            KFW = [Buf("kfw%d" % i) for i in range(2)]
            KSB = [Buf("ksb%d" % i) for i in range(4)]
            TQ = [Buf("tq%d" % i) for i in range(4)]
            b_ynq = Buf("ynq")
            b_knq = Buf("knq")
            S.alias([b_hbb, b_absb, b_ynq, b_knq] + WINT + KFW + KSB + TQ, ARAW + ZC + RCV + all_ret)
            S.dma("sp", absb, absd_d.partition_broadcast(128), writes=[b_absb])
            fbring = Ring(2)
            mixH = [Buf("mixh%d" % i) for i in range(NT)]

            for od in range(2):
                vt_col = 0
                xg_col = 512 * (od + 1)
                S.dma("sp", hbb, hbias_d[0:1, od * 512:(od + 1) * 512].partition_broadcast(128), writes=[b_hbb])
                def filt_tile(od_, i):
                    wi = i % 2
                    ACT(wint[wi], absb, AF.Exp, [b_absb, b_small["tn"]], [WINT[wi]], scale=tn[:, i:i + 1])
                    pf = pring.nxt()
                    MM(pbank[pf][:], hid2T[:, i * 128:(i + 1) * 128], w3b[:, od_ * 512:(od_ + 1) * 512], True, True,
                       [b_hid2, b_small["w3b"]], [PB[pf]])
                    pb_ = pring.nxt()
                    MM(pbank[pb_][:], hid2T[:, i * 128:(i + 1) * 128], w3b[:, 1024 + od_ * 512:1024 + (od_ + 1) * 512],
                       True, True, [b_hid2, b_small["w3b"]], [PB[pb_]])
                    if i != 0:
                        TT("dve", ksum[:, i, :], pbank[pf][:], wint[wi], ALU.mult, [PB[pf], WINT[wi]], [KS[i]])
                        TT("dve", kdiff[:, i, :], pbank[pb_][:], wint[wi], ALU.mult, [PB[pb_], WINT[wi]], [KS[i]])
                    else:
                        TT("dve", kfw[0], pbank[pf][:], wint[wi], ALU.mult, [PB[pf], WINT[wi]], [KFW[0]])
                        TT("dve", kfw[1], pbank[pb_][:], wint[wi], ALU.mult, [PB[pb_], WINT[wi]], [KFW[1]])
                        TT("dve", kfw[0][0:1, :], kfw[0][0:1, :], kfw[1][0:1, :], ALU.add, [KFW[0], KFW[1]], [KFW[0]])
                        TS("dve", kfw[0][0:1, :], kfw[0][0:1, :], 0.5, None, ALU.mult, None, [KFW[0]], [KFW[0]])
                        CP("dve", kfw[1][0:1, :], kfw[0][0:1, :], [KFW[0]], [KFW[1]])
                        CP("dve", ksum[:, i, :], kfw[0], [KFW[0]], [KS[i]])
                        CP("dve", kdiff[:, i, :], kfw[1], [KFW[1]], [KS[i]])
                if od == 0:
                    for i in range(NT):
                        filt_tile(0, i)
                vbufs = [ZT[0][i] for i in range(NT)]
                for j in range(16):
                    fi = fbring.nxt()
                    S.dma("sp", fblk[fi].rearrange("p i s r -> p (i s r)"), fd_d[j], writes=[FB[fi]])
                    pKr = pring.nxt()
                    for i in range(NT):
                        MM(pbank[pKr][:], fblk[fi][:, i, 0, :], ksum[:, i, :], i == 0, i == NT - 1, [FB[fi], KS[i]], [PB[pKr]])
                    pKi = pring.nxt()
                    for i in range(NT):
                        MM(pbank[pKi][:], fblk[fi][:, i, 1, :], kdiff[:, i, :], i == 0, i == NT - 1, [FB[fi], KS[i]], [PB[pKi]])
                    kb = (j % 2) * 2
                    CP("act", Ksb[kb], pbank[pKr][:], [PB[pKr]], [KSB[kb]])
                    CP("act", Ksb[kb + 1], pbank[pKi][:], [PB[pKi]], [KSB[kb + 1]])
                    TT("dve", Ksb[kb], Ksb[kb], hbb, ALU.add, [KSB[kb], b_hbb], [KSB[kb]])
                    pUr = pring.nxt()
                    for i in range(NT):
                        MM(pbank[pUr][:], fblk[fi][:, i, 0, :], ztok[:, i, vt_col:vt_col + 512], i == 0, i == NT - 1,
                           [FB[fi], vbufs[i]], [PB[pUr]])
                    pUi = pring.nxt()
                    for i in range(NT):
                        MM(pbank[pUi][:], fblk[fi][:, i, 1, :], ztok[:, i, vt_col:vt_col + 512], i == 0, i == NT - 1,
                           [FB[fi], vbufs[i]], [PB[pUi]])
                    TT("dve", tq[0], pbank[pUr][:], Ksb[kb], ALU.mult, [PB[pUr], KSB[kb]], [TQ[0]])
                    TT("dve", tq[1], pbank[pUi][:], Ksb[kb + 1], ALU.mult, [PB[pUi], KSB[kb + 1]], [TQ[1]])
                    TT("dve", Yre[:, j, :], tq[0], tq[1], ALU.subtract, [TQ[0], TQ[1]], [YB[j]])
                    TT("dve", tq[2], pbank[pUr][:], Ksb[kb + 1], ALU.mult, [PB[pUr], KSB[kb + 1]], [TQ[2]])
                    TT("dve", tq[3], pbank[pUi][:], Ksb[kb], ALU.mult, [PB[pUi], KSB[kb]], [TQ[3]])
                    TT("dve", Yim[:, j, :], tq[2], tq[3], ALU.add, [TQ[2], TQ[3]], [YB[j]])
                    if j == 0:
                        TS("dve", Yre[0:1, 0, :], Yre[0:1, 0, :], 0.5, None, ALU.mult, None, [YB[0]], [YB[0]])
                pN = pring.nxt()
                for i in range(NT):
                    MM(pbank[pN][0:1, :], nyqc[:, 0:1], ksum[:, i, :], i == 0, i == NT - 1, [b_small["nyq"], KS[i]], [PB[pN]])
                CP("act", knq, pbank[pN][0:1, :], [PB[pN]], [b_knq])
                TT("dve", knq, knq, hbb[0:1, :], ALU.add, [b_knq, b_hbb], [b_knq])
                pN2 = pring.nxt()
                for i in range(NT):
                    MM(pbank[pN2][0:1, :], nyqc[:, 0:1], ztok[:, i, vt_col:vt_col + 512], i == 0, i == NT - 1,
                       [b_small["nyq"], vbufs[i]], [PB[pN2]])
                STT("dve", ynq, pbank[pN2][0:1, :], 0.5, knq, ALU.mult, ALU.mult, [PB[pN2], b_knq], [b_ynq])
                if od == 1:
                    woutS = ARA[:, 0:8192].rearrange("p (i n) -> p i n", i=8)
                    b_wout = Buf("wout")
                    S.alias([b_wout], KS)
                    S.alias(mixH, KS)
                    S.dma("pool", woutS, wout_d.rearrange("(i p) n -> p i n", p=128), writes=[b_wout])
                pend_T = []
                for t in range(NT):
                    fi = fbring.nxt()
                    S.dma("sp", fblk[fi].rearrange("p i s r -> p (i s r)"), fd_d[t], writes=[FB[fi]])
                    pY = pring.nxt()
                    for i in range(NT):
                        MM(pbank[pY][:], fblk[fi][:, i, 0, :], Yre[:, i, :], i == 0, False, [FB[fi], YB[i]], [PB[pY]])
                    for i in range(NT):
                        MM(pbank[pY][:], fblk[fi][:, i, 1, :], Yim[:, i, :], False, False, [FB[fi], YB[i]], [PB[pY]])
                    MM(pbank[pY][:], nyqr[0:1, :], ynq, False, True, [b_small["nyq"], b_ynq], [PB[pY]])
                    if od == 0:
                        filt_tile(1, t)
                    if od == 1 and t < 8:
                        g1_piece(t)
                    if od == 1 and t >= 1 and pend_T:
                        yhy_T(pend_T.pop(0))
                    if od == 0:
                        STT("dve", ztok[:, t, 0:512], pbank[pY][:], 2.0 / NFFT, ztok[:, t, xg_col:xg_col + 512],
                            ALU.mult, ALU.mult, [PB[pY], ZT[1][t]], [ZT[0][t]])
                    else:
                        yi = t % 2
                        STT("dve", yhy[yi], pbank[pY][:], 2.0 / NFFT, ztok[:, t, xg_col:xg_col + 512],
                            ALU.mult, ALU.mult, [PB[pY], ZT[2][t]], [YH[yi]])
                        def yhy_T(t_):
                            yi_ = t_ % 2
                            ti = tring.nxt()
                            for c in range(4):
                                TR(ptr[ti][:, c * 128:(c + 1) * 128], yhy[yi_][:, c * 128:(c + 1) * 128], ident[:],
                                   [YH[yi_], b_ident], [PT[ti]])
                            CP("act", mixHy[:, :, t_ * 128:(t_ + 1) * 128],
                               ptr[ti][:, 0:512].rearrange("p (c i) -> p c i", c=4), [PT[ti]], [mixH[t_]])
                        pend_T.append(t)
                while pend_T:
                    yhy_T(pend_T.pop(0))
                if od == 0 and "y1" in dbg_d:
                    for i in range(NT):
                        final_ops.append(S.dma("sp", dbg_d["y1"][i * 128:(i + 1) * 128, :], ztok[:, i, 0:512],
                                               reads=[ZT[0][i]]))
            if "yhy" in dbg_d:
                for c in range(4):
                    final_ops.append(S.dma("sp", dbg_d["yhy"][c * 128:(c + 1) * 128, :], mixHy[:, c, :], reads=mixH))

            if "zlate" in dbg_d:
                for i in range(NT):
                    final_ops.append(S.dma("sp", dbg_d["zlate"][i * 128:(i + 1) * 128, :], ztok[:, i, :],
                                           reads=[ZT[0][i], ZT[1][i], ZT[2][i]]))
            CK("hyena")
            X1 = ARX[:].rearrange("p (t n) -> p t n", t=NT)
            X1B = [Buf("x1_%d" % i) for i in range(NT)]
            xt2 = [EF[:, i * 1024:(i + 1) * 1024] for i in range(2)]
            XT2 = [Buf("xt2_%d" % i) for i in range(2)]
            S.alias(XT2, YB)
            all_x_scratch = (WR + WPC + BPC + GPC + [b_rconv] + RCV + FB + GT + YH + [b_hbb, b_absb, b_ynq, b_knq] + WINT + KFW + KSB + TQ
                             + ARAW + ZC + all_ret + XT + XN + TA + HCT + [b_zf, b_hid1])
            S.alias(X1B, all_x_scratch)
            xn2 = [ARB[:, i * 1024:(i + 1) * 1024] for i in range(NT)]
            XN2 = [Buf("xn2_%d" % i) for i in range(NT)]
            S.alias(XN2, [b for l in ZT for b in l])
            xn2ring = Ring(NT)
            xi2 = {}
            for t in range(NT):
                xi = t % 2
                S.dma("sp", xt2[xi], x_d[t * 128:(t + 1) * 128, :], writes=[XT2[xi]])
                for half in range(2):
                    pi = pring.nxt()
                    for k in range(8):
                        MM(pbank[pi][:], mix_chunk(k)[:, t * 128:(t + 1) * 128], woutS[:, k, half * 512:(half + 1) * 512],
                           k == 0, k == 7, [mixH[t], MIXR[t], b_wout], [PB[pi]])
                    TT("dve", X1[:, t, half * 512:(half + 1) * 512], xt2[xi][:, half * 512:(half + 1) * 512], pbank[pi][:],
                       ALU.add, [XT2[xi], PB[pi]], [X1B[t]])
                xi2[t] = norm_s1(X1[:, t, :], [X1B[t]], 18 + t, xn2, XN2, xn2ring)
            if "x1" in dbg_d:
                for t in range(NT):
                    final_ops.append(S.dma("sp", dbg_d["x1"][t * 128:(t + 1) * 128, :], X1[:, t, :], reads=[X1B[t]]))

            CK("wout")
            h2T = ARA[:].rearrange("p (c t) -> p c t", c=8)
            H2 = [Buf("h2T%d" % i) for i in range(2 * NT)]
            S.alias(H2, [b_wout] + mixH)
            for i in range(NT):
                norm_s2(xi2[i], 4, 5, lambda c, i=i: h2T[:, c, i * 128:(i + 1) * 128], H2[2 * i:2 * i + 2], xn2, XN2)

            CK("norm2")
            actT = [ARB[:, g * 8192:(g + 1) * 8192].rearrange("p (f t) -> p f t", f=4) for g in range(2)]
            wdn = [ARB[:, 16384 + g * 4096:16384 + (g + 1) * 4096].rearrange("p (f n) -> p f n", f=4) for g in range(2)]
            ACTB = [[Buf("act%d_%d" % (g, f)) for f in range(4)] for g in range(2)]
            WDN = [[Buf("wdn%d_%d" % (g, f)) for f in range(4)] for g in range(2)]
            zall = [b for l in ZT for b in l]
            S.alias([b for l in ACTB for b in l] + [b for l in WDN for b in l], zall + XN2)
            wup = [ARE[:, s * 2048:(s + 1) * 2048].rearrange("p (i v n) -> p i v n", i=8, v=2) for s in range(3)]
            WUP = [Buf("wup%d" % s) for s in range(3)]
            S.alias(WUP, YB + XT2)
            wupring = Ring(3)
            rbuf = [EF[:, 3072 + i * 2048:3072 + (i + 1) * 2048] for i in range(2)] + [EF[:, 11272:11272 + 2048]]
            RB = [Buf("rb%d" % i) for i in range(3)]
            rbring = Ring(3)
            araw2 = [EF[:, 7168 + i * 2052:7168 + i * 2052 + 2050] for i in range(2)]
            ARAW2 = [Buf("araw2_%d" % i) for i in range(2)]
            b_rc2 = RB[2]
            e_old = YB + XT2 + [b_hid2, b_small["w3b"], b_rc, b_lg, b_DT, b_et, b_rtab] + MIXR
            S.alias(ARAW2 + [b_rc2], e_old)
            g2b = EF[:, 13320:14344]
            bmt2 = EF[:, 14344:14856]
            b_bmt2 = Buf("bmt2")
            S.alias([b_g2b, b_bmt2], e_old)
            lw2 = [ARE[:, 6144 + i * 4096:6144 + (i + 1) * 4096].rearrange("p (i n) -> p i n", i=8) for i in range(2)]
            LW2 = [Buf("lw2_%d" % i) for i in range(2)]
            S.alias(LW2, YB)
            mod_bc_half(10, g2b, b_g2b, bmt2, b_bmt2, lw2[0], LW2[0])
            mod_bc_half(11, g2b, b_g2b, bmt2, b_bmt2, lw2[1], LW2[1])
            S.alias(RB, LW2)
            for i in range(2):
                MEMSET("pool", araw2[i][:, 0:1], 0.0, [ARAW2[i]])
                MEMSET("pool", araw2[i][:, 2049:2050], 0.0, [ARAW2[i]])
            ar2ring = Ring(2)
            wupv = wup_d.rearrange("(i p) n -> p i n", p=128)

            def ffn_job(s, vg, f):
                ai = ar2ring.nxt()
                ri = rbring.nxt()
                out_ap, out_buf = rbuf[ri], RB[ri]
                wq = fcw[:, vg * NFC + f, :]
                for q in range(4):
                    pi = pring.nxt()
                    for k in range(8):
                        MM(pbank[pi][:], wup[s][:, k, vg, :], h2T[:, k, q * 512:(q + 1) * 512], k == 0, k == 7,
                           [WUP[s]] + H2[q * 8:q * 8 + 8], [PB[pi]])
                    CP("act", araw2[ai][:, 1 + q * 512:1 + (q + 1) * 512], pbank[pi][:], [PB[pi]], [ARAW2[ai]])
                    ACT(out_ap[:, q * 512:(q + 1) * 512], pbank[pi][:], AF.Identity, [PB[pi], b_small["fcw"]], [out_buf],
                        scale=wq[:, 1:2], bias=wq[:, 3:4])
                STT("dve", out_ap, araw2[ai][:, 0:2048], wq[:, 0:1], out_ap, ALU.mult, ALU.add,
                    [ARAW2[ai], b_small["fcw"], out_buf], [out_buf])
                STT("dve", out_ap, araw2[ai][:, 2:2050], wq[:, 2:3], out_ap, ALU.mult, ALU.add,
                    [ARAW2[ai], b_small["fcw"], out_buf], [out_buf])
                return ri

            groups = [list(range(g * 4, min(g * 4 + 4, NFC))) for g in range(6)]

            def ffn_up(gi):
                gb = gi % 2
                for fi, f in enumerate(groups[gi]):
                    s = wupring.nxt()
                    S.dma("pool", wup[s][:, :, 0, :], wupv[:, :, f * 128:(f + 1) * 128], writes=[WUP[s]])
                    S.dma("pool", wup[s][:, :, 1, :], wupv[:, :, DFF + f * 128:DFF + (f + 1) * 128], writes=[WUP[s]])
                    S.dma("pool", wdn[gb][:, fi, :], wdn_d[f * 128:(f + 1) * 128, :], writes=[WDN[gb][fi]])
                    TT("dve", wdn[gb][:, fi, :], wdn[gb][:, fi, :], g2b, ALU.mult, [WDN[gb][fi], b_g2b], [WDN[gb][fi]])
                    rv = ffn_job(s, 0, f)
                    rg = ffn_job(s, 1, f)
                    ACT(rbuf[rg], rbuf[rg], AF.Silu, [RB[rg]], [RB[rg]])
                    TT("dve", actT[gb][:, fi, :], rbuf[rv], rbuf[rg], ALU.mult, [RB[rv], RB[rg]], [ACTB[gb][fi]])

            def ffn_down(gi, after_tile=None):
                gb = gi % 2
                nf = len(groups[gi])
                for t in range(NT):
                    if after_tile is not None and t >= 1:
                        after_tile(t - 1)
                    for half in range(2):
                        pi = pring.nxt()
                        for fi in range(nf):
                            MM(pbank[pi][:], actT[gb][:, fi, t * 128:(t + 1) * 128], wdn[gb][:, fi, half * 512:(half + 1) * 512],
                               fi == 0, fi == nf - 1, [ACTB[gb][fi], WDN[gb][fi]], [PB[pi]])
                        TT("dve", X1[:, t, half * 512:(half + 1) * 512], X1[:, t, half * 512:(half + 1) * 512], pbank[pi][:],
                           ALU.add, [X1B[t], PB[pi]], [X1B[t]])

            for gi in range(6):
                ffn_up(gi)
                if gi >= 1:
                    ffn_down(gi - 1)
            nfb = ARAF[:, 0:1024]
            b_nfb = Buf("nfb")
            S.alias([b_nfb], H2)
            S.dma("sp", nfb, nfr_d.partition_broadcast(128), writes=[b_nfb])
            outt = [ARAF[:, 1024 + i * 1024:1024 + (i + 1) * 1024] for i in range(2)]
            OT = [Buf("ot%d" % i) for i in range(2)]
            S.alias(OT, H2)

            def final_tile(t):
                sc = 36 + t
                oi = t % 2
                ACT(outt[oi], X1[:, t, :], AF.Square, [X1B[t], SSQ[sc]], [SSQ[sc], OT[oi]], accum_out=ssq[:, sc:sc + 1])
                ACT(rstd[:, sc:sc + 1], ssq[:, sc:sc + 1], AF.Sqrt, [SSQ[sc], b_negpi], [RSTD[sc]], scale=1.0 / D,
                    bias=epsc[:, 0:1])
                RCP(rstd[:, sc:sc + 1], rstd[:, sc:sc + 1], [RSTD[sc]], [RSTD[sc]])
                STT("dve", outt[oi], X1[:, t, :], rstd[:, sc:sc + 1], nfb, ALU.mult, ALU.mult, [X1B[t], RSTD[sc], b_nfb], [OT[oi]])
                final_ops.append(S.dma("sp", y_d[t * 128:(t + 1) * 128, :], outt[oi], reads=[OT[oi]]))
            ffn_down(5, after_tile=final_tile)
            final_tile(NT - 1)

            CK("ffn")
        except _Stop:
            pass
        S.emit(final_ops=final_ops)
    return nc


def _prep_shared(inp):
    f = np.float32
    sh = {}
    sh["w_mod"] = np.ascontiguousarray(inp["w_mod"][0], f)
    b_mod = np.asarray(inp["b_mod"][0], f)
    sh["bmod_col"] = np.ascontiguousarray(b_mod.reshape(48, 128).T)
    sh["bmod_row"] = np.ascontiguousarray(b_mod.reshape(1, 6144))
    sh["n1c"] = np.ascontiguousarray(np.asarray(inp["norm1"][0], f).reshape(8, 128).T)
    sh["n2c"] = np.ascontiguousarray(np.asarray(inp["norm2"][0], f).reshape(8, 128).T)
    sh["nf_row"] = np.ascontiguousarray(np.asarray(inp["norm_f"], f).reshape(1, D))
    w_in = np.asarray(inp["w_in"][0], f)
    def swapped(c0):
        blk = w_in[:, c0:c0 + 256].reshape(D, 4, 2, 32)
        return blk[:, :, ::-1, :].reshape(D, 256)
    sh["w_in"] = np.ascontiguousarray(np.concatenate([w_in, swapped(1536), swapped(1792)], axis=1))
    hw = np.asarray(inp["hy_conv_w"][0], f)
    hb = np.asarray(inp["hy_conv_b"][0], f)
    hc = np.concatenate([hw, hb[None]], axis=0)
    sh["hcw"] = np.ascontiguousarray(hc.reshape(4, 12, 128).transpose(2, 1, 0).reshape(128, 48))
    fw = np.asarray(inp["ffn_conv_w"][0], f)
    fb = np.asarray(inp["ffn_conv_b"][0], f)
    fc = np.concatenate([fw, fb[None]], axis=0)
    sh["fcw"] = np.ascontiguousarray(fc.reshape(4, 44, 128).transpose(2, 1, 0).reshape(128, 176))
    sh["hy_w1"] = np.ascontiguousarray(inp["hy_w1"][0], f)
    sh["hyp"] = np.ascontiguousarray(np.stack([inp["hy_b1"][0], inp["hy_f1"][0], inp["hy_b2"][0], inp["hy_f2"][0]],
                                              axis=1).astype(f))
    sh["hy_w2"] = np.ascontiguousarray(inp["hy_w2"][0], f)
    sh["hy_w3"] = np.ascontiguousarray(inp["hy_w3"][0], f)
    sh["hy_bias"] = np.ascontiguousarray(np.asarray(inp["hy_bias"][0], f).reshape(1, 1024))
    sh["rlog"] = np.ascontiguousarray(np.concatenate([inp["ret_logit_f"][0], inp["ret_logit_b"][0]]).astype(f).reshape(1, 8))
    sh["w_out"] = np.ascontiguousarray(inp["w_out"][0], f)
    sh["w_up"] = np.ascontiguousarray(inp["ffn_w_up"][0], f)
    sh["w_down"] = np.ascontiguousarray(inp["ffn_w_down"][0], f)
    hc_ = host_consts()
    for k in ("fd", "nyqc", "nyqr", "rope", "rc", "zfT", "absd", "tn"):
        sh[k] = hc_[k]
    return sh


_NC_CACHE = {}


def kernel(_dbg=(), _stop=None, _cores=None, **inputs):
    inp = {k: np.asarray(v) for k, v in inputs.items()}
    key = repr((_dbg, _stop))
    if key not in _NC_CACHE:
        _NC_CACHE[key] = build(_dbg, _stop)
    nc = _NC_CACHE[key]
    sh = _prep_shared(inp)
    x = np.asarray(inp["x"], np.float32)
    ctx = np.asarray(inp["ctx"], np.float32)
    c = np.asarray(inp["c"], np.float32)
    c_ctx = np.asarray(inp["c_ctx"], np.float32)
    n = x.shape[0] if _cores is None else _cores
    in_maps = []
    for b in range(n):
        m = dict(sh)
        m["x"] = np.ascontiguousarray(x[b])
        m["ctx"] = np.ascontiguousarray(ctx[b])
        cv = np.stack([c[b].reshape(8, 128).T, c_ctx.reshape(8, 128).T], axis=2)
        m["cvec"] = np.ascontiguousarray(cv.reshape(128, 16))
        in_maps.append(m)
    res = run_bass_kernel_spmd(nc, in_maps, core_ids=list(range(n)))
    out = np.stack([np.asarray(r["y"], np.float32) for r in res.results], axis=0)
    if _dbg:
        return out, res.results
    return out
```

```python
import math
from contextlib import ExitStack

import numpy as np
import ml_dtypes

import concourse.bass as bass
import concourse.mybir as mybir
from concourse.bass_utils import run_bass_kernel_spmd

F32 = mybir.dt.float32
BF16 = mybir.dt.bfloat16
I32 = mybir.dt.int32
AF = mybir.ActivationFunctionType
ALU = mybir.AluOpType

L = 2048
D = 1024
NT = 16
NFFT = 4096
EPS = 1e-6
DFF = 2816
NFC = 22
ENGS = ("pe", "act", "dve", "pool", "sp")
PI = math.pi


class Buf:
    __slots__ = ("name", "w", "r", "rd")

    def __init__(self, name=""):
        self.name = name
        self.w = None
        self.r = {}
        self.rd = []


class Op:
    __slots__ = ("eng", "fn", "deps", "signal", "count", "dma", "chan", "cval", "cprev")

    def __init__(self, eng, fn, dma):
        self.eng = eng
        self.fn = fn
        self.dma = dma
        self.deps = []
        self.signal = False
        self.count = 0
        self.chan = None
        self.cval = 0
        self.cprev = 0


class Sched:
    def __init__(self, nc, nchan=20, self_wait=True):
        self.nc = nc
        self.ops = {e: [] for e in ENGS}
        self.nchan = nchan
        self.self_wait = self_wait

    def add(self, eng, fn, reads=(), writes=(), dma=False):
        op = Op(eng, fn, dma)
        deps = {}
        for b in reads:
            if b.w is not None:
                deps[id(b.w)] = b.w
        for b in writes:
            if b.w is not None:
                deps[id(b.w)] = b.w
            for o in b.r.values():
                deps[id(o)] = o
            for o in b.rd:
                deps[id(o)] = o
        op.deps = list(deps.values())
        for d in op.deps:
            d.signal = True
        for b in reads:
            if dma:
                b.rd.append(op)
            else:
                b.r[eng] = op
        for b in writes:
            b.w = op
            b.r = {}
            b.rd = []
        if dma:
            op.signal = True
        self.ops[eng].append(op)
        return op

    def dma(self, eng, out, in_, reads=(), writes=()):
        return self.add(eng, lambda e: e.dma_start(out=out, in_=in_), reads, writes, dma=True)

    def alias(self, new_bufs, old_bufs):
        ops = {}
        for b in old_bufs:
            if b.w is not None:
                ops[id(b.w)] = b.w
            for o in b.r.values():
                ops[id(o)] = o
            for o in b.rd:
                ops[id(o)] = o
        lst = list(ops.values())
        for b in new_bufs:
            b.rd.extend(lst)

    def emit(self, final_ops=()):
        nc = self.nc
        with ExitStack() as st:
            esem = {e: st.enter_context(nc.semaphore("s_" + e)) for e in ENGS}
            csem = {}
            for e in ENGS:
                if any(o.dma for o in self.ops[e]):
                    csem[e] = [st.enter_context(nc.semaphore("c_%s_%d" % (e, i))) for i in range(self.nchan)]
            for e in ENGS:
                n = 0
                uses = [0] * self.nchan
                k = 0
                for o in self.ops[e]:
                    if o.dma:
                        c = k % self.nchan
                        k += 1
                        o.chan = csem[e][c]
                        o.cprev = 16 * uses[c]
                        uses[c] += 1
                        o.cval = 16 * uses[c]
                    elif o.signal:
                        n += 1
                        o.count = n
            self_wait = self.self_wait

            def run(e, eng):
                waited = {}
                for o in self.ops[e]:
                    need = {}
                    for d in o.deps:
                        if d.dma:
                            key, val = d.chan, d.cval
                        else:
                            if d.eng == e and (e == "pe" or not self_wait):
                                continue
                            key, val = esem[d.eng], d.count
                        kk = id(key)
                        if kk not in need or need[kk][1] < val:
                            need[kk] = (key, val)
                    if o.dma and o.cprev > 0:
                        kk = id(o.chan)
                        if kk not in need or need[kk][1] < o.cprev:
                            need[kk] = (o.chan, o.cprev)
                    for kk, (key, val) in need.items():
                        if waited.get(kk, 0) >= val:
                            continue
                        eng.wait_ge(key, val)
                        waited[kk] = val
                    ins = o.fn(eng)
                    if o.dma:
                        ins.then_inc(o.chan, 16)
                    elif o.signal:
                        ins.then_inc(esem[e], 1)
                if e == "sp":
                    for d in final_ops:
                        if d.dma:
                            eng.wait_ge(d.chan, d.cval)
                        else:
                            eng.wait_ge(esem[d.eng], d.count)

            with nc.Block() as block:
                @block.tensor
                def _(eng):
                    run("pe", eng)

                @block.scalar
                def _(eng):
                    run("act", eng)

                @block.vector
                def _(eng):
                    run("dve", eng)

                @block.gpsimd
                def _(eng):
                    run("pool", eng)

                @block.sync
                def _(eng):
                    run("sp", eng)


class Ring:
    def __init__(self, n):
        self.n = n
        self.i = -1

    def nxt(self):
        self.i = (self.i + 1) % self.n
        return self.i


_CONSTS = None


def host_consts():
    global _CONSTS
    if _CONSTS is not None:
        return _CONSTS
    c = {}
    j = np.arange(16)[:, None, None, None]
    p = np.arange(128)[None, :, None, None]
    i = np.arange(16)[None, None, :, None]
    r = np.arange(128)[None, None, None, :]
    prod = ((128 * i + p) * (128 * j + r)) % NFFT
    ang = 2.0 * np.pi * prod.astype(np.float64) / NFFT
    fd = np.stack([np.cos(ang), -np.sin(ang)], axis=3)
    c["fd"] = fd.reshape(16, 128, 4096).astype(ml_dtypes.bfloat16)
    sgn = (1.0 - 2.0 * (np.arange(128) % 2)).astype(np.float32)
    c["nyqc"] = sgn.reshape(128, 1).astype(ml_dtypes.bfloat16)
    c["nyqr"] = sgn.reshape(1, 128).astype(ml_dtypes.bfloat16)
    t = np.arange(L)
    row = (t // 64).astype(np.float32)
    col = (t % 64).astype(np.float32)
    inv_freq = (10000.0 ** (-np.arange(16, dtype=np.float32) / 16)).astype(np.float32)
    ang = np.concatenate([row[:, None] * inv_freq, col[:, None] * inv_freq], axis=-1).astype(np.float32)
    cs, sn = np.cos(ang), np.sin(ang)
    cos64 = np.concatenate([cs, cs], axis=1).T
    sin64 = np.concatenate([-sn, sn], axis=1).T
    c["rope"] = np.concatenate([np.tile(cos64, (2, 1)), np.tile(sin64, (2, 1))], axis=1).astype(np.float32)
    rc = np.zeros((128, 776), np.float32)
    jj = np.arange(128)[:, None].astype(np.float32)
    ii = np.arange(128)[None, :].astype(np.float32)
    rc[:, 0:128] = np.maximum(ii - jj, 0)
    rc[:, 128:256] = np.maximum(jj - ii, 0)
    rc[:, 256:384] = (ii >= jj)
    rc[:, 384:512] = (jj > ii)
    rc[:, 512:640] = ii + 1.0
    rc[:, 640:768] = 128.0 - ii
    pp = np.arange(128, dtype=np.float32)
    rc[:, 768] = 127.0 - pp
    rc[:, 769] = pp
    rc[:, 770] = 255.0 - pp
    rc[:, 771] = 127.0 - pp
    rc[:, 772] = pp
    rc[:, 773] = 128.0 + pp
    rc[:, 774] = 1.0
    rc[:, 775] = -PI
    c["rc"] = rc
    pos = np.arange(L, dtype=np.float32)[:, None]
    tt = np.linspace(0.0, 1.0, L, dtype=np.float32)[:, None]
    bands = np.linspace(1e-4, 15, 16, dtype=np.float32)[None, :]
    a2 = (2.0 * np.float32(math.pi) * pos * bands / L).astype(np.float32)
    z = np.concatenate([tt, np.cos(a2), -np.sin(a2)], axis=-1).astype(np.float32)
    c["zfT"] = np.ascontiguousarray(z.T)
    max_decay = math.log(1e-2) / 0.3
    min_decay = math.log(1e-2) / 1.5
    deltas = np.linspace(min_decay, max_decay, 512, dtype=np.float32)
    c["absd"] = np.abs(deltas).reshape(1, 512).astype(np.float32)
    tn = -tt[:, 0]
    c["tn"] = np.ascontiguousarray(tn.reshape(16, 128).T).astype(np.float32)
    _CONSTS = c
    return c


class _Stop(Exception):
    pass


def build(dbg=(), stop_after=None):
    nc = bass.Bass("TRN2", target_bir_lowering=False)

    def CK(name):
        if stop_after == name:
            raise _Stop()

    def din(name, shape, dt=F32):
        return nc.dram_tensor(name, list(shape), dt, kind="ExternalInput").ap()

    x_d = din("x", [L, D])
    ctx_d = din("ctx", [256, D])
    cvec_d = din("cvec", [128, 16])
    wmod_d = din("w_mod", [D, 6144])
    bmodc_d = din("bmod_col", [128, 48])
    bmodr_d = din("bmod_row", [1, 6144])
    n1c_d = din("n1c", [128, 8])
    n2c_d = din("n2c", [128, 8])
    nfr_d = din("nf_row", [1, D])
    win_d = din("w_in", [D, 3584])
    hcw_d = din("hcw", [128, 48])
    fcw_d = din("fcw", [128, 176])
    hw1_d = din("hy_w1", [33, 64])
    hyp_d = din("hyp", [64, 4])
    hw2_d = din("hy_w2", [64, 64])
    hw3_d = din("hy_w3", [64, 2048])
    hbias_d = din("hy_bias", [1, 1024])
    rlog_d = din("rlog", [1, 8])
    wout_d = din("w_out", [D, D])
    wup_d = din("w_up", [D, 2 * DFF])
    wdn_d = din("w_down", [DFF, D])
    fd_d = din("fd", [16, 128, 4096], BF16)
    nyqc_d = din("nyqc", [128, 1], BF16)
    nyqr_d = din("nyqr", [1, 128], BF16)
    rope_d = din("rope", [128, 2 * L])
    rc_d = din("rc", [128, 776])
    zfT_d = din("zfT", [33, L])
    absd_d = din("absd", [1, 512])
    tn_d = din("tn", [128, 16])
    y_d = nc.dram_tensor("y", [L, D], F32, kind="ExternalOutput").ap()
    dbg_d = {}
    for name, shape, dts in dbg:
        dbg_d[name] = nc.dram_tensor("dbg_" + name, list(shape), BF16 if dts == "bf16" else F32,
                                     kind="ExternalOutput").ap()

    st = ExitStack()
    with st:
        def sb(name, shape, dt):
            return st.enter_context(nc.sbuf_tensor("s_" + name, list(shape), dt))

        def ps(name, shape, dt):
            return st.enter_context(nc.psum_tensor(name, list(shape), dt))

        S = Sched(nc)
        final_ops = []

        BIG = sb("big", [128, 103680], BF16)
        BIGF = BIG[:].bitcast(F32)
        BIGI = BIG[:].bitcast(I32)
        XB = BIG[:, 0:32768]
        ARX = BIGF[:, 0:16384]
        ARA = BIG[:, 32768:49152]
        ARAF = BIGF[:, 16384:24576]
        ARB = BIG[:, 49152:73728]
        ARBF = BIGF[:, 24576:36864]
        ARE = BIG[:, 73728:103680]
        EF = BIGF[:, 36864:51840]

        ident = sb("ident", [128, 128], BF16)
        ones_bf = sb("ones_bf", [128, 128], BF16)
        cvf = sb("cvf", [128, 16], F32)
        sT = sb("sT", [128, 16], BF16)
        srep = sb("srep", [128, 8, 128], BF16)
        modcol = sb("modcol", [128, 32, 2], F32)
        bmodc = sb("bmodc", [128, 48], F32)
        n1c = sb("n1c", [128, 8], F32)
        n2c = sb("n2c", [128, 8], F32)
        AB = sb("AB", [128, 6, 8], F32)
        hcw = sb("hcw", [128, 12, 4], F32)
        fcw = sb("fcw", [128, 44, 4], F32)
        ssq = sb("ssq", [128, 64], F32)
        rstd = sb("rstd", [128, 64], F32)
        hid2T = ARE[0:64, 16384:18432]
        w3b = ARE[0:64, 18432:20480]
        nyqc = sb("nyqc", [128, 1], BF16)
        nyqr = sb("nyqr", [1, 128], BF16)
        tn = sb("tn", [128, 16], F32)
        negpi = sb("negpi", [128, 1], F32)
        epsc = sb("epsc", [128, 1], F32)

        pbank = [ps("pb%d" % i, [128, 512], F32) for i in range(6)]
        PB = [Buf("pb%d" % i) for i in range(6)]
        ptr = [ps("pt%d" % i, [128, 1024], BF16) for i in range(2)]
        PT = [Buf("pt%d" % i) for i in range(2)]
        pring = Ring(6)
        tring = Ring(2)

        def MM(out, lhsT, rhs, start, stop, reads, writes):
            S.add("pe", lambda e: e.matmul(out, lhsT=lhsT, rhs=rhs, start=start, stop=stop), reads, writes)

        def TR(out, in_, idn, reads, writes):
            S.add("pe", lambda e: e.transpose(out, in_, idn), reads, writes)

        def ACT(out, in_, func, reads, writes, **kw):
            S.add("act", lambda e: e.activation(out=out, in_=in_, func=func, **kw), reads, writes)

        def TT(eng, out, in0, in1, op, reads, writes):
            S.add(eng, lambda e: e.tensor_tensor(out=out, in0=in0, in1=in1, op=op), reads, writes)

        def TS(eng, out, in0, s1, s2, op0, op1, reads, writes):
            if s2 is None:
                S.add(eng, lambda e: e.tensor_scalar(out=out, in0=in0, scalar1=s1, scalar2=None, op0=op0), reads, writes)
            else:
                S.add(eng, lambda e: e.tensor_scalar(out=out, in0=in0, scalar1=s1, scalar2=s2, op0=op0, op1=op1),
                      reads, writes)

        def STT(eng, out, in0, scalar, in1, op0, op1, reads, writes):
            S.add(eng, lambda e: e.scalar_tensor_tensor(out=out, in0=in0, scalar=scalar, in1=in1, op0=op0, op1=op1),
                  reads, writes)

        def CP(eng, out, in_, reads, writes):
            if eng == "act":
                S.add("act", lambda e: e.activation(out=out, in_=in_, func=AF.Copy), reads, writes)
            else:
                S.add(eng, lambda e: e.tensor_copy(out=out, in_=in_), reads, writes)

        def RCP(out, in_, reads, writes):
            S.add("dve", lambda e: e.reciprocal(out=out, in_=in_), reads, writes)

        def MEMSET(eng, out, val, writes):
            S.add(eng, lambda e: e.memset(out, val), (), writes)

        def DBG(name, src_ap, reads, rows=None):
            if name in dbg_d:
                final_ops.append(S.dma("sp", dbg_d[name], src_ap, reads=reads))

        def bc(ap, shape):
            return ap.to_broadcast(list(shape))

        try:
            b_ident = Buf("ident")
            MEMSET("pool", ident[:], 0.0, [b_ident])
            S.add("pool", lambda e: e.affine_select(out=ident[:], in_=ident[:], pattern=[[-1, 128]],
                                                    compare_op=ALU.not_equal, fill=1.0, base=0, channel_multiplier=1),
                  [b_ident], [b_ident])
            b_ones = Buf("ones")
            MEMSET("dve", ones_bf[:], 1.0, [b_ones])
            b_negpi = Buf("negpi")
            MEMSET("dve", negpi[:], -PI, [b_negpi])
            MEMSET("dve", epsc[:], EPS, [b_negpi])
            SSQ = [Buf("ssq%d" % i) for i in range(64)]
            RSTD = [Buf("rstd%d" % i) for i in range(64)]
            MEMSET("dve", ssq[:], 0.0, SSQ)

            b_zf = Buf("zfT")
            b_hid1 = Buf("hid1")
            b_hid2 = Buf("hid2")
            b_ft = Buf("ftmp0")
            CK("c0")
            b_small = {k: Buf(k) for k in ["cvf", "sT", "srep", "bmodc", "n1c", "n2c", "hcw", "fcw", "nyq", "tn", "w3b"]}
            S.dma("sp", cvf[:], cvec_d, writes=[b_small["cvf"]])
            S.dma("sp", bmodc[:], bmodc_d, writes=[b_small["bmodc"]])
            S.dma("sp", n1c[:], n1c_d, writes=[b_small["n1c"]])
            S.dma("sp", n2c[:], n2c_d, writes=[b_small["n2c"]])
            S.dma("sp", hcw[:].rearrange("p a b -> p (a b)"), hcw_d, writes=[b_small["hcw"]])
            S.dma("sp", fcw[:].rearrange("p a b -> p (a b)"), fcw_d, writes=[b_small["fcw"]])
            S.dma("sp", nyqc[:], nyqc_d, writes=[b_small["nyq"]])
            S.dma("sp", nyqr[:], nyqr_d, writes=[b_small["nyq"]])
            S.dma("sp", tn[:], tn_d, writes=[b_small["tn"]])
            ACT(sT[:], cvf[:], AF.Silu, [b_small["cvf"]], [b_small["sT"]])
            sT3 = sT[:].rearrange("p (i t) -> p i t", t=2)
            CP("dve", srep[:], bc(sT3[:, :, 0:1], [128, 8, 128]), [b_small["sT"]], [b_small["srep"]])

            R0 = 8192
            rc = EF[:, R0 + 0:R0 + 776]
            DTm = EF[:, R0 + 776:R0 + 1288].rearrange("p (h i) -> p h i", h=4)
            rowqm = ARE[:, 2 * (R0 + 1288):2 * (R0 + 2056)].rearrange("p (h d i) -> p h d i", h=4, d=3)
            R1 = R0 + 256
            rl = EF[:, R1 + 1800:R1 + 1808]
            lg = EF[:, R1 + 1808:R1 + 1816]
            lgcol = EF[:, R1 + 1816:R1 + 1820].rearrange("p (c d) -> p c d", c=2)
            dec = EF[:, R1 + 1820:R1 + 1824].rearrange("p (c d) -> p c d", c=2)
            colfb = EF[:, R1 + 1824:R1 + 1832].rearrange("p (d h) -> p d h", d=2)
            wcx = EF[:, R1 + 1832:R1 + 1848].rearrange("p (t d h) -> p t d h", t=2, d=2)
            etmp = EF[:, R1 + 1848:R1 + 1976]
            etmp2 = EF[:, R1 + 1976:R1 + 2104]
            b_rc = Buf("rc")
            b_lg = Buf("lg")
            b_DT = Buf("DT")
            b_et = Buf("etmp")
            b_rtab = Buf("rtab")
            S.dma("sp", rc, rc_d, writes=[b_rc])
            S.dma("sp", rl, rlog_d.partition_broadcast(128), writes=[b_lg])
            ACT(lg, rl, AF.Exp, [b_lg], [b_lg], scale=-1.0)
            ACT(lg, lg, AF.Ln, [b_lg, b_rc], [b_lg], bias=rc[:, 774:775])
            TS("dve", lg, lg, -1.0, None, ALU.mult, None, [b_lg], [b_lg])
            for d in range(2):
                for half in range(2):
                    P = slice(half * 64, half * 64 + 64)
                    CP("dve", lgcol[P, :, d], lg[P, d * 4 + half:d * 4 + half + 3:2], [b_lg], [b_rtab])
            for h in range(4):
                ACT(etmp, rc[:, 0:128], AF.Exp, [b_rc, b_lg], [b_et], scale=lg[:, h:h + 1])
                TT("dve", DTm[:, h, :], etmp, rc[:, 256:384], ALU.mult, [b_et, b_rc], [b_DT])
                ACT(etmp2, rc[:, 128:256], AF.Exp, [b_rc, b_lg], [b_et], scale=lg[:, 4 + h:5 + h])
                TT("dve", etmp2, etmp2, rc[:, 384:512], ALU.mult, [b_et, b_rc], [b_et])
                TT("dve", DTm[:, h, :], DTm[:, h, :], etmp2, ALU.add, [b_et, b_DT], [b_DT])
                ACT(colfb[:, 0, h:h + 1], rc[:, 768:769], AF.Exp, [b_rc, b_lg], [b_rtab], scale=lg[:, h:h + 1])
                ACT(colfb[:, 1, h:h + 1], rc[:, 769:770], AF.Exp, [b_rc, b_lg], [b_rtab], scale=lg[:, 4 + h:5 + h])
                ACT(wcx[:, :, 0, h], rc[:, 770:772], AF.Exp, [b_rc, b_lg], [b_rtab], scale=lg[:, h:h + 1])
                ACT(wcx[:, :, 1, h], rc[:, 772:774], AF.Exp, [b_rc, b_lg], [b_rtab], scale=lg[:, 4 + h:5 + h])
            ACT(dec, lgcol, AF.Exp, [b_rtab], [b_rtab], scale=128.0)
            for c in range(2):
                if c == 0:
                    MEMSET("dve", rowqm.rearrange("p h d i -> p (h d i)"), 0.0, [b_rtab])
                for half in range(2):
                    hs = slice(half * 64, half * 64 + 64)
                    h_ = 2 * c + half
                    ACT(rowqm[hs, h_, 0, :], rc[hs, 512:640], AF.Exp, [b_rc, b_rtab], [b_rtab], scale=lgcol[hs, c, 0:1])
                    ACT(rowqm[hs, h_, 1, :], rc[hs, 640:768], AF.Exp, [b_rc, b_rtab], [b_rtab], scale=lgcol[hs, c, 1:2])
                    MEMSET("dve", rowqm[hs, h_, 2, :], 1.0, [b_rtab])

            CK("c1")
            wr = [XB[:, s * 4096:(s + 1) * 4096].rearrange("p (i n) -> p i n", i=8) for s in range(4)]
            WR = [Buf("wr%d" % s) for s in range(4)]
            wring = Ring(4)
            wmv = wmod_d.rearrange("(i p) n -> p i n", p=128)
            winv = win_d.rearrange("(i p) n -> p i n", p=128)

            def load_wblock(src):
                s = wring.nxt()
                S.dma("pool", wr[s], src, writes=[WR[s]])
                return s

            b_modcol = Buf("modcol")
            b_AB = Buf("AB")

            def mod_col_half(j, which, pi=None):
                s = load_wblock(wmv[:, :, j * 512:(j + 1) * 512])
                if pi is None:
                    pi = pring.nxt()
                for cc in range(4):
                    for k in range(8):
                        MM(pbank[pi][:, cc * 2:cc * 2 + 2], wr[s][:, k, cc * 128:(cc + 1) * 128],
                           sT3[:, k, :], k == 0, k == 7, [WR[s], b_small["sT"]], [PB[pi]])
                m0 = which * 8 + (j % 2) * 4
                TT("dve", modcol[:, m0:m0 + 4, :], pbank[pi][:, 0:8].rearrange("p (a b) -> p a b", b=2),
                   bc(bmodc[:, j * 4:j * 4 + 4].unsqueeze(2), [128, 4, 2]), ALU.add,
                   [PB[pi], b_small["bmodc"]], [b_modcol])

            def mod_bc_half(j, dst, dst_buf, tmp, tmp_buf, wbuf, wbuf_b):
                S.dma("pool", wbuf, wmv[:, :, j * 512:(j + 1) * 512], writes=[wbuf_b])
                S.dma("sp", tmp, bmodr_d[0:1, j * 512:(j + 1) * 512].partition_broadcast(128), writes=[tmp_buf])
                pi = pring.nxt()
                for k in range(8):
                    MM(pbank[pi][:], srep[:, k, :], wbuf[:, k, :], k == 0, k == 7, [wbuf_b, b_small["srep"]], [PB[pi]])
                h0 = (j % 2) * 512
                TT("dve", dst[:, h0:h0 + 512], pbank[pi][:], tmp, ALU.add, [PB[pi], tmp_buf], [dst_buf])

            mod_col_half(0, 0)
            mod_col_half(1, 0)
            mod_col_half(2, 1)
            mod_col_half(3, 1)
            STT("dve", AB[:, 0, :], modcol[:, 8:16, 0], 1.0, n1c[:], ALU.add, ALU.mult, [b_modcol, b_small["n1c"]], [b_AB])
            CP("dve", AB[:, 1, :], modcol[:, 0:8, 0], [b_modcol], [b_AB])
            STT("dve", AB[:, 2, :], modcol[:, 8:16, 1], 1.0, n1c[:], ALU.add, ALU.mult, [b_modcol, b_small["n1c"]], [b_AB])
            CP("dve", AB[:, 3, :], modcol[:, 0:8, 1], [b_modcol], [b_AB])

            CK("c2")
            xt = [ARX[:, 8192 + i * 1024:8192 + (i + 1) * 1024] for i in range(3)]
            XT = [Buf("xt%d" % i) for i in range(3)]
            xtring = Ring(3)
            xn = [XB[:, 22528 + i * 1024:22528 + (i + 1) * 1024] for i in range(2)]
            XN = [Buf("xn%d" % i) for i in range(2)]
            xnring = Ring(2)
            b_rstd = Buf("rstd")
            evac_flip = [0]

            def norm_s1(src, src_bufs, scol, xn_list, xn_bufs, xn_ring):
                xi = xn_ring.nxt()
                ACT(xn_list[xi], src, AF.Square, src_bufs + [SSQ[scol]], [SSQ[scol], xn_bufs[xi]],
                    accum_out=ssq[:, scol:scol + 1])
                ACT(rstd[:, scol:scol + 1], ssq[:, scol:scol + 1], AF.Sqrt, [SSQ[scol], b_negpi], [RSTD[scol]], scale=1.0 / D,
                    bias=epsc[:, 0:1])
                RCP(rstd[:, scol:scol + 1], rstd[:, scol:scol + 1], [RSTD[scol]], [RSTD[scol]])
                TS("dve", xn_list[xi], src, rstd[:, scol:scol + 1], None, ALU.mult, None, src_bufs + [RSTD[scol]],
                   [xn_bufs[xi]])
                return xi

            def norm_s2(xi, acol, bcol, dst_fn, dst_bufs, xn_list, xn_bufs):
                ta = tring.nxt()
                tb = tring.nxt()
                NA = 3
                for c in range(NA):
                    TR(ptr[ta][:, c * 128:(c + 1) * 128], xn_list[xi][:, c * 128:(c + 1) * 128], ident[:],
                       [xn_bufs[xi], b_ident], [PT[ta]])
                for c in range(NA, 8):
                    TR(ptr[tb][:, (c - NA) * 128:(c - NA + 1) * 128], xn_list[xi][:, c * 128:(c + 1) * 128], ident[:],
                       [xn_bufs[xi], b_ident], [PT[tb]])
                for c in range(8 - NA):
                    if c < NA:
                        ACT(dst_fn(c), ptr[ta][:, c * 128:(c + 1) * 128], AF.Identity, [PT[ta], b_AB], [dst_bufs[0]],
                            scale=AB[:, acol, c:c + 1], bias=AB[:, bcol, c:c + 1])
                    cc_ = c + NA
                    TS("dve", dst_fn(cc_), ptr[tb][:, c * 128:(c + 1) * 128], AB[:, acol, cc_:cc_ + 1],
                       AB[:, bcol, cc_:cc_ + 1], ALU.mult, ALU.add, [PT[tb], b_AB], [dst_bufs[1]])

            hT = ARA[:].rearrange("p (c t) -> p c t", c=8)
            HT = [Buf("hT%d" % i) for i in range(2 * NT)]
            hcT = ARB[:, 8192:10240].rearrange("p (c t) -> p c t", c=8)
            HCT = [Buf("hcT%d" % i) for i in range(4)]
            def p1_s1(i):
                xi = xtring.nxt()
                src = x_d[i * 128:(i + 1) * 128, :] if i < NT else ctx_d[(i - NT) * 128:(i - NT + 1) * 128, :]
                S.dma("sp", xt[xi], src, writes=[XT[xi]])
                return norm_s1(xt[xi], [XT[xi]], i, xn, XN, xnring)

            def p1_s2(i, xi):
                if i < NT:
                    norm_s2(xi, 0, 1, lambda c: hT[:, c, i * 128:(i + 1) * 128], HT[2 * i:2 * i + 2], xn, XN)
                else:
                    j = i - NT
                    norm_s2(xi, 2, 3, lambda c: hcT[:, c, j * 128:(j + 1) * 128], HCT[2 * j:2 * j + 2], xn, XN)
            xi_cur = p1_s1(0)
            for i in range(NT + 2):
                xi_nxt = p1_s1(i + 1) if i + 1 < NT + 2 else None
                p1_s2(i, xi_cur)
                xi_cur = xi_nxt
            if "hT" in dbg_d:
                for c in range(8):
                    final_ops.append(S.dma("sp", dbg_d["hT"][c * 128:(c + 1) * 128, :], hT[:, c, :], reads=HT))

            CK("p1")
            rope = ARBF[:, 0:4096]
            b_rope = Buf("rope")
            S.dma("sp", rope, rope_d, writes=[b_rope])
            qrot = ARE[:, 0:4096].rearrange("p (c t) -> p c t", c=2)
            krot = ARE[:, 4096:8192].rearrange("p (c t) -> p c t", c=2)
            vtok = ARE[:, 8192:16384].rearrange("p (i n) -> p i n", i=NT)
            QR = [[Buf("qr%d_%d" % (c, q)) for q in range(4)] for c in range(2)]
            KR = [[Buf("kr%d_%d" % (c, q)) for q in range(4)] for c in range(2)]
            VT = [Buf("vt%d" % i) for i in range(NT)]
            tA = [ARX[:, 8192 + i * 512:8192 + (i + 1) * 512] for i in range(4)]
            TA = [Buf("tA%d" % i) for i in range(4)]
            S.alias(TA, XT)
            taring = Ring(2)

            def proj_fm(s, col0, q, pi):
                for k in range(8):
                    MM(pbank[pi][:], wr[s][:, k, col0:col0 + 128], hT[:, k, q * 512:(q + 1) * 512], k == 0, k == 7,
                       [WR[s]] + HT[q * 8:q * 8 + 8], [PB[pi]])

            s_qk = load_wblock(winv[:, :, 1536:2048])
            s_qks = load_wblock(winv[:, :, 3072:3584])
            s_v = load_wblock(winv[:, :, 2048:2560])
            for cc in range(4):
                isk = cc >= 2
                dst = krot if isk else qrot
                dbuf = KR if isk else QR
                for q in range(4):
                    p1 = pring.nxt()
                    proj_fm(s_qk, cc * 128, q, p1)
                    p2 = pring.nxt()
                    proj_fm(s_qks, cc * 128, q, p2)
                    ta = taring.nxt()
                    cosv = rope[:, q * 512:(q + 1) * 512]
                    sinv = rope[:, L + q * 512:L + (q + 1) * 512]
                    if isk:
                        STT("dve", tA[2 * ta], pbank[p1][:], 0.125, cosv, ALU.mult, ALU.mult, [PB[p1], b_rope], [TA[2 * ta]])
                        STT("dve", tA[2 * ta + 1], pbank[p2][:], 0.125, sinv, ALU.mult, ALU.mult, [PB[p2], b_rope],
                            [TA[2 * ta + 1]])
                    else:
                        TT("dve", tA[2 * ta], pbank[p1][:], cosv, ALU.mult, [PB[p1], b_rope], [TA[2 * ta]])
                        TT("dve", tA[2 * ta + 1], pbank[p2][:], sinv, ALU.mult, [PB[p2], b_rope], [TA[2 * ta + 1]])
                    TT("dve", dst[:, cc % 2, q * 512:(q + 1) * 512], tA[2 * ta], tA[2 * ta + 1], ALU.add,
                       [TA[2 * ta], TA[2 * ta + 1]], [dbuf[cc % 2][q]])
            for i in range(NT):
                pi = pring.nxt()
                for k in range(8):
                    MM(pbank[pi][:], hT[:, k, i * 128:(i + 1) * 128], wr[s_v][:, k, :], k == 0, k == 7,
                       [WR[s_v]] + HT[2 * i:2 * i + 2], [PB[pi]])
                CP("act", vtok[:, i, :], pbank[pi][:], [PB[pi]], [VT[i]])
            gT = ARB[:, 10240:18432].rearrange("p (c t) -> p c t", c=4)
            GTB = [[Buf("gT%d_%d" % (c, q)) for q in range(4)] for c in range(4)]
            s_g = load_wblock(winv[:, :, 2560:3072])

            def gproj_job(j):
                cg, q = j // 4, j % 4
                pi = pring.nxt()
                proj_fm(s_g, cg * 128, q, pi)
                ACT(gT[:, cg, q * 512:(q + 1) * 512], pbank[pi][:], AF.Silu, [PB[pi]], [GTB[cg][q]])

            if "qk" in dbg_d:
                for c in range(2):
                    for (nm, src, bb) in (("q", qrot, QR), ("k", krot, KR)):
                        r0 = (0 if nm == "q" else 256) + c * 128
                        final_ops.append(S.dma("sp", dbg_d["qk"][r0:r0 + 128, :], src[:, c, :], reads=bb[c]))

            CK("p2")
            XU = 8192
            Sallb = XB[:, 2 * XU:2 * XU + 4096].rearrange("p (n c e) -> p n c e", n=NT, c=2)
            o = XU + 2048
            Pm = [XB[:, 2 * (o + i * 256):2 * (o + i * 256) + 512].rearrange("p (h i) -> p h i", h=4) for i in range(2)]
            o += 512
            sq = [XB[:, 2 * (o + i * 256):2 * (o + i * 256) + 512] for i in range(2)]
            o += 512
            sg = [XB[:, 2 * (o + i * 256):2 * (o + i * 256) + 512].rearrange("p (h i) -> p h i", h=4) for i in range(2)]
            o += 512
            rsb = ARX[:, o:o + 512]
            o += 512
            tyb = ARX[:, o:o + 512].rearrange("p (h i) -> p h i", h=4)
            o += 512
            S32 = [ARX[:, o + d * 256:o + (d + 1) * 256].rearrange("p (c e) -> p c e", c=2) for d in range(2)]
            o += 512
            tmpS = ARX[:, o:o + 256].rearrange("p (c e) -> p c e", c=2)
            o += 256
            Sfb = [XB[:, 2 * (o + i * 128):2 * (o + i * 128) + 256].rearrange("p (c e) -> p c e", c=2) for i in range(2)]
            o += 256
            ktile = [XB[:, 2 * (o + i * 128):2 * (o + i * 128) + 256] for i in range(2)]
            o += 256
            qfb0 = XB[:, 2 * o:2 * o + 1536]
            o += 768
            qfb1 = XB[:, 2 * o:2 * o + 1536]
            qfb = [q_.rearrange("p (h d i) -> p h d i", h=4, d=3) for q_ in (qfb0, qfb1)]
            kct = XB[:, 2 * o:2 * o + 1024].rearrange("p (t d n) -> p t d n", t=2, d=2)
            o += 512
            vct = XB[:, 2 * o:2 * o + 1024].rearrange("p (t n) -> p t n", t=2)
            o += 512
            sg.append(XB[:, 2 * o:2 * o + 512].rearrange("p (h i) -> p h i", h=4))
            o += 256
            assert o <= 16384, o
            ret_bufs = {k: Buf(k) for k in ["Sallb", "rsb", "tyb", "tmpS", "kct", "vct"]}
            PM = [Buf("pm%d" % i) for i in range(2)]
            SQ = [Buf("sq%d" % i) for i in range(2)]
            SG = [Buf("sg%d" % i) for i in range(3)]
            S32B = [Buf("s32_%d" % i) for i in range(2)]
            SFB = [Buf("sfb%d" % i) for i in range(2)]
            KTL = [Buf("kt%d" % i) for i in range(2)]
            QFB = [Buf("qfb%d" % i) for i in range(2)]
            SALLB = [Buf("sallb%d" % i) for i in range(NT)]
            all_ret = list(ret_bufs.values()) + PM + SQ + SG + S32B + SFB + KTL + QFB + SALLB
            S.alias(all_ret, XT + XN + TA)

            CK("rconst")
            for t in range(2):
                pk = pring.nxt()
                for k in range(8):
                    MM(pbank[pk][:, 0:256], hcT[:, k, t * 128:(t + 1) * 128], wr[s_qk][:, k, 256:512], k == 0, k == 7,
                       [WR[s_qk]] + HCT[2 * t:2 * t + 2], [PB[pk]])
                pv = pring.nxt()
                for k in range(8):
                    MM(pbank[pv][:], hcT[:, k, t * 128:(t + 1) * 128], wr[s_v][:, k, :], k == 0, k == 7,
                       [WR[s_v]] + HCT[2 * t:2 * t + 2], [PB[pv]])
                for d in range(2):
                    STT("dve", kct[:, t, d, :].rearrange("p (h e) -> p h e", h=4),
                        pbank[pk][:, 0:256].rearrange("p (h e) -> p h e", h=4), 0.125,
                        bc(wcx[:, t, d, :].unsqueeze(2), [128, 4, 64]), ALU.mult, ALU.mult,
                        [PB[pk], b_rtab], [ret_bufs["kct"]])
                CP("act", vct[:, t, :], pbank[pv][:], [PB[pv]], [ret_bufs["vct"]])
            for d in range(2):
                pi = pring.nxt()
                for h in range(4):
                    c = h // 2
                    for t in range(2):
                        MM(pbank[pi][:, h * 128:(h + 1) * 128], kct[:, t, d, c * 128:(c + 1) * 128],
                           vct[:, t, h * 128:(h + 1) * 128], t == 0, t == 1,
                           [ret_bufs["kct"], ret_bufs["vct"]], [PB[pi]])
                pv4 = pbank[pi][:].rearrange("p (s e) -> p s e", s=4)
                CP("dve", S32[d][0:64, :, :], pv4[0:64, 0:4:2, :], [PB[pi]], [S32B[d]])
                CP("dve", S32[d][64:128, :, :], pv4[64:128, 1:4:2, :], [PB[pi]], [S32B[d]])
            DBG("s_f", S32[0], [S32B[0]])
            DBG("s_b", S32[1], [S32B[1]])

            CK("ctx")
            Sallf = ARB[:, 0:4096].rearrange("p (n c e) -> p n c e", n=NT, c=2)
            B_SF = Buf("sallf")
            S.alias([B_SF], [b_rope])
            CP("act", Sallb[:, 15, :, :], S32[1], [S32B[1]], [SALLB[15]])
            CP("act", Sallf[:, 0, :, :], S32[0], [S32B[0]], [B_SF])
            kt4 = [ktile[0], ktile[1], sg[0].rearrange("p h i -> p (h i)")[:, 0:256], sg[1].rearrange("p h i -> p (h i)")[:, 0:256]]
            KT4 = [KTL[0], KTL[1], SG[0], SG[1]]
            tmpS2 = [tmpS, sq[0].bitcast(F32).rearrange("p (c e) -> p c e", c=2)]
            TMPS2 = [ret_bufs["tmpS"], SQ[0]]
            ktring = Ring(4)

            Dm = [[sg[2].rearrange("p h i -> p (h i)")[:, (d * 2 + c) * 128:(d * 2 + c + 1) * 128] for c in range(2)]
                  for d in range(2)]
            b_Dm = SG[2]
            for d in range(2):
                for c in range(2):
                    TS("dve", Dm[d][c], ident[:], dec[:, c, d:d + 1], None, ALU.mult, None, [b_ident, b_rtab], [b_Dm])

            def scan_step(n, d, s_prev, prev_bufs, s_new, new_bufs):
                ti = tring.nxt()
                for c in range(2):
                    TR(ptr[ti][:, c * 128:(c + 1) * 128], krot[:, c, n * 128:(n + 1) * 128], ident[:],
                       [KR[c][n // 4], b_ident], [PT[ti]])
                ki = ktring.nxt()
                TT("dve", kt4[ki].rearrange("p (h e) -> p h e", h=4),
                   ptr[ti][:, 0:256].rearrange("p (h e) -> p h e", h=4),
                   bc(colfb[:, d, :].unsqueeze(2), [128, 4, 64]), ALU.mult, [PT[ti], b_rtab], [KT4[ki]])
                pi = pring.nxt()
                for h in range(4):
                    c = h // 2
                    MM(pbank[pi][:, h * 128:(h + 1) * 128], kt4[ki][:, c * 128:(c + 1) * 128],
                       vtok[:, n, h * 128:(h + 1) * 128], True, False, [KT4[ki], VT[n]], [PB[pi]])
                    MM(pbank[pi][:, h * 128:(h + 1) * 128], Dm[d][c], s_prev[:, c, :], False, True,
                       [b_Dm] + prev_bufs, [PB[pi]])
                pv4 = pbank[pi][:].rearrange("p (s e) -> p s e", s=4)
                CP("act", s_new[0:64, :, :], pv4[0:64, 0:4:2, :], [PB[pi]], new_bufs)
                CP("act", s_new[64:128, :, :], pv4[64:128, 1:4:2, :], [PB[pi]], new_bufs)

            gproj_job(0)
            for s_ in range(15):
                scan_step(s_, 0, Sallf[:, s_, :, :], [B_SF], Sallf[:, s_ + 1, :, :], [B_SF])
                scan_step(15 - s_, 1, Sallb[:, 15 - s_, :, :], [SALLB[15 - s_]], Sallb[:, 14 - s_, :, :], [SALLB[14 - s_]])
                if s_ % 2 == 0:
                    gproj_job(s_ + 1)
                    if s_ + 2 <= 15:
                        gproj_job(s_ + 2)

            S.alias([QFB[1]], [ret_bufs["kct"], ret_bufs["vct"]])
            CK("bwd")
            mixR = ARE[:, 21504:29696].rearrange("p (c t) -> p c t", c=4)
            mixHy = ARA[:, 8192:16384].rearrange("p (c t) -> p c t", c=4)

            def mix_chunk(k):
                return mixHy[:, k, :] if k < 4 else mixR[:, k - 4, :]
            MIXR = [Buf("mixr%d" % i) for i in range(NT)]
            stt = {}

            def O1(n):
                tsl = slice(n * 128, (n + 1) * 128)
                qbuf = [QR[0][n // 4], QR[1][n // 4]]
                kbuf = [KR[0][n // 4], KR[1][n // 4]]
                qi = n % 2
                pS = n % 2
                TT("dve", qfb[qi].rearrange("p (c hh) d i -> p c (hh d) i", c=2),
                   bc(qrot[:, :, tsl].unsqueeze(2), [128, 2, 6, 128]),
                   rowqm.rearrange("p (c hh) d i -> p c (hh d) i", c=2), ALU.mult, qbuf + [b_rtab], [QFB[qi]])
                for h in range(4):
                    MM(pbank[pS][:, h * 128:(h + 1) * 128], krot[:, h // 2, tsl], qfb[qi][:, h, 2, :], True, True,
                       kbuf + [QFB[qi]], [PB[pS]])

            def O2(n):
                pS = n % 2
                pmi = n % 2
                TT("dve", Pm[pmi], pbank[pS][:].rearrange("p (h i) -> p h i", h=4), DTm, ALU.mult,
                   [PB[pS], b_DT], [PM[pmi]])

            def O3(n):
                qi = n % 2
                pmi = n % 2
                pO = 2 + (n % 2)
                for h in range(4):
                    c = h // 2
                    osl = pbank[pO][:, h * 128:(h + 1) * 128]
                    MM(osl, vtok[:, n, h * 128:(h + 1) * 128], Pm[pmi][:, h, :], True, False, [VT[n], PM[pmi]], [PB[pO]])
                    MM(osl, Sallf[:, n, c, :], qfb[qi][:, h, 0, :], False, False, [B_SF, QFB[qi]], [PB[pO]])
                    MM(osl, Sallb[:, n, c, :], qfb[qi][:, h, 1, :], False, True, [SALLB[n], QFB[qi]], [PB[pO]])

            def O4a(n):
                pO = 2 + (n % 2)
                sqi = n % 2
                ACT(sq[sqi], pbank[pO][:], AF.Square, [PB[pO]], [SQ[sqi]])
                pR = 4
                MM(pbank[pR][:], ones_bf[:], sq[sqi], True, True, [b_ones, SQ[sqi]], [PB[pR]])
                ACT(rsb, pbank[pR][:], AF.Ln, [PB[pR], b_negpi], [ret_bufs["rsb"]], scale=1.0 / 128, bias=epsc[:, 0:1])
                ACT(rsb, rsb, AF.Exp, [ret_bufs["rsb"]], [ret_bufs["rsb"]], scale=-0.5)

            def O4b(n):
                tsl = slice(n * 128, (n + 1) * 128)
                pO = 2 + (n % 2)
                TT("dve", tyb, gT[:, :, tsl], rsb.rearrange("p (h i) -> p h i", h=4), ALU.mult,
                   [ret_bufs["rsb"]] + [GTB[c_][n // 4] for c_ in range(4)], [ret_bufs["tyb"]])
                TT("dve", mixR[:, :, tsl], pbank[pO][:].rearrange("p (h i) -> p h i", h=4), tyb, ALU.mult,
                   [PB[pO], ret_bufs["tyb"]], [MIXR[n]])

            for i in range(NT + 4):
                if 0 <= i - 4 < NT:
                    O4b(i - 4)
                if 0 <= i - 3 < NT:
                    O4a(i - 3)
                if 0 <= i - 2 < NT:
                    O3(i - 2)
                if 0 <= i - 1 < NT:
                    O2(i - 1)
                if i < NT:
                    O1(i)
                if i in (3, 7, 11, 15):
                    jm = 6 + (i - 3) // 4
                    mod_col_half(jm, 2 if jm < 8 else 3, pi=5)
            if "yret" in dbg_d:
                for c in range(4):
                    final_ops.append(S.dma("sp", dbg_d["yret"][c * 128:(c + 1) * 128, :], mixR[:, c, :], reads=MIXR))
            CK("ret")
            b_g1b = Buf("g1b")
            b_g2b = Buf("g2b")
            b_bmt = Buf("bmt")
            STT("dve", AB[:, 4, :], modcol[:, 24:32, 0], 1.0, n2c[:], ALU.add, ALU.mult, [b_modcol, b_small["n2c"]], [b_AB])
            CP("dve", AB[:, 5, :], modcol[:, 16:24, 0], [b_modcol], [b_AB])

            ztok = ARB[:].rearrange("p (i n) -> p i n", i=NT)
            ZT = [[Buf("zt%d_%d" % (g, i)) for i in range(NT)] for g in range(3)]
            S.alias([b for l in ZT for b in l], [b for l in GTB for b in l] + [b_rope, B_SF] + HCT)
            o = XU
            araw = [ARX[:, o + i * 2052:o + i * 2052 + 2050] for i in range(2)]
            o += 2 * 2052
            zc = [XB[:, 2 * o:2 * o + 2048]]
            o += 1024
            rcv = [ARX[:, 6144:8192], ARX[:, o:o + 2048]]
            o += 2048
            assert o <= 16384, o
            rconv = ARX[:, 6144:8192]
            ARAW = [Buf("araw%d" % i) for i in range(2)]
            ZC = [Buf("zc0")]
            RCV = [Buf("rcv%d" % i) for i in range(2)]
            b_rconv = Buf("rconv")
            S.alias(ARAW + ZC, all_ret)
            S.alias([b_rconv], [WR[3]])
            S.alias([RCV[0]], [WR[3]])
            S.alias([RCV[1]], all_ret)
            zfT = EF[0:33, 0:2048]
            hid1T = EF[0:64, 2048:4096]
            fa = EF[0:64, 4096:4608]
            fk = EF[0:64, 4608:5120]
            fki = BIGI[0:64, 36864 + 5120:36864 + 5632]
            fw = EF[0:64, 5632:6144]
            hw1 = EF[0:33, 6144:6208]
            hw2 = EF[0:64, 6208:6272]
            hyp = EF[0:64, 6272:6276]
            b_hw = Buf("hw")
            b_ft = Buf("ftmp")
            qkv_bufs = [b for l in QR + KR for b in l] + VT
            S.alias([b_hw, b_ft, b_zf, b_hid1], qkv_bufs)
            S.alias([b_hid2, b_small["w3b"]], [b_rc, b_lg, b_DT, b_et, b_rtab])
            S.dma("sp", zfT, zfT_d, writes=[b_zf])
            S.dma("sp", hw1, hw1_d, writes=[b_hw])
            S.dma("sp", hw2, hw2_d, writes=[b_hw])
            S.dma("sp", hyp, hyp_d, writes=[b_hw])
            w3sc = EF[0:64, 6400:7424]
            b_w3sc = Buf("w3sc")
            S.alias([b_w3sc], qkv_bufs)
            for od_ in range(2):
                S.dma("sp", w3sc[:, 0:512], hw3_d[:, od_ * 512:(od_ + 1) * 512], writes=[b_w3sc])
                S.dma("sp", w3sc[:, 512:1024], hw3_d[:, 1024 + od_ * 512:1024 + (od_ + 1) * 512], writes=[b_w3sc])
                TT("dve", w3b[:, od_ * 512:(od_ + 1) * 512], w3sc[:, 0:512], w3sc[:, 512:1024], ALU.add,
                   [b_w3sc], [b_small["w3b"]])
                TT("dve", w3b[:, 1024 + od_ * 512:1024 + (od_ + 1) * 512], w3sc[:, 0:512], w3sc[:, 512:1024], ALU.subtract,
                   [b_w3sc], [b_small["w3b"]])

            def sin_layer(pi, bcol, fcol, out_ap, out_buf):
                TS("dve", fa, pbank[pi][0:64, :], bcol, fcol, ALU.add, ALU.mult, [PB[pi], b_hw], [b_ft])
                TS("dve", fki, fa, 1.0 / (2 * PI), 16.5, ALU.mult, ALU.add, [b_ft], [b_ft])
                CP("dve", fk, fki, [b_ft], [b_ft])
                STT("dve", fw, fk, -2 * PI, fa, ALU.mult, ALU.add, [b_ft], [b_ft])
                TS("dve", fk, fw, -33 * PI, 2 * PI, ALU.is_lt, ALU.mult, [b_ft], [b_ft])
                STT("dve", fw, fw, 33 * PI, fk, ALU.add, ALU.add, [b_ft], [b_ft])
                ACT(out_ap, fw, AF.Sin, [b_ft, b_negpi], [out_buf], bias=negpi[0:64, 0:1])

            def filt_unit(u):
                q = u % 4
                pi = pring.nxt()
                if u < 4:
                    MM(pbank[pi][0:64, :], hw1, zfT[:, q * 512:(q + 1) * 512], True, True, [b_hw, b_zf], [PB[pi]])
                    sin_layer(pi, hyp[:, 0:1], hyp[:, 1:2], hid1T[:, q * 512:(q + 1) * 512], b_hid1)
                else:
                    MM(pbank[pi][0:64, :], hw2, hid1T[:, q * 512:(q + 1) * 512], True, True, [b_hw, b_hid1], [PB[pi]])
                    sin_layer(pi, hyp[:, 2:3], hyp[:, 3:4], hid2T[:, q * 512:(q + 1) * 512], b_hid2)
            for i in range(2):
                MEMSET("pool", araw[i][:, 0:1], 0.0, [ARAW[i]])
                MEMSET("pool", araw[i][:, 2049:2050], 0.0, [ARAW[i]])
            arring = Ring(2)
            zcring = Ring(1)
            evq = [0]

            hy_slots = {}

            def hy_A(chunk):
                g, cc = chunk // 4, chunk % 4
                if cc == 0:
                    hy_slots[g] = load_wblock(winv[:, :, g * 512:(g + 1) * 512])
                s_ = hy_slots[g]
                ai = arring.nxt()
                for q in range(4):
                    pi = pring.nxt()
                    proj_fm(s_, cc * 128, q, pi)
                    CP("act", araw[ai][:, 1 + q * 512:1 + (q + 1) * 512], pbank[pi][:], [PB[pi]], [ARAW[ai]])
                    ACT(rcv[ai][:, q * 512:(q + 1) * 512], pbank[pi][:], AF.Identity, [PB[pi], b_small["hcw"]], [RCV[ai]],
                        scale=hcw[:, chunk, 1:2], bias=hcw[:, chunk, 3:4])
                return ai

            def hy_B(chunk, ai):
                wq = hcw[:, chunk, :]
                zi = zcring.nxt()
                STT("dve", rcv[ai], araw[ai][:, 0:2048], wq[:, 0:1], rcv[ai], ALU.mult, ALU.add,
                    [ARAW[ai], b_small["hcw"], RCV[ai]], [RCV[ai]])
                STT("dve", zc[zi], araw[ai][:, 2:2050], wq[:, 2:3], rcv[ai], ALU.mult, ALU.add,
                    [ARAW[ai], b_small["hcw"], RCV[ai]], [ZC[zi]])
                return zi

            def hy_C(chunk, zi):
                g = chunk // 4
                for half in range(2):
                    ti = tring.nxt()
                    for a in range(8):
                        i = half * 8 + a
                        TR(ptr[ti][:, a * 128:(a + 1) * 128], zc[zi][:, i * 128:(i + 1) * 128], ident[:],
                           [ZC[zi], b_ident], [PT[ti]])
                    CP("act", ztok[:, half * 8:half * 8 + 8, chunk * 128:(chunk + 1) * 128],
                       ptr[ti][:].rearrange("p (a c) -> p a c", a=8), [PT[ti]], ZT[g][half * 8:half * 8 + 8])

            ai_cur = hy_A(0)
            for chunk in range(12):
                ai_nxt = hy_A(chunk + 1) if chunk + 1 < 12 else None
                zi = hy_B(chunk, ai_cur)
                hy_C(chunk, zi)
                if chunk < 8:
                    filt_unit(chunk)
                ai_cur = ai_nxt
            if "ztok" in dbg_d:
                for i in range(NT):
                    final_ops.append(S.dma("sp", dbg_d["ztok"][i * 128:(i + 1) * 128, :], ztok[:, i, :],
                                           reads=[ZT[0][i], ZT[1][i], ZT[2][i]]))

            CK("hyproj")
            ksum = ARA[:, 0:8192].rearrange("p (i n) -> p i n", i=NT)
            kdiff = ARA[:, 8192:16384].rearrange("p (i n) -> p i n", i=NT)
            KS = [Buf("ks%d" % i) for i in range(NT)]
            S.alias(KS, HT)
            Yre = ARE[:, 0:8192].rearrange("p (j n) -> p j n", j=16)
            Yim = ARE[:, 8192:16384].rearrange("p (j n) -> p j n", j=16)
            YB = [Buf("y%d" % j) for j in range(16)]
            S.alias(YB, qkv_bufs + [b_hw, b_ft, b_zf, b_hid1])
            fblk = [XB[:, i * 4096:(i + 1) * 4096].rearrange("p (i s r) -> p i s r", i=16, s=2) for i in range(2)]
            FB = [Buf("fb%d" % i) for i in range(2)]
            gt = [ARX[:, 4096 + i * 512:4096 + (i + 1) * 512] for i in range(4)]
            GT = [Buf("gt%d" % i) for i in range(4)]
            yhy = [XB[:, 2 * (6144 + i * 256):2 * (6144 + i * 256) + 512] for i in range(2)]
            YH = [Buf("yhy%d" % i) for i in range(2)]
            S.alias(FB + GT + YH, WR + [b_rconv] + RCV)
            wpc = [XB[:, 2 * (6656 + i * 512):2 * (6656 + i * 512) + 1024].rearrange("p (i n) -> p i n", i=8) for i in range(2)]
            bpc = [ARX[:, 7680 + i * 128:7680 + (i + 1) * 128] for i in range(2)]
            gpc = [ARX[:, 7936 + i * 128:7936 + (i + 1) * 128] for i in range(2)]
            WPC = [Buf("wpc%d" % i) for i in range(2)]
            BPC = [Buf("bpc%d" % i) for i in range(2)]
            GPC = [Buf("gpc%d" % i) for i in range(2)]
            S.alias(WPC + BPC + GPC, WR + [b_rconv] + RCV)

            def g1_piece(c8):
                i = c8 % 2
                c0 = 2048 + c8 * 128
                S.dma("pool", wpc[i], wmv[:, :, c0:c0 + 128], writes=[WPC[i]])
                S.dma("sp", bpc[i], bmodr_d[0:1, c0:c0 + 128].partition_broadcast(128), writes=[BPC[i]])
                pi = pring.nxt()
                for k in range(8):
                    MM(pbank[pi][:, 0:128], srep[:, k, :], wpc[i][:, k, :], k == 0, k == 7, [WPC[i], b_small["srep"]],
                       [PB[pi]])
                TT("dve", gpc[i], pbank[pi][:, 0:128], bpc[i], ALU.add, [PB[pi], BPC[i]], [GPC[i]])
                TT("dve", woutS[:, :, c8 * 128:(c8 + 1) * 128], woutS[:, :, c8 * 128:(c8 + 1) * 128],
                   bc(gpc[i].unsqueeze(1), [128, 8, 128]), ALU.mult, [b_wout, GPC[i]], [b_wout])
            o = XU
            hbb = ARX[:, o:o + 512]; o += 512
            absb = ARX[:, o:o + 512]; o += 512
            wint = [ARX[:, o + i * 512:o + (i + 1) * 512] for i in range(2)]; o += 1024
            kfw = [ARX[:, o + i * 512:o + (i + 1) * 512] for i in range(2)]; o += 1024
            Ksb = [ARX[:, o + i * 512:o + (i + 1) * 512] for i in range(4)]; o += 2048
            tq = [ARX[:, o + i * 512:o + (i + 1) * 512] for i in range(4)]; o += 2048
            ynq = XB[0:1, 2 * o:2 * o + 512]; o += 256
            knq = ARX[0:1, o:o + 512]; o += 512
            assert o <= 16384, o
            b_hbb = Buf("hbb")
            b_absb = Buf("absb")
            WINT = [Buf("win%d" % i) for i in range(2)]
            KFW = [Buf("kfw%d" % i) for i in range(2)]
            KSB = [Buf("ksb%d" % i) for i in range(4)]
            TQ = [Buf("tq%d" % i) for i in range(4)]
            b_ynq = Buf("ynq")
            b_knq = Buf("knq")
            S.alias([b_hbb, b_absb, b_ynq, b_knq] + WINT + KFW + KSB + TQ, ARAW + ZC + RCV + all_ret)
            S.dma("sp", absb, absd_d.partition_broadcast(128), writes=[b_absb])
            fbring = Ring(2)
            mixH = [Buf("mixh%d" % i) for i in range(NT)]

            for od in range(2):
                vt_col = 0
                xg_col = 512 * (od + 1)
                S.dma("sp", hbb, hbias_d[0:1, od * 512:(od + 1) * 512].partition_broadcast(128), writes=[b_hbb])
                def filt_tile(od_, i):
                    wi = i % 2
                    ACT(wint[wi], absb, AF.Exp, [b_absb, b_small["tn"]], [WINT[wi]], scale=tn[:, i:i + 1])
                    pf = pring.nxt()
                    MM(pbank[pf][:], hid2T[:, i * 128:(i + 1) * 128], w3b[:, od_ * 512:(od_ + 1) * 512], True, True,
                       [b_hid2, b_small["w3b"]], [PB[pf]])
                    pb_ = pring.nxt()
                    MM(pbank[pb_][:], hid2T[:, i * 128:(i + 1) * 128], w3b[:, 1024 + od_ * 512:1024 + (od_ + 1) * 512],
                       True, True, [b_hid2, b_small["w3b"]], [PB[pb_]])
                    if i != 0:
                        TT("dve", ksum[:, i, :], pbank[pf][:], wint[wi], ALU.mult, [PB[pf], WINT[wi]], [KS[i]])
                        TT("dve", kdiff[:, i, :], pbank[pb_][:], wint[wi], ALU.mult, [PB[pb_], WINT[wi]], [KS[i]])
                    else:
                        TT("dve", kfw[0], pbank[pf][:], wint[wi], ALU.mult, [PB[pf], WINT[wi]], [KFW[0]])
                        TT("dve", kfw[1], pbank[pb_][:], wint[wi], ALU.mult, [PB[pb_], WINT[wi]], [KFW[1]])
                        TT("dve", kfw[0][0:1, :], kfw[0][0:1, :], kfw[1][0:1, :], ALU.add, [KFW[0], KFW[1]], [KFW[0]])
                        TS("dve", kfw[0][0:1, :], kfw[0][0:1, :], 0.5, None, ALU.mult, None, [KFW[0]], [KFW[0]])
                        CP("dve", kfw[1][0:1, :], kfw[0][0:1, :], [KFW[0]], [KFW[1]])
                        CP("dve", ksum[:, i, :], kfw[0], [KFW[0]], [KS[i]])
                        CP("dve", kdiff[:, i, :], kfw[1], [KFW[1]], [KS[i]])
                if od == 0:
                    for i in range(NT):
                        filt_tile(0, i)
                vbufs = [ZT[0][i] for i in range(NT)]
                for j in range(16):
                    fi = fbring.nxt()
                    S.dma("sp", fblk[fi].rearrange("p i s r -> p (i s r)"), fd_d[j], writes=[FB[fi]])
                    pKr = pring.nxt()
                    for i in range(NT):
                        MM(pbank[pKr][:], fblk[fi][:, i, 0, :], ksum[:, i, :], i == 0, i == NT - 1, [FB[fi], KS[i]], [PB[pKr]])
                    pKi = pring.nxt()
                    for i in range(NT):
                        MM(pbank[pKi][:], fblk[fi][:, i, 1, :], kdiff[:, i, :], i == 0, i == NT - 1, [FB[fi], KS[i]], [PB[pKi]])
                    kb = (j % 2) * 2
                    CP("act", Ksb[kb], pbank[pKr][:], [PB[pKr]], [KSB[kb]])
                    CP("act", Ksb[kb + 1], pbank[pKi][:], [PB[pKi]], [KSB[kb + 1]])
                    TT("dve", Ksb[kb], Ksb[kb], hbb, ALU.add, [KSB[kb], b_hbb], [KSB[kb]])
                    pUr = pring.nxt()
                    for i in range(NT):
                        MM(pbank[pUr][:], fblk[fi][:, i, 0, :], ztok[:, i, vt_col:vt_col + 512], i == 0, i == NT - 1,
                           [FB[fi], vbufs[i]], [PB[pUr]])
                    pUi = pring.nxt()
                    for i in range(NT):
                        MM(pbank[pUi][:], fblk[fi][:, i, 1, :], ztok[:, i, vt_col:vt_col + 512], i == 0, i == NT - 1,
                           [FB[fi], vbufs[i]], [PB[pUi]])
                    TT("dve", tq[0], pbank[pUr][:], Ksb[kb], ALU.mult, [PB[pUr], KSB[kb]], [TQ[0]])
                    TT("dve", tq[1], pbank[pUi][:], Ksb[kb + 1], ALU.mult, [PB[pUi], KSB[kb + 1]], [TQ[1]])
                    TT("dve", Yre[:, j, :], tq[0], tq[1], ALU.subtract, [TQ[0], TQ[1]], [YB[j]])
                    TT("dve", tq[2], pbank[pUr][:], Ksb[kb + 1], ALU.mult, [PB[pUr], KSB[kb + 1]], [TQ[2]])
                    TT("dve", tq[3], pbank[pUi][:], Ksb[kb], ALU.mult, [PB[pUi], KSB[kb]], [TQ[3]])
                    TT("dve", Yim[:, j, :], tq[2], tq[3], ALU.add, [TQ[2], TQ[3]], [YB[j]])
                    if j == 0:
                        TS("dve", Yre[0:1, 0, :], Yre[0:1, 0, :], 0.5, None, ALU.mult, None, [YB[0]], [YB[0]])
                pN = pring.nxt()
                for i in range(NT):
                    MM(pbank[pN][0:1, :], nyqc[:, 0:1], ksum[:, i, :], i == 0, i == NT - 1, [b_small["nyq"], KS[i]], [PB[pN]])
                CP("act", knq, pbank[pN][0:1, :], [PB[pN]], [b_knq])
                TT("dve", knq, knq, hbb[0:1, :], ALU.add, [b_knq, b_hbb], [b_knq])
                pN2 = pring.nxt()
                for i in range(NT):
                    MM(pbank[pN2][0:1, :], nyqc[:, 0:1], ztok[:, i, vt_col:vt_col + 512], i == 0, i == NT - 1,
                       [b_small["nyq"], vbufs[i]], [PB[pN2]])
                STT("dve", ynq, pbank[pN2][0:1, :], 0.5, knq, ALU.mult, ALU.mult, [PB[pN2], b_knq], [b_ynq])
                if od == 1:
                    woutS = ARA[:, 0:8192].rearrange("p (i n) -> p i n", i=8)
                    b_wout = Buf("wout")
                    S.alias([b_wout], KS)
                    S.alias(mixH, KS)
                    S.dma("pool", woutS, wout_d.rearrange("(i p) n -> p i n", p=128), writes=[b_wout])
                pend_T = []
                for t in range(NT):
                    fi = fbring.nxt()
                    S.dma("sp", fblk[fi].rearrange("p i s r -> p (i s r)"), fd_d[t], writes=[FB[fi]])
                    pY = pring.nxt()
                    for i in range(NT):
                        MM(pbank[pY][:], fblk[fi][:, i, 0, :], Yre[:, i, :], i == 0, False, [FB[fi], YB[i]], [PB[pY]])
                    for i in range(NT):
                        MM(pbank[pY][:], fblk[fi][:, i, 1, :], Yim[:, i, :], False, False, [FB[fi], YB[i]], [PB[pY]])
                    MM(pbank[pY][:], nyqr[0:1, :], ynq, False, True, [b_small["nyq"], b_ynq], [PB[pY]])
                    if od == 0:
                        filt_tile(1, t)
                    if od == 1 and t < 8:
                        g1_piece(t)
                    if od == 1 and t >= 1 and pend_T:
                        yhy_T(pend_T.pop(0))
                    if od == 0:
                        STT("dve", ztok[:, t, 0:512], pbank[pY][:], 2.0 / NFFT, ztok[:, t, xg_col:xg_col + 512],
                            ALU.mult, ALU.mult, [PB[pY], ZT[1][t]], [ZT[0][t]])
                    else:
                        yi = t % 2
                        STT("dve", yhy[yi], pbank[pY][:], 2.0 / NFFT, ztok[:, t, xg_col:xg_col + 512],
                            ALU.mult, ALU.mult, [PB[pY], ZT[2][t]], [YH[yi]])
                        def yhy_T(t_):
                            yi_ = t_ % 2
                            ti = tring.nxt()
                            for c in range(4):
                                TR(ptr[ti][:, c * 128:(c + 1) * 128], yhy[yi_][:, c * 128:(c + 1) * 128], ident[:],
                                   [YH[yi_], b_ident], [PT[ti]])
                            CP("act", mixHy[:, :, t_ * 128:(t_ + 1) * 128],
                               ptr[ti][:, 0:512].rearrange("p (c i) -> p c i", c=4), [PT[ti]], [mixH[t_]])
                        pend_T.append(t)
                while pend_T:
                    yhy_T(pend_T.pop(0))
                if od == 0 and "y1" in dbg_d:
                    for i in range(NT):
                        final_ops.append(S.dma("sp", dbg_d["y1"][i * 128:(i + 1) * 128, :], ztok[:, i, 0:512],
                                               reads=[ZT[0][i]]))
            if "yhy" in dbg_d:
                for c in range(4):
                    final_ops.append(S.dma("sp", dbg_d["yhy"][c * 128:(c + 1) * 128, :], mixHy[:, c, :], reads=mixH))

            if "zlate" in dbg_d:
                for i in range(NT):
                    final_ops.append(S.dma("sp", dbg_d["zlate"][i * 128:(i + 1) * 128, :], ztok[:, i, :],
                                           reads=[ZT[0][i], ZT[1][i], ZT[2][i]]))
            CK("hyena")
            X1 = ARX[:].rearrange("p (t n) -> p t n", t=NT)
            X1B = [Buf("x1_%d" % i) for i in range(NT)]
            xt2 = [EF[:, i * 1024:(i + 1) * 1024] for i in range(2)]
            XT2 = [Buf("xt2_%d" % i) for i in range(2)]
            S.alias(XT2, YB)
            all_x_scratch = (WR + WPC + BPC + GPC + [b_rconv] + RCV + FB + GT + YH + [b_hbb, b_absb, b_ynq, b_knq] + WINT + KFW + KSB + TQ
                             + ARAW + ZC + all_ret + XT + XN + TA + HCT + [b_zf, b_hid1])
            S.alias(X1B, all_x_scratch)
            xn2 = [ARB[:, i * 1024:(i + 1) * 1024] for i in range(NT)]
            XN2 = [Buf("xn2_%d" % i) for i in range(NT)]
            S.alias(XN2, [b for l in ZT for b in l])
            xn2ring = Ring(NT)
            xi2 = {}
            for t in range(NT):
                xi = t % 2
                S.dma("sp", xt2[xi], x_d[t * 128:(t + 1) * 128, :], writes=[XT2[xi]])
                for half in range(2):
                    pi = pring.nxt()
                    for k in range(8):
                        MM(pbank[pi][:], mix_chunk(k)[:, t * 128:(t + 1) * 128], woutS[:, k, half * 512:(half + 1) * 512],
                           k == 0, k == 7, [mixH[t], MIXR[t], b_wout], [PB[pi]])
                    TT("dve", X1[:, t, half * 512:(half + 1) * 512], xt2[xi][:, half * 512:(half + 1) * 512], pbank[pi][:],
                       ALU.add, [XT2[xi], PB[pi]], [X1B[t]])
                xi2[t] = norm_s1(X1[:, t, :], [X1B[t]], 18 + t, xn2, XN2, xn2ring)
            if "x1" in dbg_d:
                for t in range(NT):
                    final_ops.append(S.dma("sp", dbg_d["x1"][t * 128:(t + 1) * 128, :], X1[:, t, :], reads=[X1B[t]]))

            CK("wout")
            h2T = ARA[:].rearrange("p (c t) -> p c t", c=8)
            H2 = [Buf("h2T%d" % i) for i in range(2 * NT)]
            S.alias(H2, [b_wout] + mixH)
            for i in range(NT):
                norm_s2(xi2[i], 4, 5, lambda c, i=i: h2T[:, c, i * 128:(i + 1) * 128], H2[2 * i:2 * i + 2], xn2, XN2)

            CK("norm2")
            actT = [ARB[:, g * 8192:(g + 1) * 8192].rearrange("p (f t) -> p f t", f=4) for g in range(2)]
            wdn = [ARB[:, 16384 + g * 4096:16384 + (g + 1) * 4096].rearrange("p (f n) -> p f n", f=4) for g in range(2)]
            ACTB = [[Buf("act%d_%d" % (g, f)) for f in range(4)] for g in range(2)]
            WDN = [[Buf("wdn%d_%d" % (g, f)) for f in range(4)] for g in range(2)]
            zall = [b for l in ZT for b in l]
            S.alias([b for l in ACTB for b in l] + [b for l in WDN for b in l], zall + XN2)
            wup = [ARE[:, s * 2048:(s + 1) * 2048].rearrange("p (i v n) -> p i v n", i=8, v=2) for s in range(3)]
            WUP = [Buf("wup%d" % s) for s in range(3)]
            S.alias(WUP, YB + XT2)
            wupring = Ring(3)
            rbuf = [EF[:, 3072 + i * 2048:3072 + (i + 1) * 2048] for i in range(2)] + [EF[:, 11272:11272 + 2048]]
            RB = [Buf("rb%d" % i) for i in range(3)]
            rbring = Ring(3)
            araw2 = [EF[:, 7168 + i * 2052:7168 + i * 2052 + 2050] for i in range(2)]
            ARAW2 = [Buf("araw2_%d" % i) for i in range(2)]
            b_rc2 = RB[2]
            e_old = YB + XT2 + [b_hid2, b_small["w3b"], b_rc, b_lg, b_DT, b_et, b_rtab] + MIXR
            S.alias(ARAW2 + [b_rc2], e_old)
            g2b = EF[:, 13320:14344]
            bmt2 = EF[:, 14344:14856]
            b_bmt2 = Buf("bmt2")
            S.alias([b_g2b, b_bmt2], e_old)
            lw2 = [ARE[:, 6144 + i * 4096:6144 + (i + 1) * 4096].rearrange("p (i n) -> p i n", i=8) for i in range(2)]
            LW2 = [Buf("lw2_%d" % i) for i in range(2)]
            S.alias(LW2, YB)
            mod_bc_half(10, g2b, b_g2b, bmt2, b_bmt2, lw2[0], LW2[0])
            mod_bc_half(11, g2b, b_g2b, bmt2, b_bmt2, lw2[1], LW2[1])
            S.alias(RB, LW2)
            for i in range(2):
                MEMSET("pool", araw2[i][:, 0:1], 0.0, [ARAW2[i]])
                MEMSET("pool", araw2[i][:, 2049:2050], 0.0, [ARAW2[i]])
            ar2ring = Ring(2)
            wupv = wup_d.rearrange("(i p) n -> p i n", p=128)

            def ffn_job(s, vg, f):
                ai = ar2ring.nxt()
                ri = rbring.nxt()
                out_ap, out_buf = rbuf[ri], RB[ri]
                wq = fcw[:, vg * NFC + f, :]
                for q in range(4):
                    pi = pring.nxt()
                    for k in range(8):
                        MM(pbank[pi][:], wup[s][:, k, vg, :], h2T[:, k, q * 512:(q + 1) * 512], k == 0, k == 7,
                           [WUP[s]] + H2[q * 8:q * 8 + 8], [PB[pi]])
                    CP("act", araw2[ai][:, 1 + q * 512:1 + (q + 1) * 512], pbank[pi][:], [PB[pi]], [ARAW2[ai]])
                    ACT(out_ap[:, q * 512:(q + 1) * 512], pbank[pi][:], AF.Identity, [PB[pi], b_small["fcw"]], [out_buf],
                        scale=wq[:, 1:2], bias=wq[:, 3:4])
                STT("dve", out_ap, araw2[ai][:, 0:2048], wq[:, 0:1], out_ap, ALU.mult, ALU.add,
                    [ARAW2[ai], b_small["fcw"], out_buf], [out_buf])
                STT("dve", out_ap, araw2[ai][:, 2:2050], wq[:, 2:3], out_ap, ALU.mult, ALU.add,
                    [ARAW2[ai], b_small["fcw"], out_buf], [out_buf])
                return ri

            groups = [list(range(g * 4, min(g * 4 + 4, NFC))) for g in range(6)]

            def ffn_up(gi):
                gb = gi % 2
                for fi, f in enumerate(groups[gi]):
                    s = wupring.nxt()
                    S.dma("pool", wup[s][:, :, 0, :], wupv[:, :, f * 128:(f + 1) * 128], writes=[WUP[s]])
                    S.dma("pool", wup[s][:, :, 1, :], wupv[:, :, DFF + f * 128:DFF + (f + 1) * 128], writes=[WUP[s]])
                    S.dma("pool", wdn[gb][:, fi, :], wdn_d[f * 128:(f + 1) * 128, :], writes=[WDN[gb][fi]])
                    TT("dve", wdn[gb][:, fi, :], wdn[gb][:, fi, :], g2b, ALU.mult, [WDN[gb][fi], b_g2b], [WDN[gb][fi]])
                    rv = ffn_job(s, 0, f)
                    rg = ffn_job(s, 1, f)
                    ACT(rbuf[rg], rbuf[rg], AF.Silu, [RB[rg]], [RB[rg]])
                    TT("dve", actT[gb][:, fi, :], rbuf[rv], rbuf[rg], ALU.mult, [RB[rv], RB[rg]], [ACTB[gb][fi]])

            def ffn_down(gi, after_tile=None):
                gb = gi % 2
                nf = len(groups[gi])
                for t in range(NT):
                    if after_tile is not None and t >= 1:
                        after_tile(t - 1)
                    for half in range(2):
                        pi = pring.nxt()
                        for fi in range(nf):
                            MM(pbank[pi][:], actT[gb][:, fi, t * 128:(t + 1) * 128], wdn[gb][:, fi, half * 512:(half + 1) * 512],
                               fi == 0, fi == nf - 1, [ACTB[gb][fi], WDN[gb][fi]], [PB[pi]])
                        TT("dve", X1[:, t, half * 512:(half + 1) * 512], X1[:, t, half * 512:(half + 1) * 512], pbank[pi][:],
                           ALU.add, [X1B[t], PB[pi]], [X1B[t]])

            for gi in range(6):
                ffn_up(gi)
                if gi >= 1:
                    ffn_down(gi - 1)
            nfb = ARAF[:, 0:1024]
            b_nfb = Buf("nfb")
            S.alias([b_nfb], H2)
            S.dma("sp", nfb, nfr_d.partition_broadcast(128), writes=[b_nfb])
            outt = [ARAF[:, 1024 + i * 1024:1024 + (i + 1) * 1024] for i in range(2)]
            OT = [Buf("ot%d" % i) for i in range(2)]
            S.alias(OT, H2)

            def final_tile(t):
                sc = 36 + t
                oi = t % 2
                ACT(outt[oi], X1[:, t, :], AF.Square, [X1B[t], SSQ[sc]], [SSQ[sc], OT[oi]], accum_out=ssq[:, sc:sc + 1])
                ACT(rstd[:, sc:sc + 1], ssq[:, sc:sc + 1], AF.Sqrt, [SSQ[sc], b_negpi], [RSTD[sc]], scale=1.0 / D,
                    bias=epsc[:, 0:1])
                RCP(rstd[:, sc:sc + 1], rstd[:, sc:sc + 1], [RSTD[sc]], [RSTD[sc]])
                STT("dve", outt[oi], X1[:, t, :], rstd[:, sc:sc + 1], nfb, ALU.mult, ALU.mult, [X1B[t], RSTD[sc], b_nfb], [OT[oi]])
                final_ops.append(S.dma("sp", y_d[t * 128:(t + 1) * 128, :], outt[oi], reads=[OT[oi]]))
            ffn_down(5, after_tile=final_tile)
            final_tile(NT - 1)

            CK("ffn")
        except _Stop:
            pass
        S.emit(final_ops=final_ops)
    return nc


def _prep_shared(inp):
    f = np.float32
    sh = {}
    sh["w_mod"] = np.ascontiguousarray(inp["w_mod"][0], f)
    b_mod = np.asarray(inp["b_mod"][0], f)
    sh["bmod_col"] = np.ascontiguousarray(b_mod.reshape(48, 128).T)
    sh["bmod_row"] = np.ascontiguousarray(b_mod.reshape(1, 6144))
    sh["n1c"] = np.ascontiguousarray(np.asarray(inp["norm1"][0], f).reshape(8, 128).T)
    sh["n2c"] = np.ascontiguousarray(np.asarray(inp["norm2"][0], f).reshape(8, 128).T)
    sh["nf_row"] = np.ascontiguousarray(np.asarray(inp["norm_f"], f).reshape(1, D))
    w_in = np.asarray(inp["w_in"][0], f)
    def swapped(c0):
        blk = w_in[:, c0:c0 + 256].reshape(D, 4, 2, 32)
        return blk[:, :, ::-1, :].reshape(D, 256)
    sh["w_in"] = np.ascontiguousarray(np.concatenate([w_in, swapped(1536), swapped(1792)], axis=1))
    hw = np.asarray(inp["hy_conv_w"][0], f)
    hb = np.asarray(inp["hy_conv_b"][0], f)
    hc = np.concatenate([hw, hb[None]], axis=0)
    sh["hcw"] = np.ascontiguousarray(hc.reshape(4, 12, 128).transpose(2, 1, 0).reshape(128, 48))
    fw = np.asarray(inp["ffn_conv_w"][0], f)
    fb = np.asarray(inp["ffn_conv_b"][0], f)
    fc = np.concatenate([fw, fb[None]], axis=0)
    sh["fcw"] = np.ascontiguousarray(fc.reshape(4, 44, 128).transpose(2, 1, 0).reshape(128, 176))
    sh["hy_w1"] = np.ascontiguousarray(inp["hy_w1"][0], f)
    sh["hyp"] = np.ascontiguousarray(np.stack([inp["hy_b1"][0], inp["hy_f1"][0], inp["hy_b2"][0], inp["hy_f2"][0]],
                                              axis=1).astype(f))
    sh["hy_w2"] = np.ascontiguousarray(inp["hy_w2"][0], f)
    sh["hy_w3"] = np.ascontiguousarray(inp["hy_w3"][0], f)
    sh["hy_bias"] = np.ascontiguousarray(np.asarray(inp["hy_bias"][0], f).reshape(1, 1024))
    sh["rlog"] = np.ascontiguousarray(np.concatenate([inp["ret_logit_f"][0], inp["ret_logit_b"][0]]).astype(f).reshape(1, 8))
    sh["w_out"] = np.ascontiguousarray(inp["w_out"][0], f)
    sh["w_up"] = np.ascontiguousarray(inp["ffn_w_up"][0], f)
    sh["w_down"] = np.ascontiguousarray(inp["ffn_w_down"][0], f)
    hc_ = host_consts()
    for k in ("fd", "nyqc", "nyqr", "rope", "rc", "zfT", "absd", "tn"):
        sh[k] = hc_[k]
    return sh


_NC_CACHE = {}


def kernel(_dbg=(), _stop=None, _cores=None, **inputs):
    inp = {k: np.asarray(v) for k, v in inputs.items()}
    key = repr((_dbg, _stop))
    if key not in _NC_CACHE:
        _NC_CACHE[key] = build(_dbg, _stop)
    nc = _NC_CACHE[key]
    sh = _prep_shared(inp)
    x = np.asarray(inp["x"], np.float32)
    ctx = np.asarray(inp["ctx"], np.float32)
    c = np.asarray(inp["c"], np.float32)
    c_ctx = np.asarray(inp["c_ctx"], np.float32)
    n = x.shape[0] if _cores is None else _cores
    in_maps = []
    for b in range(n):
        m = dict(sh)
        m["x"] = np.ascontiguousarray(x[b])
        m["ctx"] = np.ascontiguousarray(ctx[b])
        cv = np.stack([c[b].reshape(8, 128).T, c_ctx.reshape(8, 128).T], axis=2)
        m["cvec"] = np.ascontiguousarray(cv.reshape(128, 16))
        in_maps.append(m)
    res = run_bass_kernel_spmd(nc, in_maps, core_ids=list(range(n)))
    out = np.stack([np.asarray(r["y"], np.float32) for r in res.results], axis=0)
    if _dbg:
        return out, res.results
    return out
```

```python
import math
from contextlib import ExitStack

import numpy as np
import ml_dtypes

import concourse.bass as bass
import concourse.mybir as mybir
from concourse.bass_utils import run_bass_kernel_spmd

F32 = mybir.dt.float32
BF16 = mybir.dt.bfloat16
I32 = mybir.dt.int32
AF = mybir.ActivationFunctionType
ALU = mybir.AluOpType

L = 2048
D = 1024
NT = 16
NFFT = 4096
EPS = 1e-6
DFF = 2816
NFC = 22
ENGS = ("pe", "act", "dve", "pool", "sp")
PI = math.pi


class Buf:
    __slots__ = ("name", "w", "r", "rd")

    def __init__(self, name=""):
        self.name = name
        self.w = None
        self.r = {}
        self.rd = []


class Op:
    __slots__ = ("eng", "fn", "deps", "signal", "count", "dma", "chan", "cval", "cprev", "raw")

    def __init__(self, eng, fn, dma):
        self.eng = eng
        self.fn = fn
        self.dma = dma
        self.deps = []
        self.signal = False
        self.count = 0
        self.chan = None
        self.cval = 0
        self.cprev = 0
        self.raw = set()


class Sched:
    def __init__(self, nc, nchan=20, self_wait=True):
        self.nc = nc
        self.ops = {e: [] for e in ENGS}
        self.nchan = nchan
        self.self_wait = self_wait

    def add(self, eng, fn, reads=(), writes=(), dma=False):
        op = Op(eng, fn, dma)
        deps = {}
        for b in reads:
            if b.w is not None:
                deps[id(b.w)] = b.w
                op.raw.add(id(b.w))
        for b in writes:
            if b.w is not None:
                deps[id(b.w)] = b.w
            for o in b.r.values():
                deps[id(o)] = o
            for o in b.rd:
                deps[id(o)] = o
        op.deps = list(deps.values())
        for d in op.deps:
            d.signal = True
        for b in reads:
            if dma:
                b.rd.append(op)
            else:
                b.r[eng] = op
        for b in writes:
            b.w = op
            b.r = {}
            b.rd = []
        if dma:
            op.signal = True
        self.ops[eng].append(op)
        return op

    def dma(self, eng, out, in_, reads=(), writes=()):
        return self.add(eng, lambda e: e.dma_start(out=out, in_=in_), reads, writes, dma=True)

    def alias(self, new_bufs, old_bufs):
        ops = {}
        for b in old_bufs:
            if b.w is not None:
                ops[id(b.w)] = b.w
            for o in b.r.values():
                ops[id(o)] = o
            for o in b.rd:
                ops[id(o)] = o
        lst = list(ops.values())
        for b in new_bufs:
            b.rd.extend(lst)

    def emit(self, final_ops=()):
        nc = self.nc
        with ExitStack() as st:
            esem = {e: st.enter_context(nc.semaphore("s_" + e)) for e in ENGS}
            csem = {}
            for e in ENGS:
                if any(o.dma for o in self.ops[e]):
                    csem[e] = [st.enter_context(nc.semaphore("c_%s_%d" % (e, i))) for i in range(self.nchan)]
            for e in ENGS:
                n = 0
                uses = [0] * self.nchan
                k = 0
                for o in self.ops[e]:
                    if o.dma:
                        c = k % self.nchan
                        k += 1
                        o.chan = csem[e][c]
                        o.cprev = 16 * uses[c]
                        uses[c] += 1
                        o.cval = 16 * uses[c]
                    elif o.signal:
                        n += 1
                        o.count = n
            self_wait = self.self_wait

            def run(e, eng):
                waited = {}
                for o in self.ops[e]:
                    need = {}
                    for d in o.deps:
                        if d.dma:
                            key, val = d.chan, d.cval
                        else:
                            if d.eng == e and (e == "pe" or not self_wait):
                                continue
                            if d.eng == e and id(d) not in o.raw:
                                continue
                            key, val = esem[d.eng], d.count
                        kk = id(key)
                        if kk not in need or need[kk][1] < val:
                            need[kk] = (key, val)
                    if o.dma and o.cprev > 0:
                        kk = id(o.chan)
                        if kk not in need or need[kk][1] < o.cprev:
                            need[kk] = (o.chan, o.cprev)
                    for kk, (key, val) in need.items():
                        if waited.get(kk, 0) >= val:
                            continue
                        eng.wait_ge(key, val)
                        waited[kk] = val
                    ins = o.fn(eng)
                    if o.dma:
                        ins.then_inc(o.chan, 16)
                    elif o.signal:
                        ins.then_inc(esem[e], 1)
                if e == "sp":
                    for d in final_ops:
                        if d.dma:
                            eng.wait_ge(d.chan, d.cval)
                        else:
                            eng.wait_ge(esem[d.eng], d.count)

            with nc.Block() as block:
                @block.tensor
                def _(eng):
                    run("pe", eng)

                @block.scalar
                def _(eng):
                    run("act", eng)

                @block.vector
                def _(eng):
                    run("dve", eng)

                @block.gpsimd
                def _(eng):
                    run("pool", eng)

                @block.sync
                def _(eng):
                    run("sp", eng)


class Ring:
    def __init__(self, n):
        self.n = n
        self.i = -1

    def nxt(self):
        self.i = (self.i + 1) % self.n
        return self.i


_CONSTS = None


def host_consts():
    global _CONSTS
    if _CONSTS is not None:
        return _CONSTS
    c = {}
    j = np.arange(16)[:, None, None, None]
    p = np.arange(128)[None, :, None, None]
    i = np.arange(16)[None, None, :, None]
    r = np.arange(128)[None, None, None, :]
    prod = ((128 * i + p) * (128 * j + r)) % NFFT
    ang = 2.0 * np.pi * prod.astype(np.float64) / NFFT
    fd = np.stack([np.cos(ang), -np.sin(ang)], axis=3)
    c["fd"] = fd.reshape(16, 128, 4096).astype(ml_dtypes.bfloat16)
    sgn = (1.0 - 2.0 * (np.arange(128) % 2)).astype(np.float32)
    c["nyqc"] = sgn.reshape(128, 1).astype(ml_dtypes.bfloat16)
    c["nyqr"] = sgn.reshape(1, 128).astype(ml_dtypes.bfloat16)
    t = np.arange(L)
    row = (t // 64).astype(np.float32)
    col = (t % 64).astype(np.float32)
    inv_freq = (10000.0 ** (-np.arange(16, dtype=np.float32) / 16)).astype(np.float32)
    ang = np.concatenate([row[:, None] * inv_freq, col[:, None] * inv_freq], axis=-1).astype(np.float32)
    cs, sn = np.cos(ang), np.sin(ang)
    cos64 = np.concatenate([cs, cs], axis=1).T
    sin64 = np.concatenate([-sn, sn], axis=1).T
    c["rope"] = np.concatenate([np.tile(cos64, (2, 1)), np.tile(sin64, (2, 1))], axis=1).astype(np.float32)
    rc = np.zeros((128, 776), np.float32)
    jj = np.arange(128)[:, None].astype(np.float32)
    ii = np.arange(128)[None, :].astype(np.float32)
    rc[:, 0:128] = np.maximum(ii - jj, 0)
    rc[:, 128:256] = np.maximum(jj - ii, 0)
    rc[:, 256:384] = (ii >= jj)
    rc[:, 384:512] = (jj > ii)
    rc[:, 512:640] = ii + 1.0
    rc[:, 640:768] = 128.0 - ii
    pp = np.arange(128, dtype=np.float32)
    rc[:, 768] = 127.0 - pp
    rc[:, 769] = pp
    rc[:, 770] = 255.0 - pp
    rc[:, 771] = 127.0 - pp
    rc[:, 772] = pp
    rc[:, 773] = 128.0 + pp
    rc[:, 774] = 1.0
    rc[:, 775] = -PI
    c["rc"] = rc
    pos = np.arange(L, dtype=np.float32)[:, None]
    tt = np.linspace(0.0, 1.0, L, dtype=np.float32)[:, None]
    bands = np.linspace(1e-4, 15, 16, dtype=np.float32)[None, :]
    a2 = (2.0 * np.float32(math.pi) * pos * bands / L).astype(np.float32)
    z = np.concatenate([tt, np.cos(a2), -np.sin(a2)], axis=-1).astype(np.float32)
    c["zfT"] = np.ascontiguousarray(z.T)
    max_decay = math.log(1e-2) / 0.3
    min_decay = math.log(1e-2) / 1.5
    deltas = np.linspace(min_decay, max_decay, 512, dtype=np.float32)
    c["absd"] = np.abs(deltas).reshape(1, 512).astype(np.float32)
    tn = -tt[:, 0]
    c["tn"] = np.ascontiguousarray(tn.reshape(16, 128).T).astype(np.float32)
    _CONSTS = c
    return c


class _Stop(Exception):
    pass


def build(dbg=(), stop_after=None):
    nc = bass.Bass("TRN2", target_bir_lowering=False)

    def CK(name):
        if stop_after == name:
            raise _Stop()

    def din(name, shape, dt=F32):
        return nc.dram_tensor(name, list(shape), dt, kind="ExternalInput").ap()

    x_d = din("x", [L, D])
    ctx_d = din("ctx", [256, D])
    cvec_d = din("cvec", [128, 16])
    wmod_d = din("w_mod", [D, 6144])
    bmodc_d = din("bmod_col", [128, 48])
    bmodr_d = din("bmod_row", [1, 6144])
    n1c_d = din("n1c", [128, 8])
    n2c_d = din("n2c", [128, 8])
    nfr_d = din("nf_row", [1, D])
    win_d = din("w_in", [D, 3584])
    hcw_d = din("hcw", [128, 48])
    fcw_d = din("fcw", [128, 176])
    hw1_d = din("hy_w1", [33, 64])
    hyp_d = din("hyp", [64, 4])
    hw2_d = din("hy_w2", [64, 64])
    hw3_d = din("hy_w3", [64, 2048])
    hbias_d = din("hy_bias", [1, 1024])
    rlog_d = din("rlog", [1, 8])
    wout_d = din("w_out", [D, D])
    wup_d = din("w_up", [D, 2 * DFF])
    wdn_d = din("w_down", [DFF, D])
    fd_d = din("fd", [16, 128, 4096], BF16)
    nyqc_d = din("nyqc", [128, 1], BF16)
    nyqr_d = din("nyqr", [1, 128], BF16)
    rope_d = din("rope", [128, 2 * L])
    rc_d = din("rc", [128, 776])
    zfT_d = din("zfT", [33, L])
    absd_d = din("absd", [1, 512])
    tn_d = din("tn", [128, 16])
    y_d = nc.dram_tensor("y", [L, D], F32, kind="ExternalOutput").ap()
    dbg_d = {}
    for name, shape, dts in dbg:
        dbg_d[name] = nc.dram_tensor("dbg_" + name, list(shape), BF16 if dts == "bf16" else F32,
                                     kind="ExternalOutput").ap()

    st = ExitStack()
    with st:
        def sb(name, shape, dt):
            return st.enter_context(nc.sbuf_tensor("s_" + name, list(shape), dt))

        def ps(name, shape, dt):
            return st.enter_context(nc.psum_tensor(name, list(shape), dt))

        S = Sched(nc)
        final_ops = []

        BIG = sb("big", [128, 103680], BF16)
        BIGF = BIG[:].bitcast(F32)
        BIGI = BIG[:].bitcast(I32)
        XB = BIG[:, 0:32768]
        ARX = BIGF[:, 0:16384]
        ARA = BIG[:, 32768:49152]
        ARAF = BIGF[:, 16384:24576]
        ARB = BIG[:, 49152:73728]
        ARBF = BIGF[:, 24576:36864]
        ARE = BIG[:, 73728:103680]
        EF = BIGF[:, 36864:51840]

        ident = sb("ident", [128, 128], BF16)
        ones_bf = sb("ones_bf", [128, 128], BF16)
        cvf = sb("cvf", [128, 16], F32)
        sT = sb("sT", [128, 16], BF16)
        srep = sb("srep", [128, 8, 128], BF16)
        modcol = sb("modcol", [128, 32, 2], F32)
        bmodc = sb("bmodc", [128, 48], F32)
        n1c = sb("n1c", [128, 8], F32)
        n2c = sb("n2c", [128, 8], F32)
        AB = sb("AB", [128, 6, 8], F32)
        hcw = sb("hcw", [128, 12, 4], F32)
        fcw = sb("fcw", [128, 44, 4], F32)
        ssq = sb("ssq", [128, 64], F32)
        rstd = sb("rstd", [128, 64], F32)
        hid2T = ARE[0:64, 16384:18432]
        w3b = ARE[0:64, 18432:20480]
        nyqc = sb("nyqc", [128, 1], BF16)
        nyqr = sb("nyqr", [1, 128], BF16)
        tn = sb("tn", [128, 16], F32)
        negpi = sb("negpi", [128, 1], F32)
        epsc = sb("epsc", [128, 1], F32)

        pbank = [ps("pb%d" % i, [128, 512], F32) for i in range(6)]
        PB = [Buf("pb%d" % i) for i in range(6)]
        ptr = [ps("pt%d" % i, [128, 1024], BF16) for i in range(2)]
        PT = [Buf("pt%d" % i) for i in range(2)]
        pring = Ring(6)
        tring = Ring(2)

        def MM(out, lhsT, rhs, start, stop, reads, writes):
            S.add("pe", lambda e: e.matmul(out, lhsT=lhsT, rhs=rhs, start=start, stop=stop), reads, writes)

        def TR(out, in_, idn, reads, writes):
            S.add("pe", lambda e: e.transpose(out, in_, idn), reads, writes)

        def ACT(out, in_, func, reads, writes, **kw):
            S.add("act", lambda e: e.activation(out=out, in_=in_, func=func, **kw), reads, writes)

        def TT(eng, out, in0, in1, op, reads, writes):
            S.add(eng, lambda e: e.tensor_tensor(out=out, in0=in0, in1=in1, op=op), reads, writes)

        def TS(eng, out, in0, s1, s2, op0, op1, reads, writes):
            if s2 is None:
                S.add(eng, lambda e: e.tensor_scalar(out=out, in0=in0, scalar1=s1, scalar2=None, op0=op0), reads, writes)
            else:
                S.add(eng, lambda e: e.tensor_scalar(out=out, in0=in0, scalar1=s1, scalar2=s2, op0=op0, op1=op1),
                      reads, writes)

        def STT(eng, out, in0, scalar, in1, op0, op1, reads, writes):
            S.add(eng, lambda e: e.scalar_tensor_tensor(out=out, in0=in0, scalar=scalar, in1=in1, op0=op0, op1=op1),
                  reads, writes)

        def CP(eng, out, in_, reads, writes):
            if eng == "act":
                S.add("act", lambda e: e.activation(out=out, in_=in_, func=AF.Copy), reads, writes)
            else:
                S.add(eng, lambda e: e.tensor_copy(out=out, in_=in_), reads, writes)

        def RCP(out, in_, reads, writes):
            S.add("dve", lambda e: e.reciprocal(out=out, in_=in_), reads, writes)

        def MEMSET(eng, out, val, writes):
            S.add(eng, lambda e: e.memset(out, val), (), writes)

        def DBG(name, src_ap, reads, rows=None):
            if name in dbg_d:
                final_ops.append(S.dma("sp", dbg_d[name], src_ap, reads=reads))

        def bc(ap, shape):
            return ap.to_broadcast(list(shape))

        try:
            b_ident = Buf("ident")
            MEMSET("pool", ident[:], 0.0, [b_ident])
            S.add("pool", lambda e: e.affine_select(out=ident[:], in_=ident[:], pattern=[[-1, 128]],
                                                    compare_op=ALU.not_equal, fill=1.0, base=0, channel_multiplier=1),
                  [b_ident], [b_ident])
            b_ones = Buf("ones")
            MEMSET("dve", ones_bf[:], 1.0, [b_ones])
            b_negpi = Buf("negpi")
            MEMSET("dve", negpi[:], -PI, [b_negpi])
            MEMSET("dve", epsc[:], EPS, [b_negpi])
            SSQ = [Buf("ssq%d" % i) for i in range(64)]
            RSTD = [Buf("rstd%d" % i) for i in range(64)]
            MEMSET("dve", ssq[:], 0.0, SSQ)

            b_zf = Buf("zfT")
            b_hid1 = Buf("hid1")
            b_hid2 = Buf("hid2")
            b_ft = Buf("ftmp0")
            CK("c0")
            b_small = {k: Buf(k) for k in ["cvf", "sT", "srep", "bmodc", "n1c", "n2c", "hcw", "fcw", "nyq", "tn", "w3b"]}
            S.dma("sp", cvf[:], cvec_d, writes=[b_small["cvf"]])
            S.dma("sp", bmodc[:], bmodc_d, writes=[b_small["bmodc"]])
            S.dma("sp", n1c[:], n1c_d, writes=[b_small["n1c"]])
            S.dma("sp", n2c[:], n2c_d, writes=[b_small["n2c"]])
            S.dma("sp", hcw[:].rearrange("p a b -> p (a b)"), hcw_d, writes=[b_small["hcw"]])
            S.dma("sp", fcw[:].rearrange("p a b -> p (a b)"), fcw_d, writes=[b_small["fcw"]])
            S.dma("sp", nyqc[:], nyqc_d, writes=[b_small["nyq"]])
            S.dma("sp", nyqr[:], nyqr_d, writes=[b_small["nyq"]])
            S.dma("sp", tn[:], tn_d, writes=[b_small["tn"]])
            ACT(sT[:], cvf[:], AF.Silu, [b_small["cvf"]], [b_small["sT"]])
            sT3 = sT[:].rearrange("p (i t) -> p i t", t=2)
            CP("dve", srep[:], bc(sT3[:, :, 0:1], [128, 8, 128]), [b_small["sT"]], [b_small["srep"]])

            R0 = 8192
            rc = EF[:, R0 + 0:R0 + 776]
            DTm = EF[:, R0 + 776:R0 + 1288].rearrange("p (h i) -> p h i", h=4)
            rowqm = ARE[:, 2 * (R0 + 1288):2 * (R0 + 2056)].rearrange("p (h d i) -> p h d i", h=4, d=3)
            R1 = R0 + 256
            rl = EF[:, R1 + 1800:R1 + 1808]
            lg = EF[:, R1 + 1808:R1 + 1816]
            lgcol = EF[:, R1 + 1816:R1 + 1820].rearrange("p (c d) -> p c d", c=2)
            dec = EF[:, R1 + 1820:R1 + 1824].rearrange("p (c d) -> p c d", c=2)
            colfb = EF[:, R1 + 1824:R1 + 1832].rearrange("p (d h) -> p d h", d=2)
            wcx = EF[:, R1 + 1832:R1 + 1848].rearrange("p (t d h) -> p t d h", t=2, d=2)
            etmp = EF[:, R1 + 1848:R1 + 1976]
            etmp2 = EF[:, R1 + 1976:R1 + 2104]
            b_rc = Buf("rc")
            b_lg = Buf("lg")
            b_DT = Buf("DT")
            b_et = Buf("etmp")
            b_rtab = Buf("rtab")
            S.dma("sp", rc, rc_d, writes=[b_rc])
            S.dma("sp", rl, rlog_d.partition_broadcast(128), writes=[b_lg])
            ACT(lg, rl, AF.Exp, [b_lg], [b_lg], scale=-1.0)
            ACT(lg, lg, AF.Ln, [b_lg, b_rc], [b_lg], bias=rc[:, 774:775])
            TS("dve", lg, lg, -1.0, None, ALU.mult, None, [b_lg], [b_lg])
            for d in range(2):
                for half in range(2):
                    P = slice(half * 64, half * 64 + 64)
                    CP("dve", lgcol[P, :, d], lg[P, d * 4 + half:d * 4 + half + 3:2], [b_lg], [b_rtab])
            for h in range(4):
                ACT(etmp, rc[:, 0:128], AF.Exp, [b_rc, b_lg], [b_et], scale=lg[:, h:h + 1])
                TT("dve", DTm[:, h, :], etmp, rc[:, 256:384], ALU.mult, [b_et, b_rc], [b_DT])
                ACT(etmp2, rc[:, 128:256], AF.Exp, [b_rc, b_lg], [b_et], scale=lg[:, 4 + h:5 + h])
                TT("dve", etmp2, etmp2, rc[:, 384:512], ALU.mult, [b_et, b_rc], [b_et])
                TT("dve", DTm[:, h, :], DTm[:, h, :], etmp2, ALU.add, [b_et, b_DT], [b_DT])
                ACT(colfb[:, 0, h:h + 1], rc[:, 768:769], AF.Exp, [b_rc, b_lg], [b_rtab], scale=lg[:, h:h + 1])
                ACT(colfb[:, 1, h:h + 1], rc[:, 769:770], AF.Exp, [b_rc, b_lg], [b_rtab], scale=lg[:, 4 + h:5 + h])
                ACT(wcx[:, :, 0, h], rc[:, 770:772], AF.Exp, [b_rc, b_lg], [b_rtab], scale=lg[:, h:h + 1])
                ACT(wcx[:, :, 1, h], rc[:, 772:774], AF.Exp, [b_rc, b_lg], [b_rtab], scale=lg[:, 4 + h:5 + h])
            ACT(dec, lgcol, AF.Exp, [b_rtab], [b_rtab], scale=128.0)
            for c in range(2):
                if c == 0:
                    MEMSET("dve", rowqm.rearrange("p h d i -> p (h d i)"), 0.0, [b_rtab])
                for half in range(2):
                    hs = slice(half * 64, half * 64 + 64)
                    h_ = 2 * c + half
                    ACT(rowqm[hs, h_, 0, :], rc[hs, 512:640], AF.Exp, [b_rc, b_rtab], [b_rtab], scale=lgcol[hs, c, 0:1])
                    ACT(rowqm[hs, h_, 1, :], rc[hs, 640:768], AF.Exp, [b_rc, b_rtab], [b_rtab], scale=lgcol[hs, c, 1:2])
                    MEMSET("dve", rowqm[hs, h_, 2, :], 1.0, [b_rtab])

            CK("c1")
            wr = [XB[:, s * 4096:(s + 1) * 4096].rearrange("p (i n) -> p i n", i=8) for s in range(4)]
            WR = [Buf("wr%d" % s) for s in range(4)]
            wring = Ring(4)
            wmv = wmod_d.rearrange("(i p) n -> p i n", p=128)
            winv = win_d.rearrange("(i p) n -> p i n", p=128)

            def load_wblock(src):
                s = wring.nxt()
                S.dma("pool", wr[s], src, writes=[WR[s]])
                return s

            b_modcol = Buf("modcol")
            b_AB = Buf("AB")

            def mod_col_half(j, which, pi=None):
                s = load_wblock(wmv[:, :, j * 512:(j + 1) * 512])
                if pi is None:
                    pi = pring.nxt()
                for cc in range(4):
                    for k in range(8):
                        MM(pbank[pi][:, cc * 2:cc * 2 + 2], wr[s][:, k, cc * 128:(cc + 1) * 128],
                           sT3[:, k, :], k == 0, k == 7, [WR[s], b_small["sT"]], [PB[pi]])
                m0 = which * 8 + (j % 2) * 4
                TT("dve", modcol[:, m0:m0 + 4, :], pbank[pi][:, 0:8].rearrange("p (a b) -> p a b", b=2),
                   bc(bmodc[:, j * 4:j * 4 + 4].unsqueeze(2), [128, 4, 2]), ALU.add,
                   [PB[pi], b_small["bmodc"]], [b_modcol])

            def mod_bc_half(j, dst, dst_buf, tmp, tmp_buf, wbuf, wbuf_b):
                S.dma("pool", wbuf, wmv[:, :, j * 512:(j + 1) * 512], writes=[wbuf_b])
                S.dma("sp", tmp, bmodr_d[0:1, j * 512:(j + 1) * 512].partition_broadcast(128), writes=[tmp_buf])
                pi = pring.nxt()
                for k in range(8):
                    MM(pbank[pi][:], srep[:, k, :], wbuf[:, k, :], k == 0, k == 7, [wbuf_b, b_small["srep"]], [PB[pi]])
                h0 = (j % 2) * 512
                TT("dve", dst[:, h0:h0 + 512], pbank[pi][:], tmp, ALU.add, [PB[pi], tmp_buf], [dst_buf])

            mod_col_half(0, 0)
            mod_col_half(1, 0)
            mod_col_half(2, 1)
            mod_col_half(3, 1)
            STT("dve", AB[:, 0, :], modcol[:, 8:16, 0], 1.0, n1c[:], ALU.add, ALU.mult, [b_modcol, b_small["n1c"]], [b_AB])
            CP("dve", AB[:, 1, :], modcol[:, 0:8, 0], [b_modcol], [b_AB])
            STT("dve", AB[:, 2, :], modcol[:, 8:16, 1], 1.0, n1c[:], ALU.add, ALU.mult, [b_modcol, b_small["n1c"]], [b_AB])
            CP("dve", AB[:, 3, :], modcol[:, 0:8, 1], [b_modcol], [b_AB])

            CK("c2")
            xt = [ARX[:, 8192 + i * 1024:8192 + (i + 1) * 1024] for i in range(3)]
            XT = [Buf("xt%d" % i) for i in range(3)]
            xtring = Ring(3)
            xn = [XB[:, 22528 + i * 1024:22528 + (i + 1) * 1024] for i in range(2)]
            XN = [Buf("xn%d" % i) for i in range(2)]
            xnring = Ring(2)
            b_rstd = Buf("rstd")
            evac_flip = [0]

            def norm_s1(src, src_bufs, scol, xn_list, xn_bufs, xn_ring):
                xi = xn_ring.nxt()
                ACT(xn_list[xi], src, AF.Square, src_bufs + [SSQ[scol]], [SSQ[scol], xn_bufs[xi]],
                    accum_out=ssq[:, scol:scol + 1])
                ACT(rstd[:, scol:scol + 1], ssq[:, scol:scol + 1], AF.Sqrt, [SSQ[scol], b_negpi], [RSTD[scol]], scale=1.0 / D,
                    bias=epsc[:, 0:1])
                RCP(rstd[:, scol:scol + 1], rstd[:, scol:scol + 1], [RSTD[scol]], [RSTD[scol]])
                TS("dve", xn_list[xi], src, rstd[:, scol:scol + 1], None, ALU.mult, None, src_bufs + [RSTD[scol]],
                   [xn_bufs[xi]])
                return xi

            def norm_s2(xi, acol, bcol, dst_fn, dst_bufs, xn_list, xn_bufs):
                ta = tring.nxt()
                tb = tring.nxt()
                NA = 3
                for c in range(NA):
                    TR(ptr[ta][:, c * 128:(c + 1) * 128], xn_list[xi][:, c * 128:(c + 1) * 128], ident[:],
                       [xn_bufs[xi], b_ident], [PT[ta]])
                for c in range(NA, 8):
                    TR(ptr[tb][:, (c - NA) * 128:(c - NA + 1) * 128], xn_list[xi][:, c * 128:(c + 1) * 128], ident[:],
                       [xn_bufs[xi], b_ident], [PT[tb]])
                for c in range(8 - NA):
                    if c < NA:
                        ACT(dst_fn(c), ptr[ta][:, c * 128:(c + 1) * 128], AF.Identity, [PT[ta], b_AB], [dst_bufs[0]],
                            scale=AB[:, acol, c:c + 1], bias=AB[:, bcol, c:c + 1])
                    cc_ = c + NA
                    TS("dve", dst_fn(cc_), ptr[tb][:, c * 128:(c + 1) * 128], AB[:, acol, cc_:cc_ + 1],
                       AB[:, bcol, cc_:cc_ + 1], ALU.mult, ALU.add, [PT[tb], b_AB], [dst_bufs[1]])

            hT = ARA[:].rearrange("p (c t) -> p c t", c=8)
            HT = [Buf("hT%d" % i) for i in range(2 * NT)]
            hcT = ARB[:, 8192:10240].rearrange("p (c t) -> p c t", c=8)
            HCT = [Buf("hcT%d" % i) for i in range(4)]
            def p1_s1(i):
                xi = xtring.nxt()
                src = x_d[i * 128:(i + 1) * 128, :] if i < NT else ctx_d[(i - NT) * 128:(i - NT + 1) * 128, :]
                S.dma("sp", xt[xi], src, writes=[XT[xi]])
                return norm_s1(xt[xi], [XT[xi]], i, xn, XN, xnring)

            def p1_s2(i, xi):
                if i < NT:
                    norm_s2(xi, 0, 1, lambda c: hT[:, c, i * 128:(i + 1) * 128], HT[2 * i:2 * i + 2], xn, XN)
                else:
                    j = i - NT
                    norm_s2(xi, 2, 3, lambda c: hcT[:, c, j * 128:(j + 1) * 128], HCT[2 * j:2 * j + 2], xn, XN)
            xi_cur = p1_s1(0)
            for i in range(NT + 2):
                xi_nxt = p1_s1(i + 1) if i + 1 < NT + 2 else None
                p1_s2(i, xi_cur)
                xi_cur = xi_nxt
            if "hT" in dbg_d:
                for c in range(8):
                    final_ops.append(S.dma("sp", dbg_d["hT"][c * 128:(c + 1) * 128, :], hT[:, c, :], reads=HT))

            CK("p1")
            rope = ARBF[:, 0:4096]
            b_rope = Buf("rope")
            S.dma("sp", rope, rope_d, writes=[b_rope])
            qrot = ARE[:, 0:4096].rearrange("p (c t) -> p c t", c=2)
            krot = ARE[:, 4096:8192].rearrange("p (c t) -> p c t", c=2)
            vtok = ARE[:, 8192:16384].rearrange("p (i n) -> p i n", i=NT)
            QR = [[Buf("qr%d_%d" % (c, q)) for q in range(4)] for c in range(2)]
            KR = [[Buf("kr%d_%d" % (c, q)) for q in range(4)] for c in range(2)]
            VT = [Buf("vt%d" % i) for i in range(NT)]
            tA = [ARX[:, 8192 + i * 512:8192 + (i + 1) * 512] for i in range(4)]
            TA = [Buf("tA%d" % i) for i in range(4)]
            S.alias(TA, XT)
            taring = Ring(2)

            def proj_fm(s, col0, q, pi):
                for k in range(8):
                    MM(pbank[pi][:], wr[s][:, k, col0:col0 + 128], hT[:, k, q * 512:(q + 1) * 512], k == 0, k == 7,
                       [WR[s]] + HT[q * 8:q * 8 + 8], [PB[pi]])

            s_qk = load_wblock(winv[:, :, 1536:2048])
            s_qks = load_wblock(winv[:, :, 3072:3584])
            s_v = load_wblock(winv[:, :, 2048:2560])
            for cc in range(4):
                isk = cc >= 2
                dst = krot if isk else qrot
                dbuf = KR if isk else QR
                for q in range(4):
                    p1 = pring.nxt()
                    proj_fm(s_qk, cc * 128, q, p1)
                    p2 = pring.nxt()
                    proj_fm(s_qks, cc * 128, q, p2)
                    ta = taring.nxt()
                    cosv = rope[:, q * 512:(q + 1) * 512]
                    sinv = rope[:, L + q * 512:L + (q + 1) * 512]
                    if isk:
                        STT("dve", tA[2 * ta], pbank[p1][:], 0.125, cosv, ALU.mult, ALU.mult, [PB[p1], b_rope], [TA[2 * ta]])
                        STT("dve", tA[2 * ta + 1], pbank[p2][:], 0.125, sinv, ALU.mult, ALU.mult, [PB[p2], b_rope],
                            [TA[2 * ta + 1]])
                    else:
                        TT("dve", tA[2 * ta], pbank[p1][:], cosv, ALU.mult, [PB[p1], b_rope], [TA[2 * ta]])
                        TT("dve", tA[2 * ta + 1], pbank[p2][:], sinv, ALU.mult, [PB[p2], b_rope], [TA[2 * ta + 1]])
                    TT("dve", dst[:, cc % 2, q * 512:(q + 1) * 512], tA[2 * ta], tA[2 * ta + 1], ALU.add,
                       [TA[2 * ta], TA[2 * ta + 1]], [dbuf[cc % 2][q]])
            for i in range(NT):
                pi = pring.nxt()
                for k in range(8):
                    MM(pbank[pi][:], hT[:, k, i * 128:(i + 1) * 128], wr[s_v][:, k, :], k == 0, k == 7,
                       [WR[s_v]] + HT[2 * i:2 * i + 2], [PB[pi]])
                CP("act", vtok[:, i, :], pbank[pi][:], [PB[pi]], [VT[i]])
            gT = ARB[:, 10240:18432].rearrange("p (c t) -> p c t", c=4)
            GTB = [[Buf("gT%d_%d" % (c, q)) for q in range(4)] for c in range(4)]
            s_g = load_wblock(winv[:, :, 2560:3072])

            def gproj_job(j):
                cg, q = j // 4, j % 4
                pi = pring.nxt()
                proj_fm(s_g, cg * 128, q, pi)
                ACT(gT[:, cg, q * 512:(q + 1) * 512], pbank[pi][:], AF.Silu, [PB[pi]], [GTB[cg][q]])

            if "qk" in dbg_d:
                for c in range(2):
                    for (nm, src, bb) in (("q", qrot, QR), ("k", krot, KR)):
                        r0 = (0 if nm == "q" else 256) + c * 128
                        final_ops.append(S.dma("sp", dbg_d["qk"][r0:r0 + 128, :], src[:, c, :], reads=bb[c]))

            CK("p2")
            XU = 8192
            Sallb = XB[:, 2 * XU:2 * XU + 4096].rearrange("p (n c e) -> p n c e", n=NT, c=2)
            o = XU + 2048
            Pm = [XB[:, 2 * (o + i * 256):2 * (o + i * 256) + 512].rearrange("p (h i) -> p h i", h=4) for i in range(2)]
            o += 512
            sq = [XB[:, 2 * (o + i * 256):2 * (o + i * 256) + 512] for i in range(2)]
            o += 512
            sg = [XB[:, 2 * (o + i * 256):2 * (o + i * 256) + 512].rearrange("p (h i) -> p h i", h=4) for i in range(2)]
            o += 512
            rsb = ARX[:, o:o + 512]
            o += 512
            tyb = ARX[:, o:o + 512].rearrange("p (h i) -> p h i", h=4)
            o += 512
            S32 = [ARX[:, o + d * 256:o + (d + 1) * 256].rearrange("p (c e) -> p c e", c=2) for d in range(2)]
            o += 512
            tmpS = ARX[:, o:o + 256].rearrange("p (c e) -> p c e", c=2)
            o += 256
            Sfb = [XB[:, 2 * (o + i * 128):2 * (o + i * 128) + 256].rearrange("p (c e) -> p c e", c=2) for i in range(2)]
            o += 256
            ktile = [XB[:, 2 * (o + i * 128):2 * (o + i * 128) + 256] for i in range(2)]
            o += 256
            qfb0 = XB[:, 2 * o:2 * o + 1536]
            o += 768
            qfb1 = XB[:, 2 * o:2 * o + 1536]
            qfb = [q_.rearrange("p (h d i) -> p h d i", h=4, d=3) for q_ in (qfb0, qfb1)]
            kct = XB[:, 2 * o:2 * o + 1024].rearrange("p (t d n) -> p t d n", t=2, d=2)
            o += 512
            vct = XB[:, 2 * o:2 * o + 1024].rearrange("p (t n) -> p t n", t=2)
            o += 512
            sg.append(XB[:, 2 * o:2 * o + 512].rearrange("p (h i) -> p h i", h=4))
            o += 256
            assert o <= 16384, o
            ret_bufs = {k: Buf(k) for k in ["Sallb", "rsb", "tyb", "tmpS", "kct", "vct"]}
            PM = [Buf("pm%d" % i) for i in range(2)]
            SQ = [Buf("sq%d" % i) for i in range(2)]
            SG = [Buf("sg%d" % i) for i in range(3)]
            S32B = [Buf("s32_%d" % i) for i in range(2)]
            SFB = [Buf("sfb%d" % i) for i in range(2)]
            KTL = [Buf("kt%d" % i) for i in range(2)]
            QFB = [Buf("qfb%d" % i) for i in range(2)]
            SALLB = [Buf("sallb%d" % i) for i in range(NT)]
            all_ret = list(ret_bufs.values()) + PM + SQ + SG + S32B + SFB + KTL + QFB + SALLB
            S.alias(all_ret, XT + XN + TA)

            CK("rconst")
            for t in range(2):
                pk = pring.nxt()
                for k in range(8):
                    MM(pbank[pk][:, 0:256], hcT[:, k, t * 128:(t + 1) * 128], wr[s_qk][:, k, 256:512], k == 0, k == 7,
                       [WR[s_qk]] + HCT[2 * t:2 * t + 2], [PB[pk]])
                pv = pring.nxt()
                for k in range(8):
                    MM(pbank[pv][:], hcT[:, k, t * 128:(t + 1) * 128], wr[s_v][:, k, :], k == 0, k == 7,
                       [WR[s_v]] + HCT[2 * t:2 * t + 2], [PB[pv]])
                for d in range(2):
                    STT("dve", kct[:, t, d, :].rearrange("p (h e) -> p h e", h=4),
                        pbank[pk][:, 0:256].rearrange("p (h e) -> p h e", h=4), 0.125,
                        bc(wcx[:, t, d, :].unsqueeze(2), [128, 4, 64]), ALU.mult, ALU.mult,
                        [PB[pk], b_rtab], [ret_bufs["kct"]])
                CP("act", vct[:, t, :], pbank[pv][:], [PB[pv]], [ret_bufs["vct"]])
            for d in range(2):
                pi = pring.nxt()
                for h in range(4):
                    c = h // 2
                    for t in range(2):
                        MM(pbank[pi][:, h * 128:(h + 1) * 128], kct[:, t, d, c * 128:(c + 1) * 128],
                           vct[:, t, h * 128:(h + 1) * 128], t == 0, t == 1,
                           [ret_bufs["kct"], ret_bufs["vct"]], [PB[pi]])
                pv4 = pbank[pi][:].rearrange("p (s e) -> p s e", s=4)
                CP("dve", S32[d][0:64, :, :], pv4[0:64, 0:4:2, :], [PB[pi]], [S32B[d]])
                CP("dve", S32[d][64:128, :, :], pv4[64:128, 1:4:2, :], [PB[pi]], [S32B[d]])
            DBG("s_f", S32[0], [S32B[0]])
            DBG("s_b", S32[1], [S32B[1]])

            CK("ctx")
            Sallf = ARB[:, 0:4096].rearrange("p (n c e) -> p n c e", n=NT, c=2)
            B_SF = Buf("sallf")
            S.alias([B_SF], [b_rope])
            CP("act", Sallb[:, 15, :, :], S32[1], [S32B[1]], [SALLB[15]])
            CP("act", Sallf[:, 0, :, :], S32[0], [S32B[0]], [B_SF])
            kt4 = [ktile[0], ktile[1], sg[0].rearrange("p h i -> p (h i)")[:, 0:256], sg[1].rearrange("p h i -> p (h i)")[:, 0:256]]
            KT4 = [KTL[0], KTL[1], SG[0], SG[1]]
            tmpS2 = [tmpS, sq[0].bitcast(F32).rearrange("p (c e) -> p c e", c=2)]
            TMPS2 = [ret_bufs["tmpS"], SQ[0]]
            ktring = Ring(4)

            Dm = [[sg[2].rearrange("p h i -> p (h i)")[:, (d * 2 + c) * 128:(d * 2 + c + 1) * 128] for c in range(2)]
                  for d in range(2)]
            b_Dm = SG[2]
            for d in range(2):
                for c in range(2):
                    TS("dve", Dm[d][c], ident[:], dec[:, c, d:d + 1], None, ALU.mult, None, [b_ident, b_rtab], [b_Dm])

            def scan_step(n, d, s_prev, prev_bufs, s_new, new_bufs):
                ti = tring.nxt()
                for c in range(2):
                    TR(ptr[ti][:, c * 128:(c + 1) * 128], krot[:, c, n * 128:(n + 1) * 128], ident[:],
                       [KR[c][n // 4], b_ident], [PT[ti]])
                ki = ktring.nxt()
                TT("dve", kt4[ki].rearrange("p (h e) -> p h e", h=4),
                   ptr[ti][:, 0:256].rearrange("p (h e) -> p h e", h=4),
                   bc(colfb[:, d, :].unsqueeze(2), [128, 4, 64]), ALU.mult, [PT[ti], b_rtab], [KT4[ki]])
                pi = pring.nxt()
                for h in range(4):
                    c = h // 2
                    MM(pbank[pi][:, h * 128:(h + 1) * 128], kt4[ki][:, c * 128:(c + 1) * 128],
                       vtok[:, n, h * 128:(h + 1) * 128], True, False, [KT4[ki], VT[n]], [PB[pi]])
                    MM(pbank[pi][:, h * 128:(h + 1) * 128], Dm[d][c], s_prev[:, c, :], False, True,
                       [b_Dm] + prev_bufs, [PB[pi]])
                pv4 = pbank[pi][:].rearrange("p (s e) -> p s e", s=4)
                CP("act", s_new[0:64, :, :], pv4[0:64, 0:4:2, :], [PB[pi]], new_bufs)
                CP("act", s_new[64:128, :, :], pv4[64:128, 1:4:2, :], [PB[pi]], new_bufs)

            gproj_job(0)
            for s_ in range(15):
                scan_step(s_, 0, Sallf[:, s_, :, :], [B_SF], Sallf[:, s_ + 1, :, :], [B_SF])
                scan_step(15 - s_, 1, Sallb[:, 15 - s_, :, :], [SALLB[15 - s_]], Sallb[:, 14 - s_, :, :], [SALLB[14 - s_]])
                gproj_job(s_ + 1)

            S.alias([QFB[1]], [ret_bufs["kct"], ret_bufs["vct"]])
            CK("bwd")
            mixR = ARE[:, 21504:29696].rearrange("p (c t) -> p c t", c=4)
            mixHy = ARA[:, 8192:16384].rearrange("p (c t) -> p c t", c=4)

            def mix_chunk(k):
                return mixHy[:, k, :] if k < 4 else mixR[:, k - 4, :]
            MIXR = [Buf("mixr%d" % i) for i in range(NT)]
            stt = {}

            def O1(n):
                tsl = slice(n * 128, (n + 1) * 128)
                qbuf = [QR[0][n // 4], QR[1][n // 4]]
                kbuf = [KR[0][n // 4], KR[1][n // 4]]
                qi = n % 2
                pS = n % 2
                TT("dve", qfb[qi].rearrange("p (c hh) d i -> p c (hh d) i", c=2),
                   bc(qrot[:, :, tsl].unsqueeze(2), [128, 2, 6, 128]),
                   rowqm.rearrange("p (c hh) d i -> p c (hh d) i", c=2), ALU.mult, qbuf + [b_rtab], [QFB[qi]])
                for h in range(4):
                    MM(pbank[pS][:, h * 128:(h + 1) * 128], krot[:, h // 2, tsl], qfb[qi][:, h, 2, :], True, True,
                       kbuf + [QFB[qi]], [PB[pS]])

            def O2(n):
                pS = n % 2
                pmi = n % 2
                TT("dve", Pm[pmi], pbank[pS][:].rearrange("p (h i) -> p h i", h=4), DTm, ALU.mult,
                   [PB[pS], b_DT], [PM[pmi]])

            def O3(n):
                qi = n % 2
                pmi = n % 2
                pO = 2 + (n % 2)
                for h in range(4):
                    c = h // 2
                    osl = pbank[pO][:, h * 128:(h + 1) * 128]
                    MM(osl, vtok[:, n, h * 128:(h + 1) * 128], Pm[pmi][:, h, :], True, False, [VT[n], PM[pmi]], [PB[pO]])
                    MM(osl, Sallf[:, n, c, :], qfb[qi][:, h, 0, :], False, False, [B_SF, QFB[qi]], [PB[pO]])
                    MM(osl, Sallb[:, n, c, :], qfb[qi][:, h, 1, :], False, True, [SALLB[n], QFB[qi]], [PB[pO]])

            def O4a(n):
                pO = 2 + (n % 2)
                sqi = n % 2
                ACT(sq[sqi], pbank[pO][:], AF.Square, [PB[pO]], [SQ[sqi]])
                pR = 4
                MM(pbank[pR][:], ones_bf[:], sq[sqi], True, True, [b_ones, SQ[sqi]], [PB[pR]])
                ACT(rsb, pbank[pR][:], AF.Ln, [PB[pR], b_negpi], [ret_bufs["rsb"]], scale=1.0 / 128, bias=epsc[:, 0:1])
                ACT(rsb, rsb, AF.Exp, [ret_bufs["rsb"]], [ret_bufs["rsb"]], scale=-0.5)

            def O4b(n):
                tsl = slice(n * 128, (n + 1) * 128)
                pO = 2 + (n % 2)
                TT("dve", tyb, gT[:, :, tsl], rsb.rearrange("p (h i) -> p h i", h=4), ALU.mult,
                   [ret_bufs["rsb"]] + [GTB[c_][n // 4] for c_ in range(4)], [ret_bufs["tyb"]])
                TT("dve", mixR[:, :, tsl], pbank[pO][:].rearrange("p (h i) -> p h i", h=4), tyb, ALU.mult,
                   [PB[pO], ret_bufs["tyb"]], [MIXR[n]])

            for i in range(NT + 4):
                if 0 <= i - 4 < NT:
                    O4b(i - 4)
                if 0 <= i - 3 < NT:
                    O4a(i - 3)
                if 0 <= i - 2 < NT:
                    O3(i - 2)
                if 0 <= i - 1 < NT:
                    O2(i - 1)
                if i < NT:
                    O1(i)
                if i in (3, 7, 11, 15):
                    jm = 6 + (i - 3) // 4
                    mod_col_half(jm, 2 if jm < 8 else 3, pi=5)
            if "yret" in dbg_d:
                for c in range(4):
                    final_ops.append(S.dma("sp", dbg_d["yret"][c * 128:(c + 1) * 128, :], mixR[:, c, :], reads=MIXR))
            CK("ret")
            b_g1b = Buf("g1b")
            b_g2b = Buf("g2b")
            b_bmt = Buf("bmt")
            STT("dve", AB[:, 4, :], modcol[:, 24:32, 0], 1.0, n2c[:], ALU.add, ALU.mult, [b_modcol, b_small["n2c"]], [b_AB])
            CP("dve", AB[:, 5, :], modcol[:, 16:24, 0], [b_modcol], [b_AB])

            ztok = ARB[:].rearrange("p (i n) -> p i n", i=NT)
            ZT = [[Buf("zt%d_%d" % (g, i)) for i in range(NT)] for g in range(3)]
            S.alias([b for l in ZT for b in l], [b for l in GTB for b in l] + [b_rope, B_SF] + HCT)
            o = XU
            araw = [ARX[:, o + i * 2052:o + i * 2052 + 2050] for i in range(2)]
            o += 2 * 2052
            zc = [XB[:, 2 * o:2 * o + 2048]]
            o += 1024
            rcv = [ARX[:, 6144:8192], ARX[:, o:o + 2048]]
            o += 2048
            assert o <= 16384, o
            rconv = ARX[:, 6144:8192]
            ARAW = [Buf("araw%d" % i) for i in range(2)]
            ZC = [Buf("zc0")]
            RCV = [Buf("rcv%d" % i) for i in range(2)]
            b_rconv = Buf("rconv")
            S.alias(ARAW + ZC, all_ret)
            S.alias([b_rconv], [WR[3]])
            S.alias([RCV[0]], [WR[3]])
            S.alias([RCV[1]], all_ret)
            zfT = EF[0:33, 0:2048]
            hid1T = EF[0:64, 2048:4096]
            fa = EF[0:64, 4096:4608]
            fk = EF[0:64, 4608:5120]
            fki = BIGI[0:64, 36864 + 5120:36864 + 5632]
            fw = EF[0:64, 5632:6144]
            hw1 = EF[0:33, 6144:6208]
            hw2 = EF[0:64, 6208:6272]
            hyp = EF[0:64, 6272:6276]
            b_hw = Buf("hw")
            b_ft = Buf("ftmp")
            qkv_bufs = [b for l in QR + KR for b in l] + VT
            S.alias([b_hw, b_ft, b_zf, b_hid1], qkv_bufs)
            S.alias([b_hid2, b_small["w3b"]], [b_rc, b_lg, b_DT, b_et, b_rtab])
            S.dma("sp", zfT, zfT_d, writes=[b_zf])
            S.dma("sp", hw1, hw1_d, writes=[b_hw])
            S.dma("sp", hw2, hw2_d, writes=[b_hw])
            S.dma("sp", hyp, hyp_d, writes=[b_hw])
            w3sc = EF[0:64, 6400:7424]
            b_w3sc = Buf("w3sc")
            S.alias([b_w3sc], qkv_bufs)
            for od_ in range(2):
                S.dma("sp", w3sc[:, 0:512], hw3_d[:, od_ * 512:(od_ + 1) * 512], writes=[b_w3sc])
                S.dma("sp", w3sc[:, 512:1024], hw3_d[:, 1024 + od_ * 512:1024 + (od_ + 1) * 512], writes=[b_w3sc])
                TT("dve", w3b[:, od_ * 512:(od_ + 1) * 512], w3sc[:, 0:512], w3sc[:, 512:1024], ALU.add,
                   [b_w3sc], [b_small["w3b"]])
                TT("dve", w3b[:, 1024 + od_ * 512:1024 + (od_ + 1) * 512], w3sc[:, 0:512], w3sc[:, 512:1024], ALU.subtract,
                   [b_w3sc], [b_small["w3b"]])

            def sin_layer(pi, bcol, fcol, out_ap, out_buf):
                TS("dve", fa, pbank[pi][0:64, :], bcol, fcol, ALU.add, ALU.mult, [PB[pi], b_hw], [b_ft])
                TS("dve", fki, fa, 1.0 / (2 * PI), 16.5, ALU.mult, ALU.add, [b_ft], [b_ft])
                CP("dve", fk, fki, [b_ft], [b_ft])
                STT("dve", fw, fk, -2 * PI, fa, ALU.mult, ALU.add, [b_ft], [b_ft])
                TS("dve", fk, fw, -33 * PI, 2 * PI, ALU.is_lt, ALU.mult, [b_ft], [b_ft])
                STT("dve", fw, fw, 33 * PI, fk, ALU.add, ALU.add, [b_ft], [b_ft])
                ACT(out_ap, fw, AF.Sin, [b_ft, b_negpi], [out_buf], bias=negpi[0:64, 0:1])

            def filt_unit(u):
                q = u % 4
                pi = pring.nxt()
                if u < 4:
                    MM(pbank[pi][0:64, :], hw1, zfT[:, q * 512:(q + 1) * 512], True, True, [b_hw, b_zf], [PB[pi]])
                    sin_layer(pi, hyp[:, 0:1], hyp[:, 1:2], hid1T[:, q * 512:(q + 1) * 512], b_hid1)
                else:
                    MM(pbank[pi][0:64, :], hw2, hid1T[:, q * 512:(q + 1) * 512], True, True, [b_hw, b_hid1], [PB[pi]])
                    sin_layer(pi, hyp[:, 2:3], hyp[:, 3:4], hid2T[:, q * 512:(q + 1) * 512], b_hid2)
            for i in range(2):
                MEMSET("pool", araw[i][:, 0:1], 0.0, [ARAW[i]])
                MEMSET("pool", araw[i][:, 2049:2050], 0.0, [ARAW[i]])
            arring = Ring(2)
            zcring = Ring(1)
            evq = [0]

            hy_slots = {}

            def hy_A(chunk):
                g, cc = chunk // 4, chunk % 4
                if cc == 0:
                    hy_slots[g] = load_wblock(winv[:, :, g * 512:(g + 1) * 512])
                s_ = hy_slots[g]
                ai = arring.nxt()
                for q in range(4):
                    pi = pring.nxt()
                    proj_fm(s_, cc * 128, q, pi)
                    CP("act", araw[ai][:, 1 + q * 512:1 + (q + 1) * 512], pbank[pi][:], [PB[pi]], [ARAW[ai]])
                    ACT(rcv[ai][:, q * 512:(q + 1) * 512], pbank[pi][:], AF.Identity, [PB[pi], b_small["hcw"]], [RCV[ai]],
                        scale=hcw[:, chunk, 1:2], bias=hcw[:, chunk, 3:4])
                return ai

            def hy_B(chunk, ai):
                wq = hcw[:, chunk, :]
                zi = zcring.nxt()
                STT("dve", rcv[ai], araw[ai][:, 0:2048], wq[:, 0:1], rcv[ai], ALU.mult, ALU.add,
                    [ARAW[ai], b_small["hcw"], RCV[ai]], [RCV[ai]])
                STT("dve", zc[zi], araw[ai][:, 2:2050], wq[:, 2:3], rcv[ai], ALU.mult, ALU.add,
                    [ARAW[ai], b_small["hcw"], RCV[ai]], [ZC[zi]])
                return zi

            def hy_C(chunk, zi):
                g = chunk // 4
                for half in range(2):
                    ti = tring.nxt()
                    for a in range(8):
                        i = half * 8 + a
                        TR(ptr[ti][:, a * 128:(a + 1) * 128], zc[zi][:, i * 128:(i + 1) * 128], ident[:],
                           [ZC[zi], b_ident], [PT[ti]])
                    CP("act", ztok[:, half * 8:half * 8 + 8, chunk * 128:(chunk + 1) * 128],
                       ptr[ti][:].rearrange("p (a c) -> p a c", a=8), [PT[ti]], ZT[g][half * 8:half * 8 + 8])

            ai_cur = hy_A(0)
            for chunk in range(12):
                ai_nxt = hy_A(chunk + 1) if chunk + 1 < 12 else None
                zi = hy_B(chunk, ai_cur)
                hy_C(chunk, zi)
                if chunk < 8:
                    filt_unit(chunk)
                ai_cur = ai_nxt
            if "ztok" in dbg_d:
                for i in range(NT):
                    final_ops.append(S.dma("sp", dbg_d["ztok"][i * 128:(i + 1) * 128, :], ztok[:, i, :],
                                           reads=[ZT[0][i], ZT[1][i], ZT[2][i]]))

            CK("hyproj")
            ksum = ARA[:, 0:8192].rearrange("p (i n) -> p i n", i=NT)
            kdiff = ARA[:, 8192:16384].rearrange("p (i n) -> p i n", i=NT)
            KS = [Buf("ks%d" % i) for i in range(NT)]
            S.alias(KS, HT)
            Yre = ARE[:, 0:8192].rearrange("p (j n) -> p j n", j=16)
            Yim = ARE[:, 8192:16384].rearrange("p (j n) -> p j n", j=16)
            YB = [Buf("y%d" % j) for j in range(16)]
            S.alias(YB, qkv_bufs + [b_hw, b_ft, b_zf, b_hid1])
            fblk = [XB[:, i * 4096:(i + 1) * 4096].rearrange("p (i s r) -> p i s r", i=16, s=2) for i in range(2)]
            FB = [Buf("fb%d" % i) for i in range(2)]
            gt = [ARX[:, 4096 + i * 512:4096 + (i + 1) * 512] for i in range(4)]
            GT = [Buf("gt%d" % i) for i in range(4)]
            yhy = [XB[:, 2 * (6144 + i * 256):2 * (6144 + i * 256) + 512] for i in range(2)]
            YH = [Buf("yhy%d" % i) for i in range(2)]
            S.alias(FB + GT + YH, WR + [b_rconv] + RCV)
            wpc = [XB[:, 2 * (6656 + i * 512):2 * (6656 + i * 512) + 1024].rearrange("p (i n) -> p i n", i=8) for i in range(2)]
            bpc = [ARX[:, 7680 + i * 128:7680 + (i + 1) * 128] for i in range(2)]
            gpc = [ARX[:, 7936 + i * 128:7936 + (i + 1) * 128] for i in range(2)]
            WPC = [Buf("wpc%d" % i) for i in range(2)]
            BPC = [Buf("bpc%d" % i) for i in range(2)]
            GPC = [Buf("gpc%d" % i) for i in range(2)]
            S.alias(WPC + BPC + GPC, WR + [b_rconv] + RCV)

            def g1_piece(c8):
                i = c8 % 2
                c0 = 2048 + c8 * 128
                S.dma("pool", wpc[i], wmv[:, :, c0:c0 + 128], writes=[WPC[i]])
                S.dma("sp", bpc[i], bmodr_d[0:1, c0:c0 + 128].partition_broadcast(128), writes=[BPC[i]])
                pi = pring.nxt()
                for k in range(8):
                    MM(pbank[pi][:, 0:128], srep[:, k, :], wpc[i][:, k, :], k == 0, k == 7, [WPC[i], b_small["srep"]],
                       [PB[pi]])
                TT("dve", gpc[i], pbank[pi][:, 0:128], bpc[i], ALU.add, [PB[pi], BPC[i]], [GPC[i]])
                TT("dve", woutS[:, :, c8 * 128:(c8 + 1) * 128], woutS[:, :, c8 * 128:(c8 + 1) * 128],
                   bc(gpc[i].unsqueeze(1), [128, 8, 128]), ALU.mult, [b_wout, GPC[i]], [b_wout])
            o = XU
            hbb = ARX[:, o:o + 512]; o += 512
            absb = ARX[:, o:o + 512]; o += 512
            wint = [ARX[:, o + i * 512:o + (i + 1) * 512] for i in range(2)]; o += 1024
            kfw = [ARX[:, o + i * 512:o + (i + 1) * 512] for i in range(2)]; o += 1024
            Ksb = [ARX[:, o + i * 512:o + (i + 1) * 512] for i in range(4)]; o += 2048
            tq = [ARX[:, o + i * 512:o + (i + 1) * 512] for i in range(4)]; o += 2048
            ynq = XB[0:1, 2 * o:2 * o + 512]; o += 256
            knq = ARX[0:1, o:o + 512]; o += 512
            assert o <= 16384, o
            b_hbb = Buf("hbb")
            b_absb = Buf("absb")
            WINT = [Buf("win%d" % i) for i in range(2)]
            KFW = [Buf("kfw%d" % i) for i in range(2)]
            KSB = [Buf("ksb%d" % i) for i in range(4)]
            TQ = [Buf("tq%d" % i) for i in range(4)]
            b_ynq = Buf("ynq")
            b_knq = Buf("knq")
            S.alias([b_hbb, b_absb, b_ynq, b_knq] + WINT + KFW + KSB + TQ, ARAW + ZC + RCV + all_ret)
            S.dma("sp", absb, absd_d.partition_broadcast(128), writes=[b_absb])
            fbring = Ring(2)
            mixH = [Buf("mixh%d" % i) for i in range(NT)]

            for od in range(2):
                vt_col = 0
                xg_col = 512 * (od + 1)
                S.dma("sp", hbb, hbias_d[0:1, od * 512:(od + 1) * 512].partition_broadcast(128), writes=[b_hbb])
                def filt_tile(od_, i):
                    wi = i % 2
                    ACT(wint[wi], absb, AF.Exp, [b_absb, b_small["tn"]], [WINT[wi]], scale=tn[:, i:i + 1])
                    pf = pring.nxt()
                    MM(pbank[pf][:], hid2T[:, i * 128:(i + 1) * 128], w3b[:, od_ * 512:(od_ + 1) * 512], True, True,
                       [b_hid2, b_small["w3b"]], [PB[pf]])
                    pb_ = pring.nxt()
                    MM(pbank[pb_][:], hid2T[:, i * 128:(i + 1) * 128], w3b[:, 1024 + od_ * 512:1024 + (od_ + 1) * 512],
                       True, True, [b_hid2, b_small["w3b"]], [PB[pb_]])
                    if i != 0:
                        TT("dve", ksum[:, i, :], pbank[pf][:], wint[wi], ALU.mult, [PB[pf], WINT[wi]], [KS[i]])
                        TT("dve", kdiff[:, i, :], pbank[pb_][:], wint[wi], ALU.mult, [PB[pb_], WINT[wi]], [KS[i]])
                    else:
                        TT("dve", kfw[0], pbank[pf][:], wint[wi], ALU.mult, [PB[pf], WINT[wi]], [KFW[0]])
                        TT("dve", kfw[1], pbank[pb_][:], wint[wi], ALU.mult, [PB[pb_], WINT[wi]], [KFW[1]])
                        TT("dve", kfw[0][0:1, :], kfw[0][0:1, :], kfw[1][0:1, :], ALU.add, [KFW[0], KFW[1]], [KFW[0]])
                        TS("dve", kfw[0][0:1, :], kfw[0][0:1, :], 0.5, None, ALU.mult, None, [KFW[0]], [KFW[0]])
                        CP("dve", kfw[1][0:1, :], kfw[0][0:1, :], [KFW[0]], [KFW[1]])
                        CP("dve", ksum[:, i, :], kfw[0], [KFW[0]], [KS[i]])
                        CP("dve", kdiff[:, i, :], kfw[1], [KFW[1]], [KS[i]])
                if od == 0:
                    for i in range(NT):
                        filt_tile(0, i)
                vbufs = [ZT[0][i] for i in range(NT)]
                for j in range(16):
                    fi = fbring.nxt()
                    S.dma("sp", fblk[fi].rearrange("p i s r -> p (i s r)"), fd_d[j], writes=[FB[fi]])
                    pKr = pring.nxt()
                    for i in range(NT):
                        MM(pbank[pKr][:], fblk[fi][:, i, 0, :], ksum[:, i, :], i == 0, i == NT - 1, [FB[fi], KS[i]], [PB[pKr]])
                    pKi = pring.nxt()
                    for i in range(NT):
                        MM(pbank[pKi][:], fblk[fi][:, i, 1, :], kdiff[:, i, :], i == 0, i == NT - 1, [FB[fi], KS[i]], [PB[pKi]])
                    kb = (j % 2) * 2
                    CP("act", Ksb[kb], pbank[pKr][:], [PB[pKr]], [KSB[kb]])
                    CP("act", Ksb[kb + 1], pbank[pKi][:], [PB[pKi]], [KSB[kb + 1]])
                    TT("dve", Ksb[kb], Ksb[kb], hbb, ALU.add, [KSB[kb], b_hbb], [KSB[kb]])
                    pUr = pring.nxt()
                    for i in range(NT):
                        MM(pbank[pUr][:], fblk[fi][:, i, 0, :], ztok[:, i, vt_col:vt_col + 512], i == 0, i == NT - 1,
                           [FB[fi], vbufs[i]], [PB[pUr]])
                    pUi = pring.nxt()
                    for i in range(NT):
                        MM(pbank[pUi][:], fblk[fi][:, i, 1, :], ztok[:, i, vt_col:vt_col + 512], i == 0, i == NT - 1,
                           [FB[fi], vbufs[i]], [PB[pUi]])
                    TT("dve", tq[0], pbank[pUr][:], Ksb[kb], ALU.mult, [PB[pUr], KSB[kb]], [TQ[0]])
                    TT("dve", tq[1], pbank[pUi][:], Ksb[kb + 1], ALU.mult, [PB[pUi], KSB[kb + 1]], [TQ[1]])
                    TT("dve", Yre[:, j, :], tq[0], tq[1], ALU.subtract, [TQ[0], TQ[1]], [YB[j]])
                    TT("dve", tq[2], pbank[pUr][:], Ksb[kb + 1], ALU.mult, [PB[pUr], KSB[kb + 1]], [TQ[2]])
                    TT("dve", tq[3], pbank[pUi][:], Ksb[kb], ALU.mult, [PB[pUi], KSB[kb]], [TQ[3]])
                    TT("dve", Yim[:, j, :], tq[2], tq[3], ALU.add, [TQ[2], TQ[3]], [YB[j]])
                    if j == 0:
                        TS("dve", Yre[0:1, 0, :], Yre[0:1, 0, :], 0.5, None, ALU.mult, None, [YB[0]], [YB[0]])
                pN = pring.nxt()
                for i in range(NT):
                    MM(pbank[pN][0:1, :], nyqc[:, 0:1], ksum[:, i, :], i == 0, i == NT - 1, [b_small["nyq"], KS[i]], [PB[pN]])
                CP("act", knq, pbank[pN][0:1, :], [PB[pN]], [b_knq])
                TT("dve", knq, knq, hbb[0:1, :], ALU.add, [b_knq, b_hbb], [b_knq])
                pN2 = pring.nxt()
                for i in range(NT):
                    MM(pbank[pN2][0:1, :], nyqc[:, 0:1], ztok[:, i, vt_col:vt_col + 512], i == 0, i == NT - 1,
                       [b_small["nyq"], vbufs[i]], [PB[pN2]])
                STT("dve", ynq, pbank[pN2][0:1, :], 0.5, knq, ALU.mult, ALU.mult, [PB[pN2], b_knq], [b_ynq])
                if od == 1:
                    woutS = ARA[:, 0:8192].rearrange("p (i n) -> p i n", i=8)
                    b_wout = Buf("wout")
                    S.alias([b_wout], KS)
                    S.alias(mixH, KS)
                    S.dma("pool", woutS, wout_d.rearrange("(i p) n -> p i n", p=128), writes=[b_wout])
                pend_T = []
                for t in range(NT):
                    fi = fbring.nxt()
                    S.dma("sp", fblk[fi].rearrange("p i s r -> p (i s r)"), fd_d[t], writes=[FB[fi]])
                    pY = pring.nxt()
                    for i in range(NT):
                        MM(pbank[pY][:], fblk[fi][:, i, 0, :], Yre[:, i, :], i == 0, False, [FB[fi], YB[i]], [PB[pY]])
                    for i in range(NT):
                        MM(pbank[pY][:], fblk[fi][:, i, 1, :], Yim[:, i, :], False, False, [FB[fi], YB[i]], [PB[pY]])
                    MM(pbank[pY][:], nyqr[0:1, :], ynq, False, True, [b_small["nyq"], b_ynq], [PB[pY]])
                    if od == 0:
                        filt_tile(1, t)
                    if od == 1 and t < 8:
                        g1_piece(t)
                    if od == 1 and t >= 1 and pend_T:
                        yhy_T(pend_T.pop(0))
                    if od == 0:
                        STT("dve", ztok[:, t, 0:512], pbank[pY][:], 2.0 / NFFT, ztok[:, t, xg_col:xg_col + 512],
                            ALU.mult, ALU.mult, [PB[pY], ZT[1][t]], [ZT[0][t]])
                    else:
                        yi = t % 2
                        STT("dve", yhy[yi], pbank[pY][:], 2.0 / NFFT, ztok[:, t, xg_col:xg_col + 512],
                            ALU.mult, ALU.mult, [PB[pY], ZT[2][t]], [YH[yi]])
                        def yhy_T(t_):
                            yi_ = t_ % 2
                            ti = tring.nxt()
                            for c in range(4):
                                TR(ptr[ti][:, c * 128:(c + 1) * 128], yhy[yi_][:, c * 128:(c + 1) * 128], ident[:],
                                   [YH[yi_], b_ident], [PT[ti]])
                            CP("act", mixHy[:, :, t_ * 128:(t_ + 1) * 128],
                               ptr[ti][:, 0:512].rearrange("p (c i) -> p c i", c=4), [PT[ti]], [mixH[t_]])
                        pend_T.append(t)
                while pend_T:
                    yhy_T(pend_T.pop(0))
                if od == 0 and "y1" in dbg_d:
                    for i in range(NT):
                        final_ops.append(S.dma("sp", dbg_d["y1"][i * 128:(i + 1) * 128, :], ztok[:, i, 0:512],
                                               reads=[ZT[0][i]]))
            if "yhy" in dbg_d:
                for c in range(4):
                    final_ops.append(S.dma("sp", dbg_d["yhy"][c * 128:(c + 1) * 128, :], mixHy[:, c, :], reads=mixH))

            if "zlate" in dbg_d:
                for i in range(NT):
                    final_ops.append(S.dma("sp", dbg_d["zlate"][i * 128:(i + 1) * 128, :], ztok[:, i, :],
                                           reads=[ZT[0][i], ZT[1][i], ZT[2][i]]))
            CK("hyena")
            X1 = ARX[:].rearrange("p (t n) -> p t n", t=NT)
            X1B = [Buf("x1_%d" % i) for i in range(NT)]
            xt2 = [EF[:, i * 1024:(i + 1) * 1024] for i in range(2)]
            XT2 = [Buf("xt2_%d" % i) for i in range(2)]
            S.alias(XT2, YB)
            all_x_scratch = (WR + WPC + BPC + GPC + [b_rconv] + RCV + FB + GT + YH + [b_hbb, b_absb, b_ynq, b_knq] + WINT + KFW + KSB + TQ
                             + ARAW + ZC + all_ret + XT + XN + TA + HCT + [b_zf, b_hid1])
            S.alias(X1B, all_x_scratch)
            xn2 = [ARB[:, i * 1024:(i + 1) * 1024] for i in range(NT)]
            XN2 = [Buf("xn2_%d" % i) for i in range(NT)]
            S.alias(XN2, [b for l in ZT for b in l])
            xn2ring = Ring(NT)
            xi2 = {}
            for t in range(NT):
                xi = t % 2
                S.dma("sp", xt2[xi], x_d[t * 128:(t + 1) * 128, :], writes=[XT2[xi]])
                for half in range(2):
                    pi = pring.nxt()
                    for k in range(8):
                        MM(pbank[pi][:], mix_chunk(k)[:, t * 128:(t + 1) * 128], woutS[:, k, half * 512:(half + 1) * 512],
                           k == 0, k == 7, [mixH[t], MIXR[t], b_wout], [PB[pi]])
                    TT("dve", X1[:, t, half * 512:(half + 1) * 512], xt2[xi][:, half * 512:(half + 1) * 512], pbank[pi][:],
                       ALU.add, [XT2[xi], PB[pi]], [X1B[t]])
                xi2[t] = norm_s1(X1[:, t, :], [X1B[t]], 18 + t, xn2, XN2, xn2ring)
            if "x1" in dbg_d:
                for t in range(NT):
                    final_ops.append(S.dma("sp", dbg_d["x1"][t * 128:(t + 1) * 128, :], X1[:, t, :], reads=[X1B[t]]))

            CK("wout")
            h2T = ARA[:].rearrange("p (c t) -> p c t", c=8)
            H2 = [Buf("h2T%d" % i) for i in range(2 * NT)]
            S.alias(H2, [b_wout] + mixH)
            for i in range(NT):
                norm_s2(xi2[i], 4, 5, lambda c, i=i: h2T[:, c, i * 128:(i + 1) * 128], H2[2 * i:2 * i + 2], xn2, XN2)

            CK("norm2")
            actT = [ARB[:, g * 8192:(g + 1) * 8192].rearrange("p (f t) -> p f t", f=4) for g in range(2)]
            wdn = [ARB[:, 16384 + g * 4096:16384 + (g + 1) * 4096].rearrange("p (f n) -> p f n", f=4) for g in range(2)]
            ACTB = [[Buf("act%d_%d" % (g, f)) for f in range(4)] for g in range(2)]
            WDN = [[Buf("wdn%d_%d" % (g, f)) for f in range(4)] for g in range(2)]
            zall = [b for l in ZT for b in l]
            S.alias([b for l in ACTB for b in l] + [b for l in WDN for b in l], zall + XN2)
            wup = [ARE[:, s * 2048:(s + 1) * 2048].rearrange("p (i v n) -> p i v n", i=8, v=2) for s in range(3)]
            WUP = [Buf("wup%d" % s) for s in range(3)]
            S.alias(WUP, YB + XT2)
            wupring = Ring(3)
            rbuf = [EF[:, 3072 + i * 2048:3072 + (i + 1) * 2048] for i in range(2)] + [EF[:, 11272:11272 + 2048]]
            RB = [Buf("rb%d" % i) for i in range(3)]
            rbring = Ring(3)
            araw2 = [EF[:, 7168 + i * 2052:7168 + i * 2052 + 2050] for i in range(2)]
            ARAW2 = [Buf("araw2_%d" % i) for i in range(2)]
            b_rc2 = RB[2]
            e_old = YB + XT2 + [b_hid2, b_small["w3b"], b_rc, b_lg, b_DT, b_et, b_rtab] + MIXR
            S.alias(ARAW2 + [b_rc2], e_old)
            g2b = EF[:, 13320:14344]
            bmt2 = EF[:, 14344:14856]
            b_bmt2 = Buf("bmt2")
            S.alias([b_g2b, b_bmt2], e_old)
            lw2 = [ARE[:, 6144 + i * 4096:6144 + (i + 1) * 4096].rearrange("p (i n) -> p i n", i=8) for i in range(2)]
            LW2 = [Buf("lw2_%d" % i) for i in range(2)]
            S.alias(LW2, YB)
            mod_bc_half(10, g2b, b_g2b, bmt2, b_bmt2, lw2[0], LW2[0])
            mod_bc_half(11, g2b, b_g2b, bmt2, b_bmt2, lw2[1], LW2[1])
            S.alias(RB, LW2)
            for i in range(2):
                MEMSET("pool", araw2[i][:, 0:1], 0.0, [ARAW2[i]])
                MEMSET("pool", araw2[i][:, 2049:2050], 0.0, [ARAW2[i]])
            ar2ring = Ring(2)
            wupv = wup_d.rearrange("(i p) n -> p i n", p=128)

            def ffn_job(s, vg, f):
                ai = ar2ring.nxt()
                ri = rbring.nxt()
                out_ap, out_buf = rbuf[ri], RB[ri]
                wq = fcw[:, vg * NFC + f, :]
                for q in range(4):
                    pi = pring.nxt()
                    for k in range(8):
                        MM(pbank[pi][:], wup[s][:, k, vg, :], h2T[:, k, q * 512:(q + 1) * 512], k == 0, k == 7,
                           [WUP[s]] + H2[q * 8:q * 8 + 8], [PB[pi]])
                    CP("act", araw2[ai][:, 1 + q * 512:1 + (q + 1) * 512], pbank[pi][:], [PB[pi]], [ARAW2[ai]])
                    ACT(out_ap[:, q * 512:(q + 1) * 512], pbank[pi][:], AF.Identity, [PB[pi], b_small["fcw"]], [out_buf],
                        scale=wq[:, 1:2], bias=wq[:, 3:4])
                STT("dve", out_ap, araw2[ai][:, 0:2048], wq[:, 0:1], out_ap, ALU.mult, ALU.add,
                    [ARAW2[ai], b_small["fcw"], out_buf], [out_buf])
                STT("dve", out_ap, araw2[ai][:, 2:2050], wq[:, 2:3], out_ap, ALU.mult, ALU.add,
                    [ARAW2[ai], b_small["fcw"], out_buf], [out_buf])
                return ri

            groups = [list(range(g * 4, min(g * 4 + 4, NFC))) for g in range(6)]

            def ffn_up(gi):
                gb = gi % 2
                for fi, f in enumerate(groups[gi]):
                    s = wupring.nxt()
                    S.dma("pool", wup[s][:, :, 0, :], wupv[:, :, f * 128:(f + 1) * 128], writes=[WUP[s]])
                    S.dma("pool", wup[s][:, :, 1, :], wupv[:, :, DFF + f * 128:DFF + (f + 1) * 128], writes=[WUP[s]])
                    S.dma("pool", wdn[gb][:, fi, :], wdn_d[f * 128:(f + 1) * 128, :], writes=[WDN[gb][fi]])
                    TT("dve", wdn[gb][:, fi, :], wdn[gb][:, fi, :], g2b, ALU.mult, [WDN[gb][fi], b_g2b], [WDN[gb][fi]])
                    rv = ffn_job(s, 0, f)
                    rg = ffn_job(s, 1, f)
                    ACT(rbuf[rg], rbuf[rg], AF.Silu, [RB[rg]], [RB[rg]])
                    TT("dve", actT[gb][:, fi, :], rbuf[rv], rbuf[rg], ALU.mult, [RB[rv], RB[rg]], [ACTB[gb][fi]])

            def ffn_down(gi, after_tile=None):
                gb = gi % 2
                nf = len(groups[gi])
                for t in range(NT):
                    if after_tile is not None and t >= 1:
                        after_tile(t - 1)
                    for half in range(2):
                        pi = pring.nxt()
                        for fi in range(nf):
                            MM(pbank[pi][:], actT[gb][:, fi, t * 128:(t + 1) * 128], wdn[gb][:, fi, half * 512:(half + 1) * 512],
                               fi == 0, fi == nf - 1, [ACTB[gb][fi], WDN[gb][fi]], [PB[pi]])
                        TT("dve", X1[:, t, half * 512:(half + 1) * 512], X1[:, t, half * 512:(half + 1) * 512], pbank[pi][:],
                           ALU.add, [X1B[t], PB[pi]], [X1B[t]])

            for gi in range(6):
                ffn_up(gi)
                if gi >= 1:
                    ffn_down(gi - 1)
            nfb = ARAF[:, 0:1024]
            b_nfb = Buf("nfb")
            S.alias([b_nfb], H2)
            S.dma("sp", nfb, nfr_d.partition_broadcast(128), writes=[b_nfb])
            outt = [ARAF[:, 1024 + i * 1024:1024 + (i + 1) * 1024] for i in range(2)]
            OT = [Buf("ot%d" % i) for i in range(2)]
            S.alias(OT, H2)

            def final_tile(t):
                sc = 36 + t
                oi = t % 2
                ACT(outt[oi], X1[:, t, :], AF.Square, [X1B[t], SSQ[sc]], [SSQ[sc], OT[oi]], accum_out=ssq[:, sc:sc + 1])
                ACT(rstd[:, sc:sc + 1], ssq[:, sc:sc + 1], AF.Sqrt, [SSQ[sc], b_negpi], [RSTD[sc]], scale=1.0 / D,
                    bias=epsc[:, 0:1])
                RCP(rstd[:, sc:sc + 1], rstd[:, sc:sc + 1], [RSTD[sc]], [RSTD[sc]])
                STT("dve", outt[oi], X1[:, t, :], rstd[:, sc:sc + 1], nfb, ALU.mult, ALU.mult, [X1B[t], RSTD[sc], b_nfb], [OT[oi]])
                final_ops.append(S.dma("sp", y_d[t * 128:(t + 1) * 128, :], outt[oi], reads=[OT[oi]]))
            ffn_down(5, after_tile=final_tile)
            final_tile(NT - 1)

            CK("ffn")
        except _Stop:
            pass
        S.emit(final_ops=final_ops)
    return nc


def _prep_shared(inp):
    f = np.float32
    sh = {}
    sh["w_mod"] = np.ascontiguousarray(inp["w_mod"][0], f)
    b_mod = np.asarray(inp["b_mod"][0], f)
    sh["bmod_col"] = np.ascontiguousarray(b_mod.reshape(48, 128).T)
    sh["bmod_row"] = np.ascontiguousarray(b_mod.reshape(1, 6144))
    sh["n1c"] = np.ascontiguousarray(np.asarray(inp["norm1"][0], f).reshape(8, 128).T)
    sh["n2c"] = np.ascontiguousarray(np.asarray(inp["norm2"][0], f).reshape(8, 128).T)
    sh["nf_row"] = np.ascontiguousarray(np.asarray(inp["norm_f"], f).reshape(1, D))
    w_in = np.asarray(inp["w_in"][0], f)
    def swapped(c0):
        blk = w_in[:, c0:c0 + 256].reshape(D, 4, 2, 32)
        return blk[:, :, ::-1, :].reshape(D, 256)
    sh["w_in"] = np.ascontiguousarray(np.concatenate([w_in, swapped(1536), swapped(1792)], axis=1))
    hw = np.asarray(inp["hy_conv_w"][0], f)
    hb = np.asarray(inp["hy_conv_b"][0], f)
    hc = np.concatenate([hw, hb[None]], axis=0)
    sh["hcw"] = np.ascontiguousarray(hc.reshape(4, 12, 128).transpose(2, 1, 0).reshape(128, 48))
    fw = np.asarray(inp["ffn_conv_w"][0], f)
    fb = np.asarray(inp["ffn_conv_b"][0], f)
    fc = np.concatenate([fw, fb[None]], axis=0)
    sh["fcw"] = np.ascontiguousarray(fc.reshape(4, 44, 128).transpose(2, 1, 0).reshape(128, 176))
    sh["hy_w1"] = np.ascontiguousarray(inp["hy_w1"][0], f)
    sh["hyp"] = np.ascontiguousarray(np.stack([inp["hy_b1"][0], inp["hy_f1"][0], inp["hy_b2"][0], inp["hy_f2"][0]],
                                              axis=1).astype(f))
    sh["hy_w2"] = np.ascontiguousarray(inp["hy_w2"][0], f)
    sh["hy_w3"] = np.ascontiguousarray(inp["hy_w3"][0], f)
    sh["hy_bias"] = np.ascontiguousarray(np.asarray(inp["hy_bias"][0], f).reshape(1, 1024))
    sh["rlog"] = np.ascontiguousarray(np.concatenate([inp["ret_logit_f"][0], inp["ret_logit_b"][0]]).astype(f).reshape(1, 8))
    sh["w_out"] = np.ascontiguousarray(inp["w_out"][0], f)
    sh["w_up"] = np.ascontiguousarray(inp["ffn_w_up"][0], f)
    sh["w_down"] = np.ascontiguousarray(inp["ffn_w_down"][0], f)
    hc_ = host_consts()
    for k in ("fd", "nyqc", "nyqr", "rope", "rc", "zfT", "absd", "tn"):
        sh[k] = hc_[k]
    return sh


_NC_CACHE = {}


def kernel(_dbg=(), _stop=None, _cores=None, **inputs):
    inp = {k: np.asarray(v) for k, v in inputs.items()}
    key = repr((_dbg, _stop))
    if key not in _NC_CACHE:
        _NC_CACHE[key] = build(_dbg, _stop)
    nc = _NC_CACHE[key]
    sh = _prep_shared(inp)
    x = np.asarray(inp["x"], np.float32)
    ctx = np.asarray(inp["ctx"], np.float32)
    c = np.asarray(inp["c"], np.float32)
    c_ctx = np.asarray(inp["c_ctx"], np.float32)
    n = x.shape[0] if _cores is None else _cores
    in_maps = []
    for b in range(n):
        m = dict(sh)
        m["x"] = np.ascontiguousarray(x[b])
        m["ctx"] = np.ascontiguousarray(ctx[b])
        cv = np.stack([c[b].reshape(8, 128).T, c_ctx.reshape(8, 128).T], axis=2)
        m["cvec"] = np.ascontiguousarray(cv.reshape(128, 16))
        in_maps.append(m)
    res = run_bass_kernel_spmd(nc, in_maps, core_ids=list(range(n)))
    out = np.stack([np.asarray(r["y"], np.float32) for r in res.results], axis=0)
    if _dbg:
        return out, res.results
    return out
```

```python
import math
from contextlib import ExitStack

import numpy as np
import ml_dtypes

import concourse.bass as bass
import concourse.mybir as mybir
from concourse.bass_utils import run_bass_kernel_spmd

F32 = mybir.dt.float32
BF16 = mybir.dt.bfloat16
I32 = mybir.dt.int32
AF = mybir.ActivationFunctionType
ALU = mybir.AluOpType

L = 2048
D = 1024
NT = 16
NFFT = 4096
EPS = 1e-6
DFF = 2816
NFC = 22
ENGS = ("pe", "act", "dve", "pool", "sp")
PI = math.pi


class Buf:
    __slots__ = ("name", "w", "r", "rd")

    def __init__(self, name=""):
        self.name = name
        self.w = None
        self.r = {}
        self.rd = []


class Op:
    __slots__ = ("eng", "fn", "deps", "signal", "count", "dma", "chan", "cval", "cprev", "raw")

    def __init__(self, eng, fn, dma):
        self.eng = eng
        self.fn = fn
        self.dma = dma
        self.deps = []
        self.signal = False
        self.count = 0
        self.chan = None
        self.cval = 0
        self.cprev = 0
        self.raw = set()


class Sched:
    def __init__(self, nc, nchan=20, self_wait=True):
        self.nc = nc
        self.ops = {e: [] for e in ENGS}
        self.nchan = nchan
        self.self_wait = self_wait

    def add(self, eng, fn, reads=(), writes=(), dma=False):
        op = Op(eng, fn, dma)
        deps = {}
        for b in reads:
            if b.w is not None:
                deps[id(b.w)] = b.w
                op.raw.add(id(b.w))
        for b in writes:
            if b.w is not None:
                deps[id(b.w)] = b.w
            for o in b.r.values():
                deps[id(o)] = o
            for o in b.rd:
                deps[id(o)] = o
        op.deps = list(deps.values())
        for d in op.deps:
            d.signal = True
        for b in reads:
            if dma:
                b.rd.append(op)
            else:
                b.r[eng] = op
        for b in writes:
            b.w = op
            b.r = {}
            b.rd = []
        if dma:
            op.signal = True
        self.ops[eng].append(op)
        return op

    def dma(self, eng, out, in_, reads=(), writes=()):
        return self.add(eng, lambda e: e.dma_start(out=out, in_=in_), reads, writes, dma=True)

    def alias(self, new_bufs, old_bufs):
        ops = {}
        for b in old_bufs:
            if b.w is not None:
                ops[id(b.w)] = b.w
            for o in b.r.values():
                ops[id(o)] = o
            for o in b.rd:
                ops[id(o)] = o
        lst = list(ops.values())
        for b in new_bufs:
            b.rd.extend(lst)

    def emit(self, final_ops=()):
        nc = self.nc
        with ExitStack() as st:
            esem = {e: st.enter_context(nc.semaphore("s_" + e)) for e in ENGS}
            csem = {}
            for e in ENGS:
                if any(o.dma for o in self.ops[e]):
                    csem[e] = [st.enter_context(nc.semaphore("c_%s_%d" % (e, i))) for i in range(self.nchan)]
            for e in ENGS:
                n = 0
                uses = [0] * self.nchan
                k = 0
                for o in self.ops[e]:
                    if o.dma:
                        c = k % self.nchan
                        k += 1
                        o.chan = csem[e][c]
                        o.cprev = 16 * uses[c]
                        uses[c] += 1
                        o.cval = 16 * uses[c]
                    elif o.signal:
                        n += 1
                        o.count = n
            self_wait = self.self_wait

            def run(e, eng):
                waited = {}
                for o in self.ops[e]:
                    need = {}
                    for d in o.deps:
                        if d.dma:
                            key, val = d.chan, d.cval
                        else:
                            if d.eng == e and (e == "pe" or not self_wait):
                                continue
                            if d.eng == e and id(d) not in o.raw:
                                continue
                            key, val = esem[d.eng], d.count
                        kk = id(key)
                        if kk not in need or need[kk][1] < val:
                            need[kk] = (key, val)
                    if o.dma and o.cprev > 0:
                        kk = id(o.chan)
                        if kk not in need or need[kk][1] < o.cprev:
                            need[kk] = (o.chan, o.cprev)
                    for kk, (key, val) in need.items():
                        if waited.get(kk, 0) >= val:
                            continue
                        eng.wait_ge(key, val)
                        waited[kk] = val
                    ins = o.fn(eng)
                    if o.dma:
                        ins.then_inc(o.chan, 16)
                    elif o.signal:
                        ins.then_inc(esem[e], 1)
                if e == "sp":
                    for d in final_ops:
                        if d.dma:
                            eng.wait_ge(d.chan, d.cval)
                        else:
                            eng.wait_ge(esem[d.eng], d.count)

            with nc.Block() as block:
                @block.tensor
                def _(eng):
                    run("pe", eng)

                @block.scalar
                def _(eng):
                    run("act", eng)

                @block.vector
                def _(eng):
                    run("dve", eng)

                @block.gpsimd
                def _(eng):
                    run("pool", eng)

                @block.sync
                def _(eng):
                    run("sp", eng)


class Ring:
    def __init__(self, n):
        self.n = n
        self.i = -1

    def nxt(self):
        self.i = (self.i + 1) % self.n
        return self.i


_CONSTS = None


def host_consts():
    global _CONSTS
    if _CONSTS is not None:
        return _CONSTS
    c = {}
    j = np.arange(16)[:, None, None, None]
    p = np.arange(128)[None, :, None, None]
    i = np.arange(16)[None, None, :, None]
    r = np.arange(128)[None, None, None, :]
    prod = ((128 * i + p) * (128 * j + r)) % NFFT
    ang = 2.0 * np.pi * prod.astype(np.float64) / NFFT
    fd = np.stack([np.cos(ang), -np.sin(ang)], axis=3)
    c["fd"] = fd.reshape(16, 128, 4096).astype(ml_dtypes.bfloat16)
    sgn = (1.0 - 2.0 * (np.arange(128) % 2)).astype(np.float32)
    c["nyqc"] = sgn.reshape(128, 1).astype(ml_dtypes.bfloat16)
    c["nyqr"] = sgn.reshape(1, 128).astype(ml_dtypes.bfloat16)
    t = np.arange(L)
    row = (t // 64).astype(np.float32)
    col = (t % 64).astype(np.float32)
    inv_freq = (10000.0 ** (-np.arange(16, dtype=np.float32) / 16)).astype(np.float32)
    ang = np.concatenate([row[:, None] * inv_freq, col[:, None] * inv_freq], axis=-1).astype(np.float32)
    cs, sn = np.cos(ang), np.sin(ang)
    cos64 = np.concatenate([cs, cs], axis=1).T
    sin64 = np.concatenate([-sn, sn], axis=1).T
    c["rope"] = np.concatenate([np.tile(cos64, (2, 1)), np.tile(sin64, (2, 1))], axis=1).astype(np.float32)
    rc = np.zeros((128, 776), np.float32)
    jj = np.arange(128)[:, None].astype(np.float32)
    ii = np.arange(128)[None, :].astype(np.float32)
    rc[:, 0:128] = np.maximum(ii - jj, 0)
    rc[:, 128:256] = np.maximum(jj - ii, 0)
    rc[:, 256:384] = (ii >= jj)
    rc[:, 384:512] = (jj > ii)
    rc[:, 512:640] = ii + 1.0
    rc[:, 640:768] = 128.0 - ii
    pp = np.arange(128, dtype=np.float32)
    rc[:, 768] = 127.0 - pp
    rc[:, 769] = pp
    rc[:, 770] = 255.0 - pp
    rc[:, 771] = 127.0 - pp
    rc[:, 772] = pp
    rc[:, 773] = 128.0 + pp
    rc[:, 774] = 1.0
    rc[:, 775] = -PI
    c["rc"] = rc
    pos = np.arange(L, dtype=np.float32)[:, None]
    tt = np.linspace(0.0, 1.0, L, dtype=np.float32)[:, None]
    bands = np.linspace(1e-4, 15, 16, dtype=np.float32)[None, :]
    a2 = (2.0 * np.float32(math.pi) * pos * bands / L).astype(np.float32)
    z = np.concatenate([tt, np.cos(a2), -np.sin(a2)], axis=-1).astype(np.float32)
    c["zfT"] = np.ascontiguousarray(z.T)
    max_decay = math.log(1e-2) / 0.3
    min_decay = math.log(1e-2) / 1.5
    deltas = np.linspace(min_decay, max_decay, 512, dtype=np.float32)
    c["absd"] = np.abs(deltas).reshape(1, 512).astype(np.float32)
    tn = -tt[:, 0]
    c["tn"] = np.ascontiguousarray(tn.reshape(16, 128).T).astype(np.float32)
    _CONSTS = c
    return c


class _Stop(Exception):
    pass


def build(dbg=(), stop_after=None):
    nc = bass.Bass("TRN2", target_bir_lowering=False)

    def CK(name):
        if stop_after == name:
            raise _Stop()

    def din(name, shape, dt=F32):
        return nc.dram_tensor(name, list(shape), dt, kind="ExternalInput").ap()

    x_d = din("x", [L, D])
    ctx_d = din("ctx", [256, D])
    cvec_d = din("cvec", [128, 16])
    wmod_d = din("w_mod", [D, 6144])
    bmodc_d = din("bmod_col", [128, 48])
    bmodr_d = din("bmod_row", [1, 6144])
    n1c_d = din("n1c", [128, 8])
    n2c_d = din("n2c", [128, 8])
    nfr_d = din("nf_row", [1, D])
    win_d = din("w_in", [D, 3584])
    hcw_d = din("hcw", [128, 48])
    fcw_d = din("fcw", [128, 176])
    hw1_d = din("hy_w1", [33, 64])
    hyp_d = din("hyp", [64, 4])
    hw2_d = din("hy_w2", [64, 64])
    hw3_d = din("hy_w3", [64, 2048])
    hbias_d = din("hy_bias", [1, 1024])
    rlog_d = din("rlog", [1, 8])
    wout_d = din("w_out", [D, D])
    wup_d = din("w_up", [D, 2 * DFF])
    wdn_d = din("w_down", [DFF, D])
    fd_d = din("fd", [16, 128, 4096], BF16)
    nyqc_d = din("nyqc", [128, 1], BF16)
    nyqr_d = din("nyqr", [1, 128], BF16)
    rope_d = din("rope", [128, 2 * L])
    rc_d = din("rc", [128, 776])
    zfT_d = din("zfT", [33, L])
    absd_d = din("absd", [1, 512])
    tn_d = din("tn", [128, 16])
    y_d = nc.dram_tensor("y", [L, D], F32, kind="ExternalOutput").ap()
    dbg_d = {}
    for name, shape, dts in dbg:
        dbg_d[name] = nc.dram_tensor("dbg_" + name, list(shape), BF16 if dts == "bf16" else F32,
                                     kind="ExternalOutput").ap()

    st = ExitStack()
    with st:
        def sb(name, shape, dt):
            return st.enter_context(nc.sbuf_tensor("s_" + name, list(shape), dt))

        def ps(name, shape, dt):
            return st.enter_context(nc.psum_tensor(name, list(shape), dt))

        S = Sched(nc)
        final_ops = []

        BIG = sb("big", [128, 103680], BF16)
        BIGF = BIG[:].bitcast(F32)
        BIGI = BIG[:].bitcast(I32)
        XB = BIG[:, 0:32768]
        ARX = BIGF[:, 0:16384]
        ARA = BIG[:, 32768:49152]
        ARAF = BIGF[:, 16384:24576]
        ARB = BIG[:, 49152:73728]
        ARBF = BIGF[:, 24576:36864]
        ARE = BIG[:, 73728:103680]
        EF = BIGF[:, 36864:51840]

        ident = sb("ident", [128, 128], BF16)
        ones_bf = sb("ones_bf", [128, 128], BF16)
        cvf = sb("cvf", [128, 16], F32)
        sT = sb("sT", [128, 16], BF16)
        srep = sb("srep", [128, 8, 128], BF16)
        modcol = sb("modcol", [128, 32, 2], F32)
        bmodc = sb("bmodc", [128, 48], F32)
        n1c = sb("n1c", [128, 8], F32)
        n2c = sb("n2c", [128, 8], F32)
        AB = sb("AB", [128, 6, 8], F32)
        hcw = sb("hcw", [128, 12, 4], F32)
        fcw = sb("fcw", [128, 44, 4], F32)
        ssq = sb("ssq", [128, 64], F32)
        rstd = sb("rstd", [128, 64], F32)
        hid2T = ARE[0:64, 16384:18432]
        w3b = ARE[0:64, 18432:20480]
        nyqc = sb("nyqc", [128, 1], BF16)
        nyqr = sb("nyqr", [1, 128], BF16)
        tn = sb("tn", [128, 16], F32)
        negpi = sb("negpi", [128, 1], F32)
        epsc = sb("epsc", [128, 1], F32)

        pbank = [ps("pb%d" % i, [128, 512], F32) for i in range(6)]
        PB = [Buf("pb%d" % i) for i in range(6)]
        ptr = [ps("pt%d" % i, [128, 1024], BF16) for i in range(2)]
        PT = [Buf("pt%d" % i) for i in range(2)]
        pring = Ring(6)
        tring = Ring(2)

        def MM(out, lhsT, rhs, start, stop, reads, writes):
            S.add("pe", lambda e: e.matmul(out, lhsT=lhsT, rhs=rhs, start=start, stop=stop), reads, writes)

        def TR(out, in_, idn, reads, writes):
            S.add("pe", lambda e: e.transpose(out, in_, idn), reads, writes)

        def ACT(out, in_, func, reads, writes, **kw):
            S.add("act", lambda e: e.activation(out=out, in_=in_, func=func, **kw), reads, writes)

        def TT(eng, out, in0, in1, op, reads, writes):
            S.add(eng, lambda e: e.tensor_tensor(out=out, in0=in0, in1=in1, op=op), reads, writes)

        def TS(eng, out, in0, s1, s2, op0, op1, reads, writes):
            if s2 is None:
                S.add(eng, lambda e: e.tensor_scalar(out=out, in0=in0, scalar1=s1, scalar2=None, op0=op0), reads, writes)
            else:
                S.add(eng, lambda e: e.tensor_scalar(out=out, in0=in0, scalar1=s1, scalar2=s2, op0=op0, op1=op1),
                      reads, writes)

        def STT(eng, out, in0, scalar, in1, op0, op1, reads, writes):
            S.add(eng, lambda e: e.scalar_tensor_tensor(out=out, in0=in0, scalar=scalar, in1=in1, op0=op0, op1=op1),
                  reads, writes)

        def CP(eng, out, in_, reads, writes):
            if eng == "act":
                S.add("act", lambda e: e.activation(out=out, in_=in_, func=AF.Copy), reads, writes)
            else:
                S.add(eng, lambda e: e.tensor_copy(out=out, in_=in_), reads, writes)

        def RCP(out, in_, reads, writes):
            S.add("dve", lambda e: e.reciprocal(out=out, in_=in_), reads, writes)

        def MEMSET(eng, out, val, writes):
            S.add(eng, lambda e: e.memset(out, val), (), writes)

        def DBG(name, src_ap, reads, rows=None):
            if name in dbg_d:
                final_ops.append(S.dma("sp", dbg_d[name], src_ap, reads=reads))

        def bc(ap, shape):
            return ap.to_broadcast(list(shape))

        try:
            b_ident = Buf("ident")
            MEMSET("pool", ident[:], 0.0, [b_ident])
            S.add("pool", lambda e: e.affine_select(out=ident[:], in_=ident[:], pattern=[[-1, 128]],
                                                    compare_op=ALU.not_equal, fill=1.0, base=0, channel_multiplier=1),
                  [b_ident], [b_ident])
            b_ones = Buf("ones")
            MEMSET("dve", ones_bf[:], 1.0, [b_ones])
            b_negpi = Buf("negpi")
            MEMSET("dve", negpi[:], -PI, [b_negpi])
            MEMSET("dve", epsc[:], EPS, [b_negpi])
            SSQ = [Buf("ssq%d" % i) for i in range(64)]
            RSTD = [Buf("rstd%d" % i) for i in range(64)]
            MEMSET("dve", ssq[:], 0.0, SSQ)

            b_zf = Buf("zfT")
            b_hid1 = Buf("hid1")
            b_hid2 = Buf("hid2")
            b_ft = Buf("ftmp0")
            CK("c0")
            b_small = {k: Buf(k) for k in ["cvf", "sT", "srep", "bmodc", "n1c", "n2c", "hcw", "fcw", "nyq", "tn", "w3b"]}
            S.dma("sp", cvf[:], cvec_d, writes=[b_small["cvf"]])
            S.dma("sp", bmodc[:], bmodc_d, writes=[b_small["bmodc"]])
            S.dma("sp", n1c[:], n1c_d, writes=[b_small["n1c"]])
            S.dma("sp", n2c[:], n2c_d, writes=[b_small["n2c"]])
            S.dma("sp", hcw[:].rearrange("p a b -> p (a b)"), hcw_d, writes=[b_small["hcw"]])
            S.dma("sp", fcw[:].rearrange("p a b -> p (a b)"), fcw_d, writes=[b_small["fcw"]])
            S.dma("sp", nyqc[:], nyqc_d, writes=[b_small["nyq"]])
            S.dma("sp", nyqr[:], nyqr_d, writes=[b_small["nyq"]])
            S.dma("sp", tn[:], tn_d, writes=[b_small["tn"]])
            ACT(sT[:], cvf[:], AF.Silu, [b_small["cvf"]], [b_small["sT"]])
            sT3 = sT[:].rearrange("p (i t) -> p i t", t=2)
            CP("dve", srep[:], bc(sT3[:, :, 0:1], [128, 8, 128]), [b_small["sT"]], [b_small["srep"]])

            R0 = 8192
            rc = EF[:, R0 + 0:R0 + 776]
            DTm = EF[:, R0 + 776:R0 + 1288].rearrange("p (h i) -> p h i", h=4)
            rowqm = ARE[:, 2 * (R0 + 1288):2 * (R0 + 2056)].rearrange("p (h d i) -> p h d i", h=4, d=3)
            R1 = R0 + 256
            rl = EF[:, R1 + 1800:R1 + 1808]
            lg = EF[:, R1 + 1808:R1 + 1816]
            lgcol = EF[:, R1 + 1816:R1 + 1820].rearrange("p (c d) -> p c d", c=2)
            dec = EF[:, R1 + 1820:R1 + 1824].rearrange("p (c d) -> p c d", c=2)
            colfb = EF[:, R1 + 1824:R1 + 1832].rearrange("p (d h) -> p d h", d=2)
            wcx = EF[:, R1 + 1832:R1 + 1848].rearrange("p (t d h) -> p t d h", t=2, d=2)
            etmp = EF[:, R1 + 1848:R1 + 1976]
            etmp2 = EF[:, R1 + 1976:R1 + 2104]
            b_rc = Buf("rc")
            b_lg = Buf("lg")
            b_DT = Buf("DT")
            b_et = Buf("etmp")
            b_rtab = Buf("rtab")
            S.dma("sp", rc, rc_d, writes=[b_rc])
            S.dma("sp", rl, rlog_d.partition_broadcast(128), writes=[b_lg])
            ACT(lg, rl, AF.Exp, [b_lg], [b_lg], scale=-1.0)
            ACT(lg, lg, AF.Ln, [b_lg, b_rc], [b_lg], bias=rc[:, 774:775])
            TS("dve", lg, lg, -1.0, None, ALU.mult, None, [b_lg], [b_lg])
            for d in range(2):
                for half in range(2):
                    P = slice(half * 64, half * 64 + 64)
                    CP("dve", lgcol[P, :, d], lg[P, d * 4 + half:d * 4 + half + 3:2], [b_lg], [b_rtab])
            for h in range(4):
                ACT(etmp, rc[:, 0:128], AF.Exp, [b_rc, b_lg], [b_et], scale=lg[:, h:h + 1])
                TT("dve", DTm[:, h, :], etmp, rc[:, 256:384], ALU.mult, [b_et, b_rc], [b_DT])
                ACT(etmp2, rc[:, 128:256], AF.Exp, [b_rc, b_lg], [b_et], scale=lg[:, 4 + h:5 + h])
                TT("dve", etmp2, etmp2, rc[:, 384:512], ALU.mult, [b_et, b_rc], [b_et])
                TT("dve", DTm[:, h, :], DTm[:, h, :], etmp2, ALU.add, [b_et, b_DT], [b_DT])
                ACT(colfb[:, 0, h:h + 1], rc[:, 768:769], AF.Exp, [b_rc, b_lg], [b_rtab], scale=lg[:, h:h + 1])
                ACT(colfb[:, 1, h:h + 1], rc[:, 769:770], AF.Exp, [b_rc, b_lg], [b_rtab], scale=lg[:, 4 + h:5 + h])
                ACT(wcx[:, :, 0, h], rc[:, 770:772], AF.Exp, [b_rc, b_lg], [b_rtab], scale=lg[:, h:h + 1])
                ACT(wcx[:, :, 1, h], rc[:, 772:774], AF.Exp, [b_rc, b_lg], [b_rtab], scale=lg[:, 4 + h:5 + h])
            ACT(dec, lgcol, AF.Exp, [b_rtab], [b_rtab], scale=128.0)
            for c in range(2):
                if c == 0:
                    MEMSET("dve", rowqm.rearrange("p h d i -> p (h d i)"), 0.0, [b_rtab])
                for half in range(2):
                    hs = slice(half * 64, half * 64 + 64)
                    h_ = 2 * c + half
                    ACT(rowqm[hs, h_, 0, :], rc[hs, 512:640], AF.Exp, [b_rc, b_rtab], [b_rtab], scale=lgcol[hs, c, 0:1])
                    ACT(rowqm[hs, h_, 1, :], rc[hs, 640:768], AF.Exp, [b_rc, b_rtab], [b_rtab], scale=lgcol[hs, c, 1:2])
                    MEMSET("dve", rowqm[hs, h_, 2, :], 1.0, [b_rtab])

            CK("c1")
            wr = [XB[:, s * 4096:(s + 1) * 4096].rearrange("p (i n) -> p i n", i=8) for s in range(4)]
            WR = [Buf("wr%d" % s) for s in range(4)]
            wring = Ring(4)
            wmv = wmod_d.rearrange("(i p) n -> p i n", p=128)
            winv = win_d.rearrange("(i p) n -> p i n", p=128)

            def load_wblock(src):
                s = wring.nxt()
                S.dma("pool", wr[s], src, writes=[WR[s]])
                return s

            b_modcol = Buf("modcol")
            b_AB = Buf("AB")

            def mod_col_half(j, which, pi=None):
                s = load_wblock(wmv[:, :, j * 512:(j + 1) * 512])
                if pi is None:
                    pi = pring.nxt()
                for cc in range(4):
                    for k in range(8):
                        MM(pbank[pi][:, cc * 2:cc * 2 + 2], wr[s][:, k, cc * 128:(cc + 1) * 128],
                           sT3[:, k, :], k == 0, k == 7, [WR[s], b_small["sT"]], [PB[pi]])
                m0 = which * 8 + (j % 2) * 4
                TT("dve", modcol[:, m0:m0 + 4, :], pbank[pi][:, 0:8].rearrange("p (a b) -> p a b", b=2),
                   bc(bmodc[:, j * 4:j * 4 + 4].unsqueeze(2), [128, 4, 2]), ALU.add,
                   [PB[pi], b_small["bmodc"]], [b_modcol])

            def mod_bc_half(j, dst, dst_buf, tmp, tmp_buf, wbuf, wbuf_b):
                S.dma("pool", wbuf, wmv[:, :, j * 512:(j + 1) * 512], writes=[wbuf_b])
                S.dma("sp", tmp, bmodr_d[0:1, j * 512:(j + 1) * 512].partition_broadcast(128), writes=[tmp_buf])
                pi = pring.nxt()
                for k in range(8):
                    MM(pbank[pi][:], srep[:, k, :], wbuf[:, k, :], k == 0, k == 7, [wbuf_b, b_small["srep"]], [PB[pi]])
                h0 = (j % 2) * 512
                TT("dve", dst[:, h0:h0 + 512], pbank[pi][:], tmp, ALU.add, [PB[pi], tmp_buf], [dst_buf])

            mod_col_half(0, 0)
            mod_col_half(1, 0)
            mod_col_half(2, 1)
            mod_col_half(3, 1)
            STT("dve", AB[:, 0, :], modcol[:, 8:16, 0], 1.0, n1c[:], ALU.add, ALU.mult, [b_modcol, b_small["n1c"]], [b_AB])
            CP("dve", AB[:, 1, :], modcol[:, 0:8, 0], [b_modcol], [b_AB])
            STT("dve", AB[:, 2, :], modcol[:, 8:16, 1], 1.0, n1c[:], ALU.add, ALU.mult, [b_modcol, b_small["n1c"]], [b_AB])
            CP("dve", AB[:, 3, :], modcol[:, 0:8, 1], [b_modcol], [b_AB])

            CK("c2")
            xt = [ARX[:, 8192 + i * 1024:8192 + (i + 1) * 1024] for i in range(3)]
            XT = [Buf("xt%d" % i) for i in range(3)]
            xtring = Ring(3)
            xn = [XB[:, 22528 + i * 1024:22528 + (i + 1) * 1024] for i in range(2)]
            XN = [Buf("xn%d" % i) for i in range(2)]
            xnring = Ring(2)
            b_rstd = Buf("rstd")
            evac_flip = [0]

            def norm_s1(src, src_bufs, scol, xn_list, xn_bufs, xn_ring):
                xi = xn_ring.nxt()
                ACT(xn_list[xi], src, AF.Square, src_bufs + [SSQ[scol]], [SSQ[scol], xn_bufs[xi]],
                    accum_out=ssq[:, scol:scol + 1])
                ACT(rstd[:, scol:scol + 1], ssq[:, scol:scol + 1], AF.Sqrt, [SSQ[scol], b_negpi], [RSTD[scol]], scale=1.0 / D,
                    bias=epsc[:, 0:1])
                RCP(rstd[:, scol:scol + 1], rstd[:, scol:scol + 1], [RSTD[scol]], [RSTD[scol]])
                TS("dve", xn_list[xi], src, rstd[:, scol:scol + 1], None, ALU.mult, None, src_bufs + [RSTD[scol]],
                   [xn_bufs[xi]])
                return xi

            def norm_s2(xi, acol, bcol, dst_fn, dst_bufs, xn_list, xn_bufs):
                ta = tring.nxt()
                tb = tring.nxt()
                NA = 3
                for c in range(NA):
                    TR(ptr[ta][:, c * 128:(c + 1) * 128], xn_list[xi][:, c * 128:(c + 1) * 128], ident[:],
                       [xn_bufs[xi], b_ident], [PT[ta]])
                for c in range(NA, 8):
                    TR(ptr[tb][:, (c - NA) * 128:(c - NA + 1) * 128], xn_list[xi][:, c * 128:(c + 1) * 128], ident[:],
                       [xn_bufs[xi], b_ident], [PT[tb]])
                for c in range(8 - NA):
                    if c < NA:
                        ACT(dst_fn(c), ptr[ta][:, c * 128:(c + 1) * 128], AF.Identity, [PT[ta], b_AB], [dst_bufs[0]],
                            scale=AB[:, acol, c:c + 1], bias=AB[:, bcol, c:c + 1])
                    cc_ = c + NA
                    TS("dve", dst_fn(cc_), ptr[tb][:, c * 128:(c + 1) * 128], AB[:, acol, cc_:cc_ + 1],
                       AB[:, bcol, cc_:cc_ + 1], ALU.mult, ALU.add, [PT[tb], b_AB], [dst_bufs[1]])

            hT = ARA[:].rearrange("p (c t) -> p c t", c=8)
            HT = [Buf("hT%d" % i) for i in range(2 * NT)]
            hcT = ARB[:, 8192:10240].rearrange("p (c t) -> p c t", c=8)
            HCT = [Buf("hcT%d" % i) for i in range(4)]
            def p1_s1(i):
                xi = xtring.nxt()
                src = x_d[i * 128:(i + 1) * 128, :] if i < NT else ctx_d[(i - NT) * 128:(i - NT + 1) * 128, :]
                S.dma("sp", xt[xi], src, writes=[XT[xi]])
                return norm_s1(xt[xi], [XT[xi]], i, xn, XN, xnring)

            def p1_s2(i, xi):
                if i < NT:
                    norm_s2(xi, 0, 1, lambda c: hT[:, c, i * 128:(i + 1) * 128], HT[2 * i:2 * i + 2], xn, XN)
                else:
                    j = i - NT
                    norm_s2(xi, 2, 3, lambda c: hcT[:, c, j * 128:(j + 1) * 128], HCT[2 * j:2 * j + 2], xn, XN)
            xi_cur = p1_s1(0)
            for i in range(NT + 2):
                xi_nxt = p1_s1(i + 1) if i + 1 < NT + 2 else None
                p1_s2(i, xi_cur)
                xi_cur = xi_nxt
            if "hT" in dbg_d:
                for c in range(8):
                    final_ops.append(S.dma("sp", dbg_d["hT"][c * 128:(c + 1) * 128, :], hT[:, c, :], reads=HT))

            CK("p1")
            rope = ARBF[:, 0:4096]
            b_rope = Buf("rope")
            S.dma("sp", rope, rope_d, writes=[b_rope])
            qrot = ARE[:, 0:4096].rearrange("p (c t) -> p c t", c=2)
            krot = ARE[:, 4096:8192].rearrange("p (c t) -> p c t", c=2)
            vtok = ARE[:, 8192:16384].rearrange("p (i n) -> p i n", i=NT)
            QR = [[Buf("qr%d_%d" % (c, q)) for q in range(4)] for c in range(2)]
            KR = [[Buf("kr%d_%d" % (c, q)) for q in range(4)] for c in range(2)]
            VT = [Buf("vt%d" % i) for i in range(NT)]
            tA = [ARX[:, 8192 + i * 512:8192 + (i + 1) * 512] for i in range(4)]
            TA = [Buf("tA%d" % i) for i in range(4)]
            S.alias(TA, XT)
            taring = Ring(2)

            def proj_fm(s, col0, q, pi):
                for k in range(8):
                    MM(pbank[pi][:], wr[s][:, k, col0:col0 + 128], hT[:, k, q * 512:(q + 1) * 512], k == 0, k == 7,
                       [WR[s]] + HT[q * 8:q * 8 + 8], [PB[pi]])

            s_qk = load_wblock(winv[:, :, 1536:2048])
            s_qks = load_wblock(winv[:, :, 3072:3584])
            s_v = load_wblock(winv[:, :, 2048:2560])
            for cc in range(4):
                isk = cc >= 2
                dst = krot if isk else qrot
                dbuf = KR if isk else QR
                for q in range(4):
                    p1 = pring.nxt()
                    proj_fm(s_qk, cc * 128, q, p1)
                    p2 = pring.nxt()
                    proj_fm(s_qks, cc * 128, q, p2)
                    ta = taring.nxt()
                    cosv = rope[:, q * 512:(q + 1) * 512]
                    sinv = rope[:, L + q * 512:L + (q + 1) * 512]
                    if isk:
                        STT("dve", tA[2 * ta], pbank[p1][:], 0.125, cosv, ALU.mult, ALU.mult, [PB[p1], b_rope], [TA[2 * ta]])
                        STT("dve", tA[2 * ta + 1], pbank[p2][:], 0.125, sinv, ALU.mult, ALU.mult, [PB[p2], b_rope],
                            [TA[2 * ta + 1]])
                    else:
                        TT("dve", tA[2 * ta], pbank[p1][:], cosv, ALU.mult, [PB[p1], b_rope], [TA[2 * ta]])
                        TT("dve", tA[2 * ta + 1], pbank[p2][:], sinv, ALU.mult, [PB[p2], b_rope], [TA[2 * ta + 1]])
                    TT("dve", dst[:, cc % 2, q * 512:(q + 1) * 512], tA[2 * ta], tA[2 * ta + 1], ALU.add,
                       [TA[2 * ta], TA[2 * ta + 1]], [dbuf[cc % 2][q]])
            for i in range(NT):
                pi = pring.nxt()
                for k in range(8):
                    MM(pbank[pi][:], hT[:, k, i * 128:(i + 1) * 128], wr[s_v][:, k, :], k == 0, k == 7,
                       [WR[s_v]] + HT[2 * i:2 * i + 2], [PB[pi]])
                CP("act", vtok[:, i, :], pbank[pi][:], [PB[pi]], [VT[i]])
            gT = ARB[:, 10240:18432].rearrange("p (c t) -> p c t", c=4)
            GTB = [[Buf("gT%d_%d" % (c, q)) for q in range(4)] for c in range(4)]
            s_g = load_wblock(winv[:, :, 2560:3072])

            def gproj_job(j):
                cg, q = j // 4, j % 4
                pi = pring.nxt()
                proj_fm(s_g, cg * 128, q, pi)
                ACT(gT[:, cg, q * 512:(q + 1) * 512], pbank[pi][:], AF.Silu, [PB[pi]], [GTB[cg][q]])

            if "qk" in dbg_d:
                for c in range(2):
                    for (nm, src, bb) in (("q", qrot, QR), ("k", krot, KR)):
                        r0 = (0 if nm == "q" else 256) + c * 128
                        final_ops.append(S.dma("sp", dbg_d["qk"][r0:r0 + 128, :], src[:, c, :], reads=bb[c]))

            CK("p2")
            XU = 8192
            Sallb = XB[:, 2 * XU:2 * XU + 4096].rearrange("p (n c e) -> p n c e", n=NT, c=2)
            o = XU + 2048
            Pm = [XB[:, 2 * (o + i * 256):2 * (o + i * 256) + 512].rearrange("p (h i) -> p h i", h=4) for i in range(2)]
            o += 512
            sq = [XB[:, 2 * (o + i * 256):2 * (o + i * 256) + 512] for i in range(2)]
            o += 512
            sg = [XB[:, 2 * (o + i * 256):2 * (o + i * 256) + 512].rearrange("p (h i) -> p h i", h=4) for i in range(2)]
            o += 512
            rsb = ARX[:, o:o + 512]
            o += 512
            tyb = ARX[:, o:o + 512].rearrange("p (h i) -> p h i", h=4)
            o += 512
            S32 = [ARX[:, o + d * 256:o + (d + 1) * 256].rearrange("p (c e) -> p c e", c=2) for d in range(2)]
            o += 512
            tmpS = ARX[:, o:o + 256].rearrange("p (c e) -> p c e", c=2)
            o += 256
            Sfb = [XB[:, 2 * (o + i * 128):2 * (o + i * 128) + 256].rearrange("p (c e) -> p c e", c=2) for i in range(2)]
            o += 256
            ktile = [XB[:, 2 * (o + i * 128):2 * (o + i * 128) + 256] for i in range(2)]
            o += 256
            qfb0 = XB[:, 2 * o:2 * o + 1536]
            o += 768
            qfb1 = XB[:, 2 * o:2 * o + 1536]
            qfb = [q_.rearrange("p (h d i) -> p h d i", h=4, d=3) for q_ in (qfb0, qfb1)]
            kct = XB[:, 2 * o:2 * o + 1024].rearrange("p (t d n) -> p t d n", t=2, d=2)
            o += 512
            vct = XB[:, 2 * o:2 * o + 1024].rearrange("p (t n) -> p t n", t=2)
            o += 512
            sg.append(XB[:, 2 * o:2 * o + 512].rearrange("p (h i) -> p h i", h=4))
            o += 256
            assert o <= 16384, o
            ret_bufs = {k: Buf(k) for k in ["Sallb", "rsb", "tyb", "tmpS", "kct", "vct"]}
            PM = [Buf("pm%d" % i) for i in range(2)]
            SQ = [Buf("sq%d" % i) for i in range(2)]
            SG = [Buf("sg%d" % i) for i in range(3)]
            S32B = [Buf("s32_%d" % i) for i in range(2)]
            SFB = [Buf("sfb%d" % i) for i in range(2)]
            KTL = [Buf("kt%d" % i) for i in range(2)]
            QFB = [Buf("qfb%d" % i) for i in range(2)]
            SALLB = [Buf("sallb%d" % i) for i in range(NT)]
            all_ret = list(ret_bufs.values()) + PM + SQ + SG + S32B + SFB + KTL + QFB + SALLB
            S.alias(all_ret, XT + XN + TA)

            CK("rconst")
            for t in range(2):
                pk = pring.nxt()
                for k in range(8):
                    MM(pbank[pk][:, 0:256], hcT[:, k, t * 128:(t + 1) * 128], wr[s_qk][:, k, 256:512], k == 0, k == 7,
                       [WR[s_qk]] + HCT[2 * t:2 * t + 2], [PB[pk]])
                pv = pring.nxt()
                for k in range(8):
                    MM(pbank[pv][:], hcT[:, k, t * 128:(t + 1) * 128], wr[s_v][:, k, :], k == 0, k == 7,
                       [WR[s_v]] + HCT[2 * t:2 * t + 2], [PB[pv]])
                for d in range(2):
                    STT("dve", kct[:, t, d, :].rearrange("p (h e) -> p h e", h=4),
                        pbank[pk][:, 0:256].rearrange("p (h e) -> p h e", h=4), 0.125,
                        bc(wcx[:, t, d, :].unsqueeze(2), [128, 4, 64]), ALU.mult, ALU.mult,
                        [PB[pk], b_rtab], [ret_bufs["kct"]])
                CP("act", vct[:, t, :], pbank[pv][:], [PB[pv]], [ret_bufs["vct"]])
            for d in range(2):
                pi = pring.nxt()
                for h in range(4):
                    c = h // 2
                    for t in range(2):
                        MM(pbank[pi][:, h * 128:(h + 1) * 128], kct[:, t, d, c * 128:(c + 1) * 128],
                           vct[:, t, h * 128:(h + 1) * 128], t == 0, t == 1,
                           [ret_bufs["kct"], ret_bufs["vct"]], [PB[pi]])
                pv4 = pbank[pi][:].rearrange("p (s e) -> p s e", s=4)
                CP("dve", S32[d][0:64, :, :], pv4[0:64, 0:4:2, :], [PB[pi]], [S32B[d]])
                CP("dve", S32[d][64:128, :, :], pv4[64:128, 1:4:2, :], [PB[pi]], [S32B[d]])
            DBG("s_f", S32[0], [S32B[0]])
            DBG("s_b", S32[1], [S32B[1]])

            CK("ctx")
            Sallf = ARB[:, 0:4096].rearrange("p (n c e) -> p n c e", n=NT, c=2)
            B_SF = Buf("sallf")
            S.alias([B_SF], [b_rope])
            CP("act", Sallb[:, 15, :, :], S32[1], [S32B[1]], [SALLB[15]])
            CP("act", Sallf[:, 0, :, :], S32[0], [S32B[0]], [B_SF])
            kt4 = [ktile[0], ktile[1], sg[0].rearrange("p h i -> p (h i)")[:, 0:256], sg[1].rearrange("p h i -> p (h i)")[:, 0:256]]
            KT4 = [KTL[0], KTL[1], SG[0], SG[1]]
            tmpS2 = [tmpS, sq[0].bitcast(F32).rearrange("p (c e) -> p c e", c=2)]
            TMPS2 = [ret_bufs["tmpS"], SQ[0]]
            ktring = Ring(4)

            Dm = [[sg[2].rearrange("p h i -> p (h i)")[:, (d * 2 + c) * 128:(d * 2 + c + 1) * 128] for c in range(2)]
                  for d in range(2)]
            b_Dm = SG[2]
            for d in range(2):
                for c in range(2):
                    TS("dve", Dm[d][c], ident[:], dec[:, c, d:d + 1], None, ALU.mult, None, [b_ident, b_rtab], [b_Dm])

            def scan_step(n, d, s_prev, prev_bufs, s_new, new_bufs):
                ti = tring.nxt()
                for c in range(2):
                    TR(ptr[ti][:, c * 128:(c + 1) * 128], krot[:, c, n * 128:(n + 1) * 128], ident[:],
                       [KR[c][n // 4], b_ident], [PT[ti]])
                ki = ktring.nxt()
                TT("dve", kt4[ki].rearrange("p (h e) -> p h e", h=4),
                   ptr[ti][:, 0:256].rearrange("p (h e) -> p h e", h=4),
                   bc(colfb[:, d, :].unsqueeze(2), [128, 4, 64]), ALU.mult, [PT[ti], b_rtab], [KT4[ki]])
                pi = pring.nxt()
                for h in range(4):
                    c = h // 2
                    MM(pbank[pi][:, h * 128:(h + 1) * 128], kt4[ki][:, c * 128:(c + 1) * 128],
                       vtok[:, n, h * 128:(h + 1) * 128], True, False, [KT4[ki], VT[n]], [PB[pi]])
                    MM(pbank[pi][:, h * 128:(h + 1) * 128], Dm[d][c], s_prev[:, c, :], False, True,
                       [b_Dm] + prev_bufs, [PB[pi]])
                pv4 = pbank[pi][:].rearrange("p (s e) -> p s e", s=4)
                CP("act", s_new[0:64, :, :], pv4[0:64, 0:4:2, :], [PB[pi]], new_bufs)
                CP("act", s_new[64:128, :, :], pv4[64:128, 1:4:2, :], [PB[pi]], new_bufs)

            gproj_job(0)
            for s_ in range(15):
                scan_step(s_, 0, Sallf[:, s_, :, :], [B_SF], Sallf[:, s_ + 1, :, :], [B_SF])
                scan_step(15 - s_, 1, Sallb[:, 15 - s_, :, :], [SALLB[15 - s_]], Sallb[:, 14 - s_, :, :], [SALLB[14 - s_]])
                gproj_job(s_ + 1)

            S.alias([QFB[1]], [ret_bufs["kct"], ret_bufs["vct"]])
            CK("bwd")
            mixR = ARE[:, 21504:29696].rearrange("p (c t) -> p c t", c=4)
            mixHy = ARA[:, 8192:16384].rearrange("p (c t) -> p c t", c=4)

            def mix_chunk(k):
                return mixHy[:, k, :] if k < 4 else mixR[:, k - 4, :]
            MIXR = [Buf("mixr%d" % i) for i in range(NT)]
            stt = {}

            def O1(n):
                tsl = slice(n * 128, (n + 1) * 128)
                qbuf = [QR[0][n // 4], QR[1][n // 4]]
                kbuf = [KR[0][n // 4], KR[1][n // 4]]
                qi = n % 2
                pS = n % 2
                TT("dve", qfb[qi].rearrange("p (c hh) d i -> p c (hh d) i", c=2),
                   bc(qrot[:, :, tsl].unsqueeze(2), [128, 2, 6, 128]),
                   rowqm.rearrange("p (c hh) d i -> p c (hh d) i", c=2), ALU.mult, qbuf + [b_rtab], [QFB[qi]])
                for h in range(4):
                    MM(pbank[pS][:, h * 128:(h + 1) * 128], krot[:, h // 2, tsl], qfb[qi][:, h, 2, :], True, True,
                       kbuf + [QFB[qi]], [PB[pS]])

            def O2(n):
                pS = n % 2
                pmi = n % 2
                TT("dve", Pm[pmi], pbank[pS][:].rearrange("p (h i) -> p h i", h=4), DTm, ALU.mult,
                   [PB[pS], b_DT], [PM[pmi]])

            def O3(n):
                qi = n % 2
                pmi = n % 2
                pO = 2 + (n % 2)
                for h in range(4):
                    c = h // 2
                    osl = pbank[pO][:, h * 128:(h + 1) * 128]
                    MM(osl, vtok[:, n, h * 128:(h + 1) * 128], Pm[pmi][:, h, :], True, False, [VT[n], PM[pmi]], [PB[pO]])
                    MM(osl, Sallf[:, n, c, :], qfb[qi][:, h, 0, :], False, False, [B_SF, QFB[qi]], [PB[pO]])
                    MM(osl, Sallb[:, n, c, :], qfb[qi][:, h, 1, :], False, True, [SALLB[n], QFB[qi]], [PB[pO]])

            def O4a(n):
                pO = 2 + (n % 2)
                sqi = n % 2
                ACT(sq[sqi], pbank[pO][:], AF.Square, [PB[pO]], [SQ[sqi]])
                pR = 4
                MM(pbank[pR][:], ones_bf[:], sq[sqi], True, True, [b_ones, SQ[sqi]], [PB[pR]])
                ACT(rsb, pbank[pR][:], AF.Ln, [PB[pR], b_negpi], [ret_bufs["rsb"]], scale=1.0 / 128, bias=epsc[:, 0:1])
                ACT(rsb, rsb, AF.Exp, [ret_bufs["rsb"]], [ret_bufs["rsb"]], scale=-0.5)

            def O4b(n):
                tsl = slice(n * 128, (n + 1) * 128)
                pO = 2 + (n % 2)
                TT("dve", tyb, gT[:, :, tsl], rsb.rearrange("p (h i) -> p h i", h=4), ALU.mult,
                   [ret_bufs["rsb"]] + [GTB[c_][n // 4] for c_ in range(4)], [ret_bufs["tyb"]])
                TT("dve", mixR[:, :, tsl], pbank[pO][:].rearrange("p (h i) -> p h i", h=4), tyb, ALU.mult,
                   [PB[pO], ret_bufs["tyb"]], [MIXR[n]])

            for i in range(NT + 4):
                if 0 <= i - 4 < NT:
                    O4b(i - 4)
                if 0 <= i - 3 < NT:
                    O4a(i - 3)
                if 0 <= i - 2 < NT:
                    O3(i - 2)
                if 0 <= i - 1 < NT:
                    O2(i - 1)
                if i < NT:
                    O1(i)
                if i in (3, 7, 11, 15):
                    jm = 6 + (i - 3) // 4
                    mod_col_half(jm, 2 if jm < 8 else 3, pi=5)
            if "yret" in dbg_d:
                for c in range(4):
                    final_ops.append(S.dma("sp", dbg_d["yret"][c * 128:(c + 1) * 128, :], mixR[:, c, :], reads=MIXR))
            CK("ret")
            b_g1b = Buf("g1b")
            b_g2b = Buf("g2b")
            b_bmt = Buf("bmt")
            STT("dve", AB[:, 4, :], modcol[:, 24:32, 0], 1.0, n2c[:], ALU.add, ALU.mult, [b_modcol, b_small["n2c"]], [b_AB])
            CP("dve", AB[:, 5, :], modcol[:, 16:24, 0], [b_modcol], [b_AB])

            ztok = ARB[:].rearrange("p (i n) -> p i n", i=NT)
            ZT = [[Buf("zt%d_%d" % (g, i)) for i in range(NT)] for g in range(3)]
            S.alias([b for l in ZT for b in l], [b for l in GTB for b in l] + [b_rope, B_SF] + HCT)
            o = XU
            araw = [ARX[:, o + i * 2052:o + i * 2052 + 2050] for i in range(2)]
            o += 2 * 2052
            zc = [XB[:, 2 * o:2 * o + 2048]]
            o += 1024
            rcv = [ARX[:, 6144:8192], ARX[:, o:o + 2048]]
            o += 2048
            assert o <= 16384, o
            rconv = ARX[:, 6144:8192]
            ARAW = [Buf("araw%d" % i) for i in range(2)]
            ZC = [Buf("zc0")]
            RCV = [Buf("rcv%d" % i) for i in range(2)]
            b_rconv = Buf("rconv")
            S.alias(ARAW + ZC, all_ret)
            S.alias([b_rconv], [WR[3]])
            S.alias([RCV[0]], [WR[3]])
            S.alias([RCV[1]], all_ret)
            zfT = EF[0:33, 0:2048]
            hid1T = EF[0:64, 2048:4096]
            fa = EF[0:64, 4096:4608]
            fk = EF[0:64, 4608:5120]
            fki = BIGI[0:64, 36864 + 5120:36864 + 5632]
            fw = EF[0:64, 5632:6144]
            hw1 = EF[0:33, 6144:6208]
            hw2 = EF[0:64, 6208:6272]
            hyp = EF[0:64, 6272:6276]
            b_hw = Buf("hw")
            b_ft = Buf("ftmp")
            qkv_bufs = [b for l in QR + KR for b in l] + VT
            S.alias([b_hw, b_ft, b_zf, b_hid1], qkv_bufs)
            S.alias([b_hid2, b_small["w3b"]], [b_rc, b_lg, b_DT, b_et, b_rtab])
            S.dma("sp", zfT, zfT_d, writes=[b_zf])
            S.dma("sp", hw1, hw1_d, writes=[b_hw])
            S.dma("sp", hw2, hw2_d, writes=[b_hw])
            S.dma("sp", hyp, hyp_d, writes=[b_hw])
            w3sc = EF[0:64, 6400:7424]
            b_w3sc = Buf("w3sc")
            S.alias([b_w3sc], qkv_bufs)
            for od_ in range(2):
                S.dma("sp", w3sc[:, 0:512], hw3_d[:, od_ * 512:(od_ + 1) * 512], writes=[b_w3sc])
                S.dma("sp", w3sc[:, 512:1024], hw3_d[:, 1024 + od_ * 512:1024 + (od_ + 1) * 512], writes=[b_w3sc])
                TT("dve", w3b[:, od_ * 512:(od_ + 1) * 512], w3sc[:, 0:512], w3sc[:, 512:1024], ALU.add,
                   [b_w3sc], [b_small["w3b"]])
                TT("dve", w3b[:, 1024 + od_ * 512:1024 + (od_ + 1) * 512], w3sc[:, 0:512], w3sc[:, 512:1024], ALU.subtract,
                   [b_w3sc], [b_small["w3b"]])

            def sin_layer(pi, bcol, fcol, out_ap, out_buf):
                TS("dve", fa, pbank[pi][0:64, :], bcol, fcol, ALU.add, ALU.mult, [PB[pi], b_hw], [b_ft])
                TS("dve", fki, fa, 1.0 / (2 * PI), 16.5, ALU.mult, ALU.add, [b_ft], [b_ft])
                CP("dve", fk, fki, [b_ft], [b_ft])
                STT("dve", fw, fk, -2 * PI, fa, ALU.mult, ALU.add, [b_ft], [b_ft])
                TS("dve", fk, fw, -33 * PI, 2 * PI, ALU.is_lt, ALU.mult, [b_ft], [b_ft])
                STT("dve", fw, fw, 33 * PI, fk, ALU.add, ALU.add, [b_ft], [b_ft])
                ACT(out_ap, fw, AF.Sin, [b_ft, b_negpi], [out_buf], bias=negpi[0:64, 0:1])

            def filt_unit(u):
                q = u % 4
                pi = pring.nxt()
                if u < 4:
                    MM(pbank[pi][0:64, :], hw1, zfT[:, q * 512:(q + 1) * 512], True, True, [b_hw, b_zf], [PB[pi]])
                    sin_layer(pi, hyp[:, 0:1], hyp[:, 1:2], hid1T[:, q * 512:(q + 1) * 512], b_hid1)
                else:
                    MM(pbank[pi][0:64, :], hw2, hid1T[:, q * 512:(q + 1) * 512], True, True, [b_hw, b_hid1], [PB[pi]])
                    sin_layer(pi, hyp[:, 2:3], hyp[:, 3:4], hid2T[:, q * 512:(q + 1) * 512], b_hid2)
            for i in range(2):
                MEMSET("pool", araw[i][:, 0:1], 0.0, [ARAW[i]])
                MEMSET("pool", araw[i][:, 2049:2050], 0.0, [ARAW[i]])
            arring = Ring(2)
            zcring = Ring(1)
            evq = [0]

            hy_slots = {}

            def hy_A(chunk):
                g, cc = chunk // 4, chunk % 4
                if cc == 0:
                    hy_slots[g] = load_wblock(winv[:, :, g * 512:(g + 1) * 512])
                s_ = hy_slots[g]
                ai = arring.nxt()
                for q in range(4):
                    pi = pring.nxt()
                    proj_fm(s_, cc * 128, q, pi)
                    CP("act", araw[ai][:, 1 + q * 512:1 + (q + 1) * 512], pbank[pi][:], [PB[pi]], [ARAW[ai]])
                    ACT(rcv[ai][:, q * 512:(q + 1) * 512], pbank[pi][:], AF.Identity, [PB[pi], b_small["hcw"]], [RCV[ai]],
                        scale=hcw[:, chunk, 1:2], bias=hcw[:, chunk, 3:4])
                return ai

            def hy_B(chunk, ai):
                wq = hcw[:, chunk, :]
                zi = zcring.nxt()
                STT("dve", rcv[ai], araw[ai][:, 0:2048], wq[:, 0:1], rcv[ai], ALU.mult, ALU.add,
                    [ARAW[ai], b_small["hcw"], RCV[ai]], [RCV[ai]])
                STT("dve", zc[zi], araw[ai][:, 2:2050], wq[:, 2:3], rcv[ai], ALU.mult, ALU.add,
                    [ARAW[ai], b_small["hcw"], RCV[ai]], [ZC[zi]])
                return zi

            def hy_C(chunk, zi):
                g = chunk // 4
                for half in range(2):
                    ti = tring.nxt()
                    for a in range(8):
                        i = half * 8 + a
                        TR(ptr[ti][:, a * 128:(a + 1) * 128], zc[zi][:, i * 128:(i + 1) * 128], ident[:],
                           [ZC[zi], b_ident], [PT[ti]])
                    CP("act", ztok[:, half * 8:half * 8 + 8, chunk * 128:(chunk + 1) * 128],
                       ptr[ti][:].rearrange("p (a c) -> p a c", a=8), [PT[ti]], ZT[g][half * 8:half * 8 + 8])

            ai_cur = hy_A(0)
            for chunk in range(12):
                ai_nxt = hy_A(chunk + 1) if chunk + 1 < 12 else None
                zi = hy_B(chunk, ai_cur)
                hy_C(chunk, zi)
                if chunk < 8:
                    filt_unit(chunk)
                ai_cur = ai_nxt
            if "ztok" in dbg_d:
                for i in range(NT):
                    final_ops.append(S.dma("sp", dbg_d["ztok"][i * 128:(i + 1) * 128, :], ztok[:, i, :],
                                           reads=[ZT[0][i], ZT[1][i], ZT[2][i]]))

            CK("hyproj")
            ksum = ARA[:, 0:8192].rearrange("p (i n) -> p i n", i=NT)
            kdiff = ARA[:, 8192:16384].rearrange("p (i n) -> p i n", i=NT)
            KS = [Buf("ks%d" % i) for i in range(NT)]
            S.alias(KS, HT)
            Yre = ARE[:, 0:8192].rearrange("p (j n) -> p j n", j=16)
            Yim = ARE[:, 8192:16384].rearrange("p (j n) -> p j n", j=16)
            YB = [Buf("y%d" % j) for j in range(16)]
            S.alias(YB, qkv_bufs + [b_hw, b_ft, b_zf, b_hid1])
            fblk = [XB[:, i * 4096:(i + 1) * 4096].rearrange("p (i s r) -> p i s r", i=16, s=2) for i in range(2)]
            FB = [Buf("fb%d" % i) for i in range(2)]
            gt = [ARX[:, 4096 + i * 512:4096 + (i + 1) * 512] for i in range(4)]
            GT = [Buf("gt%d" % i) for i in range(4)]
            yhy = [XB[:, 2 * (6144 + i * 256):2 * (6144 + i * 256) + 512] for i in range(2)]
            YH = [Buf("yhy%d" % i) for i in range(2)]
            S.alias(FB + GT + YH, WR + [b_rconv] + RCV)
            wpc = [XB[:, 2 * (6656 + i * 512):2 * (6656 + i * 512) + 1024].rearrange("p (i n) -> p i n", i=8) for i in range(2)]
            bpc = [ARX[:, 7680 + i * 128:7680 + (i + 1) * 128] for i in range(2)]
            gpc = [ARX[:, 7936 + i * 128:7936 + (i + 1) * 128] for i in range(2)]
            WPC = [Buf("wpc%d" % i) for i in range(2)]
            BPC = [Buf("bpc%d" % i) for i in range(2)]
            GPC = [Buf("gpc%d" % i) for i in range(2)]
            S.alias(WPC + BPC + GPC, WR + [b_rconv] + RCV)

            def g1_piece(c8):
                i = c8 % 2
                c0 = 2048 + c8 * 128
                S.dma("pool", wpc[i], wmv[:, :, c0:c0 + 128], writes=[WPC[i]])
                S.dma("sp", bpc[i], bmodr_d[0:1, c0:c0 + 128].partition_broadcast(128), writes=[BPC[i]])
                pi = pring.nxt()
                for k in range(8):
                    MM(pbank[pi][:, 0:128], srep[:, k, :], wpc[i][:, k, :], k == 0, k == 7, [WPC[i], b_small["srep"]],
                       [PB[pi]])
                TT("dve", gpc[i], pbank[pi][:, 0:128], bpc[i], ALU.add, [PB[pi], BPC[i]], [GPC[i]])
                TT("dve", woutS[:, :, c8 * 128:(c8 + 1) * 128], woutS[:, :, c8 * 128:(c8 + 1) * 128],
                   bc(gpc[i].unsqueeze(1), [128, 8, 128]), ALU.mult, [b_wout, GPC[i]], [b_wout])
            o = XU
            hbb = ARX[:, o:o + 512]; o += 512
            absb = ARX[:, o:o + 512]; o += 512
            wint = [ARX[:, o + i * 512:o + (i + 1) * 512] for i in range(2)]; o += 1024
            kfw = [ARX[:, o + i * 512:o + (i + 1) * 512] for i in range(2)]; o += 1024
            Ksb = [ARX[:, o + i * 512:o + (i + 1) * 512] for i in range(4)]; o += 2048
            tq = [ARX[:, o + i * 512:o + (i + 1) * 512] for i in range(4)]; o += 2048
            ynq = XB[0:1, 2 * o:2 * o + 512]; o += 256
            knq = ARX[0:1, o:o + 512]; o += 512
            assert o <= 16384, o
            b_hbb = Buf("hbb")
            b_absb = Buf("absb")
            WINT = [Buf("win%d" % i) for i in range(2)]
            KFW = [Buf("kfw%d" % i) for i in range(2)]
            KSB = [Buf("ksb%d" % i) for i in range(4)]
            TQ = [Buf("tq%d" % i) for i in range(4)]
            b_ynq = Buf("ynq")
            b_knq = Buf("knq")
            S.alias([b_hbb, b_absb, b_ynq, b_knq] + WINT + KFW + KSB + TQ, ARAW + ZC + RCV + all_ret)
            S.dma("sp", absb, absd_d.partition_broadcast(128), writes=[b_absb])
            fbring = Ring(2)
            mixH = [Buf("mixh%d" % i) for i in range(NT)]

            for od in range(2):
                vt_col = 0
                xg_col = 512 * (od + 1)
                S.dma("sp", hbb, hbias_d[0:1, od * 512:(od + 1) * 512].partition_broadcast(128), writes=[b_hbb])
                def filt_tile(od_, i):
                    wi = i % 2
                    ACT(wint[wi], absb, AF.Exp, [b_absb, b_small["tn"]], [WINT[wi]], scale=tn[:, i:i + 1])
                    pf = pring.nxt()
                    MM(pbank[pf][:], hid2T[:, i * 128:(i + 1) * 128], w3b[:, od_ * 512:(od_ + 1) * 512], True, True,
                       [b_hid2, b_small["w3b"]], [PB[pf]])
                    pb_ = pring.nxt()
                    MM(pbank[pb_][:], hid2T[:, i * 128:(i + 1) * 128], w3b[:, 1024 + od_ * 512:1024 + (od_ + 1) * 512],
                       True, True, [b_hid2, b_small["w3b"]], [PB[pb_]])
                    if i != 0:
                        TT("dve", ksum[:, i, :], pbank[pf][:], wint[wi], ALU.mult, [PB[pf], WINT[wi]], [KS[i]])
                        TT("dve", kdiff[:, i, :], pbank[pb_][:], wint[wi], ALU.mult, [PB[pb_], WINT[wi]], [KS[i]])
                    else:
                        TT("dve", kfw[0], pbank[pf][:], wint[wi], ALU.mult, [PB[pf], WINT[wi]], [KFW[0]])
                        TT("dve", kfw[1], pbank[pb_][:], wint[wi], ALU.mult, [PB[pb_], WINT[wi]], [KFW[1]])
                        TT("dve", kfw[0][0:1, :], kfw[0][0:1, :], kfw[1][0:1, :], ALU.add, [KFW[0], KFW[1]], [KFW[0]])
                        TS("dve", kfw[0][0:1, :], kfw[0][0:1, :], 0.5, None, ALU.mult, None, [KFW[0]], [KFW[0]])
                        CP("dve", kfw[1][0:1, :], kfw[0][0:1, :], [KFW[0]], [KFW[1]])
                        CP("dve", ksum[:, i, :], kfw[0], [KFW[0]], [KS[i]])
                        CP("dve", kdiff[:, i, :], kfw[1], [KFW[1]], [KS[i]])
                if od == 0:
                    for i in range(NT):
                        filt_tile(0, i)
                vbufs = [ZT[0][i] for i in range(NT)]
                for j in range(16):
                    fi = fbring.nxt()
                    S.dma("sp", fblk[fi].rearrange("p i s r -> p (i s r)"), fd_d[j], writes=[FB[fi]])
                    pKr = pring.nxt()
                    for i in range(NT):
                        MM(pbank[pKr][:], fblk[fi][:, i, 0, :], ksum[:, i, :], i == 0, i == NT - 1, [FB[fi], KS[i]], [PB[pKr]])
                    pKi = pring.nxt()
                    for i in range(NT):
                        MM(pbank[pKi][:], fblk[fi][:, i, 1, :], kdiff[:, i, :], i == 0, i == NT - 1, [FB[fi], KS[i]], [PB[pKi]])
                    kb = (j % 2) * 2
                    CP("act", Ksb[kb], pbank[pKr][:], [PB[pKr]], [KSB[kb]])
                    CP("act", Ksb[kb + 1], pbank[pKi][:], [PB[pKi]], [KSB[kb + 1]])
                    TT("dve", Ksb[kb], Ksb[kb], hbb, ALU.add, [KSB[kb], b_hbb], [KSB[kb]])
                    pUr = pring.nxt()
                    for i in range(NT):
                        MM(pbank[pUr][:], fblk[fi][:, i, 0, :], ztok[:, i, vt_col:vt_col + 512], i == 0, i == NT - 1,
                           [FB[fi], vbufs[i]], [PB[pUr]])
                    pUi = pring.nxt()
                    for i in range(NT):
                        MM(pbank[pUi][:], fblk[fi][:, i, 1, :], ztok[:, i, vt_col:vt_col + 512], i == 0, i == NT - 1,
                           [FB[fi], vbufs[i]], [PB[pUi]])
                    TT("dve", tq[0], pbank[pUr][:], Ksb[kb], ALU.mult, [PB[pUr], KSB[kb]], [TQ[0]])
                    TT("dve", tq[1], pbank[pUi][:], Ksb[kb + 1], ALU.mult, [PB[pUi], KSB[kb + 1]], [TQ[1]])
                    TT("dve", Yre[:, j, :], tq[0], tq[1], ALU.subtract, [TQ[0], TQ[1]], [YB[j]])
                    TT("dve", tq[2], pbank[pUr][:], Ksb[kb + 1], ALU.mult, [PB[pUr], KSB[kb + 1]], [TQ[2]])
                    TT("dve", tq[3], pbank[pUi][:], Ksb[kb], ALU.mult, [PB[pUi], KSB[kb]], [TQ[3]])
                    TT("dve", Yim[:, j, :], tq[2], tq[3], ALU.add, [TQ[2], TQ[3]], [YB[j]])
                    if j == 0:
                        TS("dve", Yre[0:1, 0, :], Yre[0:1, 0, :], 0.5, None, ALU.mult, None, [YB[0]], [YB[0]])
                pN = pring.nxt()
                for i in range(NT):
                    MM(pbank[pN][0:1, :], nyqc[:, 0:1], ksum[:, i, :], i == 0, i == NT - 1, [b_small["nyq"], KS[i]], [PB[pN]])
                CP("act", knq, pbank[pN][0:1, :], [PB[pN]], [b_knq])
                TT("dve", knq, knq, hbb[0:1, :], ALU.add, [b_knq, b_hbb], [b_knq])
                pN2 = pring.nxt()
                for i in range(NT):
                    MM(pbank[pN2][0:1, :], nyqc[:, 0:1], ztok[:, i, vt_col:vt_col + 512], i == 0, i == NT - 1,
                       [b_small["nyq"], vbufs[i]], [PB[pN2]])
                STT("dve", ynq, pbank[pN2][0:1, :], 0.5, knq, ALU.mult, ALU.mult, [PB[pN2], b_knq], [b_ynq])
                if od == 1:
                    woutS = ARA[:, 0:8192].rearrange("p (i n) -> p i n", i=8)
                    b_wout = Buf("wout")
                    S.alias([b_wout], KS)
                    S.alias(mixH, KS)
                    S.dma("pool", woutS, wout_d.rearrange("(i p) n -> p i n", p=128), writes=[b_wout])
                pend_T = []
                for t in range(NT):
                    fi = fbring.nxt()
                    S.dma("sp", fblk[fi].rearrange("p i s r -> p (i s r)"), fd_d[t], writes=[FB[fi]])
                    pY = pring.nxt()
                    for i in range(NT):
                        MM(pbank[pY][:], fblk[fi][:, i, 0, :], Yre[:, i, :], i == 0, False, [FB[fi], YB[i]], [PB[pY]])
                    for i in range(NT):
                        MM(pbank[pY][:], fblk[fi][:, i, 1, :], Yim[:, i, :], False, False, [FB[fi], YB[i]], [PB[pY]])
                    MM(pbank[pY][:], nyqr[0:1, :], ynq, False, True, [b_small["nyq"], b_ynq], [PB[pY]])
                    if od == 0:
                        filt_tile(1, t)
                    if od == 1 and t < 8:
                        g1_piece(t)
                    if od == 1 and t >= 1 and pend_T:
                        yhy_T(pend_T.pop(0))
                    if od == 0:
                        STT("dve", ztok[:, t, 0:512], pbank[pY][:], 2.0 / NFFT, ztok[:, t, xg_col:xg_col + 512],
                            ALU.mult, ALU.mult, [PB[pY], ZT[1][t]], [ZT[0][t]])
                    else:
                        yi = t % 2
                        STT("dve", yhy[yi], pbank[pY][:], 2.0 / NFFT, ztok[:, t, xg_col:xg_col + 512],
                            ALU.mult, ALU.mult, [PB[pY], ZT[2][t]], [YH[yi]])
                        def yhy_T(t_):
                            yi_ = t_ % 2
                            ti = tring.nxt()
                            for c in range(4):
                                TR(ptr[ti][:, c * 128:(c + 1) * 128], yhy[yi_][:, c * 128:(c + 1) * 128], ident[:],
                                   [YH[yi_], b_ident], [PT[ti]])
                            CP("act", mixHy[:, :, t_ * 128:(t_ + 1) * 128],
                               ptr[ti][:, 0:512].rearrange("p (c i) -> p c i", c=4), [PT[ti]], [mixH[t_]])
                        pend_T.append(t)
                while pend_T:
                    yhy_T(pend_T.pop(0))
                if od == 0 and "y1" in dbg_d:
                    for i in range(NT):
                        final_ops.append(S.dma("sp", dbg_d["y1"][i * 128:(i + 1) * 128, :], ztok[:, i, 0:512],
                                               reads=[ZT[0][i]]))
            if "yhy" in dbg_d:
                for c in range(4):
                    final_ops.append(S.dma("sp", dbg_d["yhy"][c * 128:(c + 1) * 128, :], mixHy[:, c, :], reads=mixH))

            if "zlate" in dbg_d:
                for i in range(NT):
                    final_ops.append(S.dma("sp", dbg_d["zlate"][i * 128:(i + 1) * 128, :], ztok[:, i, :],
                                           reads=[ZT[0][i], ZT[1][i], ZT[2][i]]))
            CK("hyena")
            X1 = ARX[:].rearrange("p (t n) -> p t n", t=NT)
            X1B = [Buf("x1_%d" % i) for i in range(2 * NT)]
            xt2 = [EF[:, i * 1024:(i + 1) * 1024] for i in range(2)]
            XT2 = [Buf("xt2_%d" % i) for i in range(2)]
            S.alias(XT2, YB)
            all_x_scratch = (WR + WPC + BPC + GPC + [b_rconv] + RCV + FB + GT + YH + [b_hbb, b_absb, b_ynq, b_knq] + WINT + KFW + KSB + TQ
                             + ARAW + ZC + all_ret + XT + XN + TA + HCT + [b_zf, b_hid1])
            S.alias(X1B, all_x_scratch)
            xn2 = [ARB[:, i * 1024:(i + 1) * 1024] for i in range(NT)]
            XN2 = [Buf("xn2_%d" % i) for i in range(NT)]
            S.alias(XN2, [b for l in ZT for b in l])
            xn2ring = Ring(NT)
            xi2 = {}
            for t in range(NT):
                xi = t % 2
                S.dma("sp", xt2[xi], x_d[t * 128:(t + 1) * 128, :], writes=[XT2[xi]])
                for half in range(2):
                    pi = pring.nxt()
                    for k in range(8):
                        MM(pbank[pi][:], mix_chunk(k)[:, t * 128:(t + 1) * 128], woutS[:, k, half * 512:(half + 1) * 512],
                           k == 0, k == 7, [mixH[t], MIXR[t], b_wout], [PB[pi]])
                    TT("dve", X1[:, t, half * 512:(half + 1) * 512], xt2[xi][:, half * 512:(half + 1) * 512], pbank[pi][:],
                       ALU.add, [XT2[xi], PB[pi]], [X1B[2 * t + half]])
                xi2[t] = norm_s1(X1[:, t, :], X1B[2 * t:2 * t + 2], 18 + t, xn2, XN2, xn2ring)
            if "x1" in dbg_d:
                for t in range(NT):
                    final_ops.append(S.dma("sp", dbg_d["x1"][t * 128:(t + 1) * 128, :], X1[:, t, :], reads=X1B[2 * t:2 * t + 2]))

            CK("wout")
            h2T = ARA[:].rearrange("p (c t) -> p c t", c=8)
            H2 = [Buf("h2T%d" % i) for i in range(2 * NT)]
            S.alias(H2, [b_wout] + mixH)
            for i in range(NT):
                norm_s2(xi2[i], 4, 5, lambda c, i=i: h2T[:, c, i * 128:(i + 1) * 128], H2[2 * i:2 * i + 2], xn2, XN2)

            CK("norm2")
            actT = [ARB[:, g * 8192:(g + 1) * 8192].rearrange("p (f t) -> p f t", f=4) for g in range(2)]
            wdn = [ARB[:, 16384 + g * 4096:16384 + (g + 1) * 4096].rearrange("p (f n) -> p f n", f=4) for g in range(2)]
            ACTB = [[Buf("act%d_%d" % (g, f)) for f in range(4)] for g in range(2)]
            WDN = [[Buf("wdn%d_%d" % (g, f)) for f in range(4)] for g in range(2)]
            zall = [b for l in ZT for b in l]
            S.alias([b for l in ACTB for b in l] + [b for l in WDN for b in l], zall + XN2)
            wup = [ARE[:, s * 2048:(s + 1) * 2048].rearrange("p (i v n) -> p i v n", i=8, v=2) for s in range(3)]
            WUP = [Buf("wup%d" % s) for s in range(3)]
            S.alias(WUP, YB + XT2)
            wupring = Ring(3)
            rbuf = [EF[:, 3072 + i * 2048:3072 + (i + 1) * 2048] for i in range(2)] + [EF[:, 11272:11272 + 2048]]
            RB = [Buf("rb%d" % i) for i in range(3)]
            rbring = Ring(3)
            araw2 = [EF[:, 7168 + i * 2052:7168 + i * 2052 + 2050] for i in range(2)]
            ARAW2 = [Buf("araw2_%d" % i) for i in range(2)]
            b_rc2 = RB[2]
            e_old = YB + XT2 + [b_hid2, b_small["w3b"], b_rc, b_lg, b_DT, b_et, b_rtab] + MIXR
            S.alias(ARAW2 + [b_rc2], e_old)
            g2b = EF[:, 13320:14344]
            bmt2 = EF[:, 14344:14856]
            b_bmt2 = Buf("bmt2")
            S.alias([b_g2b, b_bmt2], e_old)
            lw2 = [ARE[:, 6144 + i * 4096:6144 + (i + 1) * 4096].rearrange("p (i n) -> p i n", i=8) for i in range(2)]
            LW2 = [Buf("lw2_%d" % i) for i in range(2)]
            S.alias(LW2, YB)
            mod_bc_half(10, g2b, b_g2b, bmt2, b_bmt2, lw2[0], LW2[0])
            mod_bc_half(11, g2b, b_g2b, bmt2, b_bmt2, lw2[1], LW2[1])
            S.alias(RB, LW2)
            for i in range(2):
                MEMSET("pool", araw2[i][:, 0:1], 0.0, [ARAW2[i]])
                MEMSET("pool", araw2[i][:, 2049:2050], 0.0, [ARAW2[i]])
            ar2ring = Ring(2)
            wupv = wup_d.rearrange("(i p) n -> p i n", p=128)

            def ffn_job(s, vg, f):
                ai = ar2ring.nxt()
                ri = rbring.nxt()
                out_ap, out_buf = rbuf[ri], RB[ri]
                wq = fcw[:, vg * NFC + f, :]
                for q in range(4):
                    pi = pring.nxt()
                    for k in range(8):
                        MM(pbank[pi][:], wup[s][:, k, vg, :], h2T[:, k, q * 512:(q + 1) * 512], k == 0, k == 7,
                           [WUP[s]] + H2[q * 8:q * 8 + 8], [PB[pi]])
                    CP("act", araw2[ai][:, 1 + q * 512:1 + (q + 1) * 512], pbank[pi][:], [PB[pi]], [ARAW2[ai]])
                    ACT(out_ap[:, q * 512:(q + 1) * 512], pbank[pi][:], AF.Identity, [PB[pi], b_small["fcw"]], [out_buf],
                        scale=wq[:, 1:2], bias=wq[:, 3:4])
                STT("dve", out_ap, araw2[ai][:, 0:2048], wq[:, 0:1], out_ap, ALU.mult, ALU.add,
                    [ARAW2[ai], b_small["fcw"], out_buf], [out_buf])
                STT("dve", out_ap, araw2[ai][:, 2:2050], wq[:, 2:3], out_ap, ALU.mult, ALU.add,
                    [ARAW2[ai], b_small["fcw"], out_buf], [out_buf])
                return ri

            groups = [list(range(g * 4, min(g * 4 + 4, NFC))) for g in range(6)]

            def ffn_up(gi):
                gb = gi % 2
                for fi, f in enumerate(groups[gi]):
                    s = wupring.nxt()
                    S.dma("pool", wup[s][:, :, 0, :], wupv[:, :, f * 128:(f + 1) * 128], writes=[WUP[s]])
                    S.dma("pool", wup[s][:, :, 1, :], wupv[:, :, DFF + f * 128:DFF + (f + 1) * 128], writes=[WUP[s]])
                    S.dma("pool", wdn[gb][:, fi, :], wdn_d[f * 128:(f + 1) * 128, :], writes=[WDN[gb][fi]])
                    TT("dve", wdn[gb][:, fi, :], wdn[gb][:, fi, :], g2b, ALU.mult, [WDN[gb][fi], b_g2b], [WDN[gb][fi]])
                    rv = ffn_job(s, 0, f)
                    rg = ffn_job(s, 1, f)
                    ACT(rbuf[rg], rbuf[rg], AF.Silu, [RB[rg]], [RB[rg]])
                    TT("dve", actT[gb][:, fi, :], rbuf[rv], rbuf[rg], ALU.mult, [RB[rv], RB[rg]], [ACTB[gb][fi]])

            def ffn_down(gi, after_tile=None):
                gb = gi % 2
                nf = len(groups[gi])
                for t in range(NT):
                    if after_tile is not None and t >= 1:
                        after_tile(t - 1)
                    for half in range(2):
                        pi = pring.nxt()
                        for fi in range(nf):
                            MM(pbank[pi][:], actT[gb][:, fi, t * 128:(t + 1) * 128], wdn[gb][:, fi, half * 512:(half + 1) * 512],
                               fi == 0, fi == nf - 1, [ACTB[gb][fi], WDN[gb][fi]], [PB[pi]])
                        TT("dve", X1[:, t, half * 512:(half + 1) * 512], X1[:, t, half * 512:(half + 1) * 512], pbank[pi][:],
                           ALU.add, [X1B[2 * t + half], PB[pi]], [X1B[2 * t + half]])

            for gi in range(6):
                ffn_up(gi)
                if gi >= 1:
                    ffn_down(gi - 1)
            nfb = ARAF[:, 0:1024]
            b_nfb = Buf("nfb")
            S.alias([b_nfb], H2)
            S.dma("sp", nfb, nfr_d.partition_broadcast(128), writes=[b_nfb])
            outt = [ARAF[:, 1024 + i * 1024:1024 + (i + 1) * 1024] for i in range(2)]
            OT = [Buf("ot%d" % i) for i in range(2)]
            S.alias(OT, H2)

            def final_tile(t):
                sc = 36 + t
                oi = t % 2
                ACT(outt[oi], X1[:, t, :], AF.Square, X1B[2 * t:2 * t + 2] + [SSQ[sc]], [SSQ[sc], OT[oi]], accum_out=ssq[:, sc:sc + 1])
                ACT(rstd[:, sc:sc + 1], ssq[:, sc:sc + 1], AF.Sqrt, [SSQ[sc], b_negpi], [RSTD[sc]], scale=1.0 / D,
                    bias=epsc[:, 0:1])
                RCP(rstd[:, sc:sc + 1], rstd[:, sc:sc + 1], [RSTD[sc]], [RSTD[sc]])
                STT("dve", outt[oi], X1[:, t, :], rstd[:, sc:sc + 1], nfb, ALU.mult, ALU.mult, X1B[2 * t:2 * t + 2] + [RSTD[sc], b_nfb], [OT[oi]])
                final_ops.append(S.dma("sp", y_d[t * 128:(t + 1) * 128, :], outt[oi], reads=[OT[oi]]))
            ffn_down(5, after_tile=final_tile)
            final_tile(NT - 1)

            CK("ffn")
        except _Stop:
            pass
        S.emit(final_ops=final_ops)
    return nc


def _prep_shared(inp):
    f = np.float32
    sh = {}
    sh["w_mod"] = np.ascontiguousarray(inp["w_mod"][0], f)
    b_mod = np.asarray(inp["b_mod"][0], f)
    sh["bmod_col"] = np.ascontiguousarray(b_mod.reshape(48, 128).T)
    sh["bmod_row"] = np.ascontiguousarray(b_mod.reshape(1, 6144))
    sh["n1c"] = np.ascontiguousarray(np.asarray(inp["norm1"][0], f).reshape(8, 128).T)
    sh["n2c"] = np.ascontiguousarray(np.asarray(inp["norm2"][0], f).reshape(8, 128).T)
    sh["nf_row"] = np.ascontiguousarray(np.asarray(inp["norm_f"], f).reshape(1, D))
    w_in = np.asarray(inp["w_in"][0], f)
    def swapped(c0):
        blk = w_in[:, c0:c0 + 256].reshape(D, 4, 2, 32)
        return blk[:, :, ::-1, :].reshape(D, 256)
    sh["w_in"] = np.ascontiguousarray(np.concatenate([w_in, swapped(1536), swapped(1792)], axis=1))
    hw = np.asarray(inp["hy_conv_w"][0], f)
    hb = np.asarray(inp["hy_conv_b"][0], f)
    hc = np.concatenate([hw, hb[None]], axis=0)
    sh["hcw"] = np.ascontiguousarray(hc.reshape(4, 12, 128).transpose(2, 1, 0).reshape(128, 48))
    fw = np.asarray(inp["ffn_conv_w"][0], f)
    fb = np.asarray(inp["ffn_conv_b"][0], f)
    fc = np.concatenate([fw, fb[None]], axis=0)
    sh["fcw"] = np.ascontiguousarray(fc.reshape(4, 44, 128).transpose(2, 1, 0).reshape(128, 176))
    sh["hy_w1"] = np.ascontiguousarray(inp["hy_w1"][0], f)
    sh["hyp"] = np.ascontiguousarray(np.stack([inp["hy_b1"][0], inp["hy_f1"][0], inp["hy_b2"][0], inp["hy_f2"][0]],
                                              axis=1).astype(f))
    sh["hy_w2"] = np.ascontiguousarray(inp["hy_w2"][0], f)
    sh["hy_w3"] = np.ascontiguousarray(inp["hy_w3"][0], f)
    sh["hy_bias"] = np.ascontiguousarray(np.asarray(inp["hy_bias"][0], f).reshape(1, 1024))
    sh["rlog"] = np.ascontiguousarray(np.concatenate([inp["ret_logit_f"][0], inp["ret_logit_b"][0]]).astype(f).reshape(1, 8))
    sh["w_out"] = np.ascontiguousarray(inp["w_out"][0], f)
    sh["w_up"] = np.ascontiguousarray(inp["ffn_w_up"][0], f)
    sh["w_down"] = np.ascontiguousarray(inp["ffn_w_down"][0], f)
    hc_ = host_consts()
    for k in ("fd", "nyqc", "nyqr", "rope", "rc", "zfT", "absd", "tn"):
        sh[k] = hc_[k]
    return sh


_NC_CACHE = {}


def kernel(_dbg=(), _stop=None, _cores=None, **inputs):
    inp = {k: np.asarray(v) for k, v in inputs.items()}
    key = repr((_dbg, _stop))
    if key not in _NC_CACHE:
        _NC_CACHE[key] = build(_dbg, _stop)
    nc = _NC_CACHE[key]
    sh = _prep_shared(inp)
    x = np.asarray(inp["x"], np.float32)
    ctx = np.asarray(inp["ctx"], np.float32)
    c = np.asarray(inp["c"], np.float32)
    c_ctx = np.asarray(inp["c_ctx"], np.float32)
    n = x.shape[0] if _cores is None else _cores
    in_maps = []
    for b in range(n):
        m = dict(sh)
        m["x"] = np.ascontiguousarray(x[b])
        m["ctx"] = np.ascontiguousarray(ctx[b])
        cv = np.stack([c[b].reshape(8, 128).T, c_ctx.reshape(8, 128).T], axis=2)
        m["cvec"] = np.ascontiguousarray(cv.reshape(128, 16))
        in_maps.append(m)
    res = run_bass_kernel_spmd(nc, in_maps, core_ids=list(range(n)))
    out = np.stack([np.asarray(r["y"], np.float32) for r in res.results], axis=0)
    if _dbg:
        return out, res.results
    return out
```
